# Optimizing a Trainium2 kernel written in Bass

```python
import jax, jax.numpy as jnp
from jax import lax
import numpy as np

D_MODEL = 1024
BATCH = 16
SEQ = 256
DEPTH = 2
DEC_BATCH = 4
DEC_SEQ = 1024
PAST_LEN = 256

GRID_W = 64
GROUP_W = D_MODEL // 4
CONV_W = 31
CHUNK = 128
GMLP_HEADS = 4
GMLP_HD = GROUP_W // GMLP_HEADS
POOL_WINDOWS = (2, 4, 8, 16)
POOL_GROUPS = len(POOL_WINDOWS)
POOL_GW = GROUP_W // POOL_GROUPS
RET_HEADS = 4
RET_HD = GROUP_W // RET_HEADS
RET_CHUNK = 128
ROPE_BASE = 10000.0
D_FF = -(-8 * D_MODEL // (3 * 256)) * 256
IN_COLS = 11 * GROUP_W
EPS = 1e-6

kernel_name = "hybrid_parallel_groups_diffusion_step"

F32 = jnp.float32


def _rms_norm(x, g):
    xf = x.astype(F32)
    y = xf * lax.rsqrt(jnp.mean(xf * xf, axis=-1, keepdims=True) + EPS)
    return (y * g.astype(F32)).astype(x.dtype)


def _conformer_conv(a, dw, b, ln_g, ln_b, pw):
    a1, a2 = jnp.split(a, 2, axis=-1)
    h = a1 * jax.nn.sigmoid(a2)
    h = lax.conv_general_dilated(h, dw[:, None, :].astype(h.dtype), window_strides=(1,),
                                 padding=[(CONV_W // 2, CONV_W // 2)],
                                 dimension_numbers=('NWC', 'WIO', 'NWC'),
                                 feature_group_count=GROUP_W) + b
    hf = h.astype(F32)
    mu = jnp.mean(hf, axis=-1, keepdims=True)
    var = jnp.mean(jnp.square(hf - mu), axis=-1, keepdims=True)
    hn = (hf - mu) * lax.rsqrt(var + EPS) * ln_g.astype(F32) + ln_b.astype(F32)
    return jax.nn.silu(hn).astype(a.dtype) @ pw


def _chunk_gmlp(uv, ws, b):
    u, v = jnp.split(uv, 2, axis=-1)
    B, L, _ = v.shape
    vh = v.reshape(B, L // CHUNK, CHUNK, GMLP_HEADS, GMLP_HD)
    s = jnp.einsum('hpq,bnqhd->bnphd', ws, vh) + jnp.swapaxes(b, 0, 1)[:, :, None]
    return u * s.reshape(B, L, GROUP_W)


def _multiscale_pool(p, pool_w, pool_scale):
    B, L, _ = p.shape
    pf = p.astype(F32)
    csum = jnp.concatenate([jnp.zeros((B, 1, GROUP_W), F32), jnp.cumsum(pf, axis=1)], axis=1)
    t = jnp.arange(L)
    outs = []
    for gi, w in enumerate(POOL_WINDOWS):
        lo = jnp.clip(t - w // 2, 0, L)
        hi = jnp.clip(t + w // 2, 0, L)
        sl = slice(gi * POOL_GW, (gi + 1) * POOL_GW)
        cs = csum[:, :, sl]
        mean = (cs[:, hi] - cs[:, lo]) / (hi - lo).astype(F32)[None, :, None]
        outs.append(mean - pf[:, :, sl])
    d = jnp.stack(outs, axis=2).astype(p.dtype)
    y = jnp.einsum('blgc,gcd->blgd', d, pool_w).reshape(B, L, GROUP_W)
    return y * pool_scale


def _axial_rope(x):
    L = x.shape[2]
    rows = L // GRID_W
    r = jnp.repeat(jnp.arange(rows, dtype=F32), GRID_W)
    c = (jnp.arange(rows * GRID_W) % GRID_W).astype(F32)
    nf = RET_HD // 4
    inv = ROPE_BASE ** (-jnp.arange(nf, dtype=F32) / nf)
    xf = x.astype(F32)

    def rot(xh, pos):
        ang = pos[:, None] * inv
        cos, sin = jnp.cos(ang), jnp.sin(ang)
        x1, x2 = xh[..., :nf], xh[..., nf:]
        return jnp.concatenate([x1 * cos - x2 * sin, x1 * sin + x2 * cos], axis=-1)

    half = RET_HD // 2
    return jnp.concatenate([rot(xf[..., :half], r), rot(xf[..., half:], c)], axis=-1).astype(x.dtype)


def _retention_scan(q, k, v, log_gamma, s0):
    B, H, L, d = q.shape
    n = L // RET_CHUNK

    def chunks(z):
        return jnp.moveaxis(z.reshape(B, H, n, RET_CHUNK, d), 2, 0)

    i = jnp.arange(RET_CHUNK, dtype=F32)
    diff = i[:, None] - i[None, :]
    lower = diff >= 0
    dmask = jnp.where(lower, jnp.exp(jnp.where(lower, diff, 0.0)[None] * log_gamma[:, None, None]), 0.0)
    q_dec = jnp.exp((i + 1.0)[None, :] * log_gamma[:, None])
    k_dec = jnp.exp((RET_CHUNK - 1.0 - i)[None, :] * log_gamma[:, None])
    s_dec = jnp.exp(RET_CHUNK * log_gamma)

    def step(s, qkv):
        qc, kc, vc = qkv
        att = jnp.einsum('bhid,bhjd->bhij', qc, kc) * dmask
        o = (jnp.einsum('bhij,bhje->bhie', att, vc)
             + jnp.einsum('bhid,bhde->bhie', qc, s) * q_dec[:, :, None])
        s = s * s_dec[:, None, None] + jnp.einsum('bhjd,bhje->bhde', kc * k_dec[:, :, None], vc)
        return s, o

    s_fin, o = lax.scan(step, s0, (chunks(q), chunks(k), chunks(v)))
    return jnp.moveaxis(o, 0, 2).reshape(B, H, L, d), s_fin


def _retention(r, log_gamma2, s0, rotate):
    B, L, _ = r.shape

    def heads(z):
        return jnp.swapaxes(z.reshape(B, L, RET_HEADS, RET_HD), 1, 2)

    qf, kf, qb, kb, v, g = jnp.split(r, 6, axis=-1)
    qf, kf, qb, kb, v = heads(qf), heads(kf), heads(qb), heads(kb), heads(v)
    if rotate:
        qf, kf, qb, kb = _axial_rope(qf), _axial_rope(kf), _axial_rope(qb), _axial_rope(kb)
    ks = RET_HD ** -0.5
    vf = v.astype(F32)
    o_f, s_f = _retention_scan(qf.astype(F32), kf.astype(F32) * ks, vf, log_gamma2[0], s0[:, 0])
    flip = lambda z: jnp.flip(z, axis=2)
    o_b, s_b = _retention_scan(flip(qb.astype(F32)), flip(kb.astype(F32)) * ks, flip(vf), log_gamma2[1], s0[:, 1])
    o = o_f + flip(o_b)
    mu = jnp.mean(o, axis=-1, keepdims=True)
    var = jnp.mean(jnp.square(o - mu), axis=-1, keepdims=True)
    on = ((o - mu) * lax.rsqrt(var + EPS))
    on = jnp.swapaxes(on, 1, 2).reshape(B, L, GROUP_W).astype(r.dtype)
    return jax.nn.silu(g) * on, jnp.stack([s_f, s_b], axis=1)


def _layer(x, mod, s0, rotate, g_norm1, g_norm2, w_in, w_out, conv_dw, conv_b, conv_ln_g, conv_ln_b,
           conv_pw, gmlp_ws, gmlp_b, pool_w, pool_scale, ret_decay, w_ffn_in, w_ffn_out):
    sh1, sc1, ga1, sh2, sc2, ga2 = jnp.split(mod, 6, axis=-1)
    h = _rms_norm(x, g_norm1) * (1.0 + sc1) + sh1
    proj = h @ w_in
    a, uv, p, r = jnp.split(proj, [2 * GROUP_W, 4 * GROUP_W, 5 * GROUP_W], axis=-1)
    ya = _conformer_conv(a, conv_dw, conv_b, conv_ln_g, conv_ln_b, conv_pw)
    yb = _chunk_gmlp(uv, gmlp_ws, gmlp_b)
    yc = _multiscale_pool(p, pool_w, pool_scale)
    yd, s_new = _retention(r, jax.nn.log_sigmoid(ret_decay.astype(F32)), s0, rotate)
    x = x + ga1 * (jnp.concatenate([ya, yb, yc, yd], axis=-1) @ w_out)
    h = _rms_norm(x, g_norm2) * (1.0 + sc2) + sh2
    gt, up = jnp.split(h @ w_ffn_in, 2, axis=-1)
    x = x + ga2 * ((jax.nn.silu(gt) * up) @ w_ffn_out)
    return x, s_new


def setup_inputs(seed: int = 0) -> dict:
    key = jax.random.key(seed)
    ks = jax.random.split(key, 32)
    nrm = lambda k, shape, s: jax.random.normal(k, shape, F32) * s
    gam = 1.0 - 2.0 ** jnp.linspace(-5.0, -12.0, RET_HEADS, dtype=F32)
    logit = jnp.log(gam) - jnp.log1p(-gam)
    return {
        'x_prompt': nrm(ks[0], (BATCH, SEQ, D_MODEL), 1.0),
        'x_sample': nrm(ks[1], (DEC_BATCH, DEC_SEQ, D_MODEL), 1.0),
        'state_ret': nrm(ks[2], (DEC_BATCH, DEPTH, 2, RET_HEADS, RET_HD, RET_HD), 0.25),
        'c': nrm(ks[3], (DEC_BATCH, D_MODEL), 1.0),
        'c_ctx': nrm(ks[4], (D_MODEL,), 1.0),
        'w_ada': nrm(ks[5], (DEPTH, D_MODEL, 6 * D_MODEL), 0.5 * D_MODEL ** -0.5),
        'b_ada': nrm(ks[6], (DEPTH, 6 * D_MODEL), 0.02),
        'g_norm1': 1.0 + nrm(ks[7], (DEPTH, D_MODEL), 0.02),
        'g_norm2': 1.0 + nrm(ks[8], (DEPTH, D_MODEL), 0.02),
        'w_in': nrm(ks[9], (DEPTH, D_MODEL, IN_COLS), D_MODEL ** -0.5),
        'w_out': nrm(ks[10], (DEPTH, D_MODEL, D_MODEL), D_MODEL ** -0.5),
        'conv_dw': nrm(ks[11], (DEPTH, CONV_W, GROUP_W), CONV_W ** -0.5),
        'conv_b': nrm(ks[12], (DEPTH, GROUP_W), 0.02),
        'conv_ln_g': 1.0 + nrm(ks[13], (DEPTH, GROUP_W), 0.02),
        'conv_ln_b': nrm(ks[14], (DEPTH, GROUP_W), 0.02),
        'conv_pw': nrm(ks[15], (DEPTH, GROUP_W, GROUP_W), GROUP_W ** -0.5),
        'gmlp_ws': nrm(ks[16], (DEPTH, GMLP_HEADS, CHUNK, CHUNK), CHUNK ** -0.5),
        'gmlp_b': 1.0 + nrm(ks[17], (DEPTH, GMLP_HEADS, CHUNK), 0.02),
        'pool_w': nrm(ks[18], (DEPTH, POOL_GROUPS, POOL_GW, POOL_GW), POOL_GW ** -0.5),
        'pool_scale': 1.0 + nrm(ks[19], (DEPTH, GROUP_W), 0.02),
        'ret_decay': logit[None, None, :] + nrm(ks[20], (DEPTH, 2, RET_HEADS), 0.05),
        'w_ffn_in': nrm(ks[21], (DEPTH, D_MODEL, 2 * D_FF), D_MODEL ** -0.5),
        'w_ffn_out': nrm(ks[22], (DEPTH, D_FF, D_MODEL), D_FF ** -0.5),
        'g_final': 1.0 + nrm(ks[23], (D_MODEL,), 0.02),
    }


def reference(x_prompt, x_sample, state_ret, c, c_ctx, w_ada, b_ada, g_norm1, g_norm2, w_in, w_out,
              conv_dw, conv_b, conv_ln_g, conv_ln_b, conv_pw, gmlp_ws, gmlp_b, pool_w, pool_scale,
              ret_decay, w_ffn_in, w_ffn_out, g_final):
    xc = x_prompt
    s_zero = jnp.zeros((x_prompt.shape[0], 2, RET_HEADS, RET_HD, RET_HD), F32)
    ctx_states = []
    xl = x_sample
    for l in range(DEPTH):
        mod_ctx = (jax.nn.silu(c_ctx) @ w_ada[l] + b_ada[l])[None, None, :]
        xc, s_new = _layer(xc, mod_ctx, s_zero, False, g_norm1[l], g_norm2[l], w_in[l], w_out[l],
                           conv_dw[l], conv_b[l], conv_ln_g[l], conv_ln_b[l], conv_pw[l], gmlp_ws[l],
                           gmlp_b[l], pool_w[l], pool_scale[l], ret_decay[l], w_ffn_in[l], w_ffn_out[l])
        ctx_states.append(s_new)
        mod_lat = (jax.nn.silu(c) @ w_ada[l] + b_ada[l])[:, None, :]
        xl, _ = _layer(xl, mod_lat, state_ret[:, l].astype(F32), True, g_norm1[l], g_norm2[l], w_in[l],
                       w_out[l], conv_dw[l], conv_b[l], conv_ln_g[l], conv_ln_b[l], conv_pw[l], gmlp_ws[l],
                       gmlp_b[l], pool_w[l], pool_scale[l], ret_decay[l], w_ffn_in[l], w_ffn_out[l])
    y_prompt = _rms_norm(xc, g_final)
    y_sample = _rms_norm(xl, g_final)
    new_state_ret = jnp.stack(ctx_states, axis=1).astype(x_prompt.dtype)
    return (y_prompt, y_sample, new_state_ret)
```

```python
import contextlib
import math
import numpy as np
import concourse.bass as bass
import concourse.mybir as mybir
from concourse.bass_utils import run_bass_kernel_spmd

F32 = mybir.dt.float32
BF16 = mybir.dt.bfloat16
ALU = mybir.AluOpType
AF = mybir.ActivationFunctionType

DEPTH = 2
DFF = 2816
EPS = 1e-6
NSLOT = 4
SLOTW = 4096
ARENA_WORDS = 15872


class Prog:
    ENG = ['pe', 'act', 'dve', 'pool', 'sp']

    def __init__(self, nc, stack, n_dma_sems=32):
        self.nc = nc
        self.streams = {e: [] for e in self.ENG}
        self.sems = []
        self.eidx = {}
        for e in self.ENG:
            self.eidx[e] = len(self.sems)
            self.sems.append(stack.enter_context(nc.semaphore("es_" + e)))
        self.ecount = {e: 0 for e in self.ENG}
        self.dma_base = len(self.sems)
        self.n_dma = n_dma_sems
        for i in range(n_dma_sems):
            self.sems.append(stack.enter_context(nc.semaphore("ds_%d" % i)))
        self.dma_count = [0] * n_dma_sems
        self.dma_rr = 0
        self.dma_rr_pool = 0
        self.waited = {e: {} for e in self.ENG}
        self.res_w = {}
        self.res_r = {}
        self.nwaits = 0
        self.nops = {e: 0 for e in self.ENG}

    def _deps(self, reads, writes):
        deps = {}

        def add(tok):
            if tok is None:
                return
            s, v = tok
            if deps.get(s, 0) < v:
                deps[s] = v
        for r in reads:
            add(self.res_w.get(r))
        for w in writes:
            add(self.res_w.get(w))
            for s, v in self.res_r.get(w, {}).items():
                add((s, v))
        return deps

    def _emit_waits(self, eng, deps):
        for s, v in sorted(deps.items()):
            if eng == 'pe' and s == self.eidx['pe']:
                continue
            if self.waited[eng].get(s, 0) >= v:
                continue
            self.waited[eng][s] = v
            self.streams[eng].append(('wait', s, v))
            self.nwaits += 1

    def _record(self, tok, reads, writes):
        s, v = tok
        for r in reads:
            d = self.res_r.setdefault(r, {})
            if d.get(s, 0) < v:
                d[s] = v
        for w in writes:
            self.res_w[w] = tok
            self.res_r[w] = {}

    def op(self, eng, fn, reads=(), writes=()):
        deps = self._deps(reads, writes)
        self._emit_waits(eng, deps)
        self.ecount[eng] += 1
        tok = (self.eidx[eng], self.ecount[eng])
        self.streams[eng].append(('op', fn, self.eidx[eng]))
        self._record(tok, reads, writes)
        self.nops[eng] += 1
        return tok

    def dma(self, eng, out, in_, reads=(), writes=(), **kw):
        half = self.n_dma // 2
        if eng == 'pool':
            k = self.dma_rr_pool
            self.dma_rr_pool = (k + 1) % half
        else:
            k = half + self.dma_rr
            self.dma_rr = (self.dma_rr + 1) % (self.n_dma - half)
        s = self.dma_base + k
        deps = self._deps(reads, writes)
        if self.dma_count[k] > 0 and deps.get(s, 0) < self.dma_count[k]:
            deps[s] = self.dma_count[k]
        self._emit_waits(eng, deps)
        self.dma_count[k] += 16
        tok = (s, self.dma_count[k])
        self.streams[eng].append(('dma', out, in_, s, kw))
        self._record(tok, reads, writes)
        return tok

    def wait_tok(self, eng, tok):
        self._emit_waits(eng, {tok[0]: tok[1]})

    def run(self, block):
        sems = self.sems

        def runner(name):
            def f(e):
                for item in self.streams[name]:
                    if item[0] == 'wait':
                        e.wait_ge(sems[item[1]], item[2])
                    elif item[0] == 'op':
                        ins = item[1](e)
                        ins.then_inc(sems[item[2]], 1)
                    else:
                        _, out, in_, s, kw = item
                        e.dma_start(out=out, in_=in_, **kw).then_inc(sems[s], 16)
            return f
        block.tensor(runner('pe'))
        block.scalar(runner('act'))
        block.vector(runner('dve'))
        block.gpsimd(runner('pool'))
        block.sync(runner('sp'))


class Arena:
    def __init__(self, t, words):
        self.t = t
        self.words = words
        self.off = 0
        self.epoch = 0

    def reset(self):
        self.off = 0
        self.epoch += 1

    def alloc(self, dtype, shape):
        n = int(np.prod(shape[1:]))
        sz = 4 if dtype == F32 else 2
        words = (n * sz + 3) // 4
        words = (words + 7) // 8 * 8
        assert self.off + words <= self.words, ("arena overflow", self.off, words)
        ap = self.t[:, self.off:self.off + words]
        self.off += words
        if dtype != F32:
            ap = ap.bitcast(dtype)
        ap = ap[:, 0:n]
        if len(shape) == 3:
            ap = ap.rearrange("p (a b) -> p a b", a=shape[1])
        elif len(shape) == 4:
            ap = ap.rearrange("p (a b c) -> p a b c", a=shape[1], b=shape[2])
        elif len(shape) == 5:
            ap = ap.rearrange("p (a b c d) -> p a b c d", a=shape[1], b=shape[2], c=shape[3])
        return ap


def _sp32_layout():
    off = {}
    cur = 0

    def add(name, n):
        nonlocal cur
        off[name] = (cur, n)
        cur += n
    add('cvec', 8)
    add('flag', 1)
    add('g1', 16)
    add('g2', 16)
    add('gF', 8)
    add('b_fm', 96)
    add('conv_dw', 124)
    add('conv_b', 4)
    add('ln_g', 4)
    add('ln_b', 4)
    add('pool_scale', 4)
    add('gb_tab', 512)
    add('rdcol', 8)
    add('rdb', 16)
    add('diffT', 256)
    add('triT', 256)
    add('ramp_q', 256)
    add('rampcol_k', 2)
    add('bdmask', 128)
    return off, cur


SP32_OFF, NS = _sp32_layout()


def _spbf_layout():
    off = {}
    cur = 0

    def add(name, n):
        nonlocal cur
        off[name] = (cur, n)
        cur += n
    add('conv_pw', 1024)
    add('wsT', 1024)
    add('pool_bd', 512)
    add('ident', 128)
    add('Pm', 128)
    return off, cur


SPBF_OFF, NB = _spbf_layout()


class _Stop(Exception):
    pass


def build_program(debug=None, stop_after=None, stop_sub=None):
    nc = bass.Bass("TRN2", target_bir_lowering=False)
    dram_in = lambda n, s: nc.dram_tensor(n, s, F32, kind="ExternalInput").ap()
    d_x = dram_in("xT", [1024, 1024])
    d_sp = dram_in("sp32", [128, NS])
    d_bf = dram_in("spbf", [128, NB])
    d_tabs = dram_in("tabs", [128, 4096])
    d_s0 = dram_in("s0", [128, 8, 128])
    d_wada = dram_in("w_ada", [2, 1024, 6144])
    d_win = dram_in("w_in", [2, 1024, 2816])
    d_wout = dram_in("w_out", [2, 1024, 1024])
    d_wfi = dram_in("w_ffn_in", [2, 1024, 5632])
    d_wfo = dram_in("w_ffn_out", [2, 2816, 1024])
    d_y = nc.dram_tensor("yT", [1024, 1024], F32, kind="ExternalOutput").ap()
    d_st = nc.dram_tensor("st", [2, 2, 4, 4, 64, 64], F32, kind="ExternalOutput").ap()
    dbg_out = {}
    if debug:
        for name, shape in debug.items():
            dbg_out[name] = nc.dram_tensor("dbg_" + name, list(shape), F32, kind="ExternalOutput").ap()

    with contextlib.ExitStack() as st:
        P = Prog(nc, st)
        T = lambda name, shape, dt: st.enter_context(nc.sbuf_tensor(name, shape, dt))
        xT = T("xT_sb", [128, 8, 1024], F32)
        hT = T("hT", [128, 8, 1024], BF16)
        yT = T("yTm", [128, 8, 1024], BF16)
        wring = T("wring", [128, NSLOT, SLOTW], BF16)
        tabs = T("tabs_sb", [128, 2048], F32)
        arena_t = T("arena", [128, ARENA_WORDS], F32)
        sp = T("sp32_sb", [128, NS], F32)
        sb = T("spbf_sb", [128, NB], BF16)
        dmaskT = T("dmaskT", [128, 2, 4, 128], F32)
        qdec = T("qdec", [128, 4, 128], F32)
        kdec = T("kdec", [128, 8], F32)
        sdec = T("sdec", [128, 4], F32)
        lgcol = T("lgcol", [128, 8], F32)
        lgb = T("lgb", [128, 16], F32)
        lg128 = T("lg128", [128, 8], F32)
        mod_l = [T("mod%d" % i, [128, 48], F32) for i in range(2)]
        modA_l = [T("modA%d" % i, [128, 16], F32) for i in range(2)]
        silu_c = T("silu_c", [128, 8], BF16)
        ones_bf = T("ones_bf", [128, 128], BF16)
        ones256 = T("ones256", [128, 128], BF16)
        hbd = T("hbd", [128, 128], BF16)
        onef = T("onef", [1, 8], F32)
        rowtmp = T("rowtmp", [1, 2, 512], F32)
        rs_n = T("rs_n", [128, 2, 512], F32)
        sqb = T("sqb", [128, 4, 512], BF16)
        ntmp = T("ntmp", [128, 4, 512], F32)
        dummy = T("dummy_t", [128, 8], F32)
        ps = [st.enter_context(nc.psum_tensor("ps%d" % i, [128, 512], F32)) for i in range(8)]
        blk = st.enter_context(nc.Block())

        arena = Arena(arena_t, ARENA_WORDS)
        subc = {'n': 0}

        def sub():
            subc['n'] += 1
            if stop_sub is not None and subc['n'] >= stop_sub:
                raise _Stop()
        invc = tabs[:, 0:2048].rearrange("p (a b) -> p a b", a=2)

        def SP(name, *idx):
            o, n = SP32_OFF[name]
            return sp[:, o:o + n]

        def SB(name):
            o, n = SPBF_OFF[name]
            return sb[:, o:o + n]

        rot = {'big': 0, 'small': 0}

        def ps_big():
            i = rot['big']
            rot['big'] = (i + 1) % 4
            return ps[i], 'ps%d' % i

        def ps_small():
            i = 4 + rot['small']
            rot['small'] = (rot['small'] + 1) % 3
            return ps[i], 'ps%d' % i

        def EK():
            return 'EPOCH'

        def rop(eng, fn, reads=(), writes=()):
            return P.op(eng, fn, reads=list(reads) + [EK()], writes=writes)

        def release():
            P.op('dve', lambda e: e.memset(dummy[0:1, 0:8], 0.0), reads=[], writes=[EK(), 'dummy'])
            arena.reset()

        def dump(name, ap, reads):
            if debug and name in dbg_out:
                P.dma('sp', dbg_out[name], ap, reads=list(reads) + [EK()])

        wchunks = []
        wstate = {'issued': 0, 'slot': 0}
        wkeys = {}

        def wq_add(cid, dst_fn, parts):
            wchunks.append((cid, dst_fn, parts))

        def build_weight_queue():
            defs = {}
            for l in range(DEPTH):
                wa = d_wada[l].rearrange("(kt p) c -> p kt c", p=128)
                for cc in range(12):
                    defs[('ada', l, cc)] = (lambda s: s.rearrange("p (kt c) -> p kt c", kt=8),
                                            [(lambda v: v, wa[:, :, cc * 512:(cc + 1) * 512])])
                wi = d_win[l]
                for ti in range(2):
                    parts = []
                    for s4 in range(4):
                        c0 = 1280 + s4 * 256 + ti * 128
                        parts.append((lambda v, s4=s4: v[:, :, s4, :], wi[:, c0:c0 + 128].rearrange("(kt p) c -> p kt c", p=128)))
                    defs[('rqk', l, ti)] = (lambda s: s.rearrange("p (kt s c) -> p kt s c", kt=8, s=4), parts)
                    parts = []
                    for s2 in range(2):
                        c0 = 2304 + s2 * 256 + ti * 128
                        parts.append((lambda v, s2=s2: v[:, :, s2, :], wi[:, c0:c0 + 128].rearrange("(kt p) c -> p kt c", p=128)))
                    defs[('rvg', l, ti)] = (lambda s: s[:, 0:2048].rearrange("p (kt s c) -> p kt s c", kt=8, s=2), parts)
                defs[('conv', l)] = (lambda s: s.rearrange("p (kt c) -> p kt c", kt=8),
                                     [(lambda v: v, wi[:, 0:512].rearrange("(kt p) c -> p kt c", p=128))])
                defs[('gmlp', l)] = (lambda s: s.rearrange("p (kt c) -> p kt c", kt=8),
                                     [(lambda v: v, wi[:, 512:1024].rearrange("(kt p) c -> p kt c", p=128))])
                defs[('pool', l)] = (lambda s: s[:, 0:2048].rearrange("p (kt c) -> p kt c", kt=8),
                                     [(lambda v: v, wi[:, 1024:1280].rearrange("(kt p) c -> p kt c", p=128))])
                wo = d_wout[l].rearrange("(kt p) c -> p kt c", p=128)
                for hh in range(2):
                    defs[('wo', l, hh)] = (lambda s: s.rearrange("p (kt c) -> p kt c", kt=8),
                                           [(lambda v: v, wo[:, :, hh * 512:(hh + 1) * 512])])
                wf = d_wfi[l].rearrange("(kt p) f -> p kt f", p=128)
                for jj in range(11):
                    parts = []
                    for two in range(2):
                        c0 = two * 2816 + jj * 256
                        parts.append((lambda v, two=two: v[:, :, two, :], wf[:, :, c0:c0 + 256]))
                    defs[('ffi', l, jj)] = (lambda s: s.rearrange("p (kt two c) -> p kt two c", kt=8, two=2), parts)
                wfo = d_wfo[l].rearrange("(j p) c -> p j c", p=128)
                for ct in range(8):
                    for hf in range(2):
                        defs[('ffo', l, ct, hf)] = (lambda s: s[:, 0:2816].rearrange("p (j c) -> p j c", j=22),
                                                    [(lambda v: v, wfo[:, :, ct * 128:(ct + 1) * 128])])
            order = []
            order += [('ada', 0, cc) for cc in range(4)]
            order += [('rqk', 0, 0), ('rvg', 0, 0)] + [('ada', 0, cc) for cc in range(4, 8)] + [('ada', 1, cc) for cc in range(0, 3)]
            order += [('rqk', 0, 1), ('rvg', 0, 1)] + [('ada', 0, cc) for cc in range(8, 12)] + [('ada', 1, cc) for cc in range(3, 6)]
            order += [('pool', 0), ('conv', 0), ('gmlp', 0)] + [('ada', 1, cc) for cc in range(6, 9)]
            order += [('wo', 0, 0), ('wo', 0, 1)] + [('ada', 1, cc) for cc in range(9, 12)]
            order += [('ffi', 0, jj) for jj in range(11)]
            order += [('ffo', 0, ct, hf) for hf in range(2) for ct in range(8)]
            order += [('rqk', 1, 0), ('rvg', 1, 0), ('rqk', 1, 1), ('rvg', 1, 1), ('pool', 1), ('conv', 1), ('gmlp', 1),
                      ('wo', 1, 0), ('wo', 1, 1)]
            order += [('ffi', 1, jj) for jj in range(11)] + [('ffo', 1, ct, hf) for hf in range(2) for ct in range(8)]
            assert len(order) == len(defs)
            for cid in order:
                wq_add(cid, defs[cid][0], defs[cid][1])

        def w_issue():
            i = wstate['issued']
            if i >= len(wchunks):
                return
            cid, dst_fn, parts = wchunks[i]
            slot = i % NSLOT
            dst = dst_fn(wring[:, slot, :])
            for (sel, src) in parts:
                P.dma('pool', sel(dst), src, writes=['wslot%d' % slot])
            wkeys[cid] = (slot, dst)
            wstate['issued'] = i + 1

        wnext = {'i': 0}

        def w_get(cid, ahead=0):
            i = wnext['i'] + ahead
            assert wchunks[i][0] == cid, (wchunks[i][0], cid)
            while wstate['issued'] <= i:
                w_issue()
            slot, dst = wkeys[cid]
            return dst, 'wslot%d' % slot

        def w_done():
            wnext['i'] += 1
            w_issue()
            while wstate['issued'] < min(len(wchunks), wnext['i'] + NSLOT):
                w_issue()

        build_weight_queue()

        xTd = d_x.rearrange("(kt p) t -> p kt t", p=128)
        P.dma('sp', sp[:], d_sp, writes=['sp'])
        for kt in range(8):
            P.dma('sp', xT[:, kt, :], xTd[:, kt, :], writes=['xh%d_0' % kt, 'xh%d_1' % kt])
        P.dma('sp', tabs[:], d_tabs[:, 2048:4096], writes=['tabs'])
        for _ in range(NSLOT):
            w_issue()
        P.dma('pool', sb[:], d_bf, writes=['sb'])
        P.op('dve', lambda e: e.memset(ones_bf[:], 1.0 / 1024.0), writes=['ones_bf'])
        P.op('dve', lambda e: e.memset(ones256[:], 1.0 / 256.0), writes=['ones256'])
        P.op('dve', lambda e: e.memset(hbd[:], 0.0), writes=['hbd'])
        P.op('dve', lambda e: e.memset(hbd[0:64, 0:64], 1.0 / 64.0), reads=['hbd'], writes=['hbd'])
        P.op('dve', lambda e: e.memset(hbd[64:128, 64:128], 1.0 / 64.0), reads=['hbd'], writes=['hbd'])
        P.op('dve', lambda e: e.memset(onef[:], 1.0), writes=['onef'])
        P.op('act', lambda e: e.activation(out=silu_c[:], in_=SP('cvec'), func=AF.Silu), reads=['sp'], writes=['silu_c'])
        P.op('act', lambda e: e.activation(out=lgcol[:], in_=SP('rdcol'), func=AF.Exp, scale=-1.0), reads=['sp'], writes=['lgcol'])
        P.op('act', lambda e: e.activation(out=lgb[:], in_=SP('rdb'), func=AF.Exp, scale=-1.0), reads=['sp'], writes=['lgb'])
        P.op('act', lambda e: e.activation(out=lgcol[:], in_=lgcol[:], func=AF.Ln, bias=1.0), reads=['lgcol'], writes=['lgcol'])
        P.op('act', lambda e: e.activation(out=lgb[:], in_=lgb[:], func=AF.Ln, bias=1.0), reads=['lgb'], writes=['lgb'])
        P.op('dve', lambda e: e.tensor_scalar(out=lgcol[:], in0=lgcol[:], scalar1=-1.0, scalar2=None, op0=ALU.mult),
             reads=['lgcol'], writes=['lgcol'])
        P.op('dve', lambda e: e.tensor_scalar(out=lgb[:], in0=lgb[:], scalar1=-1.0, scalar2=None, op0=ALU.mult),
             reads=['lgb'], writes=['lgb'])
        P.op('dve', lambda e: e.tensor_scalar(out=lg128[:], in0=lgcol[:], scalar1=128.0, scalar2=None, op0=ALU.mult),
             reads=['lgcol'], writes=['lg128'])

        LN_KS = math.log(0.125)
        flag = SP('flag')

        def rmsnorm_stats(half):
            hs = slice(half * 512, (half + 1) * 512)
            pst, pk = ps_small()
            for kt in range(8):
                r = kt % 4
                P.op('act', lambda e, kt=kt, r=r: e.activation(out=sqb[:, r, :], in_=xT[:, kt, hs], func=AF.Square),
                     reads=['xh%d_%d' % (kt, half)], writes=['sqb%d' % r])
                P.op('pe', lambda e, kt=kt, r=r: e.matmul(pst[:], lhsT=ones_bf[:], rhs=sqb[:, r, :], start=(kt == 0), stop=(kt == 7)),
                     reads=['sqb%d' % r, 'ones_bf'], writes=[pk])
            P.op('act', lambda e: e.activation(out=rs_n[:, half, :], in_=pst[:], func=AF.Ln, bias=EPS, scale=1.0),
                 reads=[pk], writes=['rs_n%d' % half])
            P.op('act', lambda e: e.activation(out=rs_n[:, half, :], in_=rs_n[:, half, :], func=AF.Exp, scale=-0.5),
                 reads=['rs_n%d' % half], writes=['rs_n%d' % half])

        def norm_mod(l, which, skip_stats=False):
            Acol = modA_l[l][:, 8 * which:8 * which + 8]
            Bcol = mod_l[l][:, 24 * which:24 * which + 8]
            akey = 'modA%d_%d' % (l, which)
            bkey = 'modp%d_%d' % (l, 3 * which)
            if not skip_stats:
                rmsnorm_stats(0)
                rmsnorm_stats(1)
            cnt = 0
            for half in range(2):
                hs = slice(half * 512, (half + 1) * 512)
                for kt in range(8):
                    r = cnt % 4
                    cnt += 1
                    P.op('dve', lambda e, kt=kt, r=r, hs=hs, half=half: e.scalar_tensor_tensor(
                        out=ntmp[:, r, :], in0=xT[:, kt, hs], scalar=Acol[:, kt:kt + 1], in1=rs_n[:, half, :],
                        op0=ALU.mult, op1=ALU.mult), reads=['xh%d_%d' % (kt, half), 'rs_n%d' % half, akey], writes=['ntmp%d' % r])
                    P.op('act', lambda e, kt=kt, r=r, hs=hs: e.activation(
                        out=hT[:, kt, hs], in_=ntmp[:, r, :], func=AF.Identity, bias=Bcol[:, kt:kt + 1], scale=1.0),
                        reads=['ntmp%d' % r, bkey], writes=['hT%d_%d' % (kt, half)])

        hT_keys = lambda half: ['hT%d_%d' % (kt, half) for kt in range(8)]
        hT_all = hT_keys(0) + hT_keys(1)

        def mod_chunk(l, cc):
            mod_a(l, cc)
            mod_b(l, cc)

        def mod_group(l, ccs):
            prev = None
            for cc in ccs:
                mod_a(l, cc)
                if prev is not None:
                    mod_b(l, prev)
                prev = cc
            mod_b(l, prev)

        def mod_a(l, cc, bank=None):
            wv, wk = w_get(('ada', l, cc))
            prow, prk = (bank or ps_small)()

            def mm(e, wv=wv, prow=prow):
                for kt in range(8):
                    ins = e.matmul(prow[0:1, :], lhsT=silu_c[:, kt:kt + 1], rhs=wv[:, kt, :],
                                   start=(kt == 0), stop=(kt == 7))
                return ins
            P.op('pe', mm, reads=[wk, 'silu_c'], writes=[prk])
            w_done()
            r = cc % 2
            P.op('act', lambda e, prow=prow, r=r: e.activation(out=rowtmp[0:1, r, :], in_=prow[0:1, :], func=AF.Copy),
                 reads=[prk], writes=['rowtmp%d' % r])

        def mod_b(l, cc, bank=None):
            r = cc % 2
            pc, pck = (bank or ps_small)()

            def mmT(e, pc=pc, r=r):
                for j in range(4):
                    ins = e.matmul(pc[:, j:j + 1], lhsT=rowtmp[0:1, r, j * 128:(j + 1) * 128],
                                   rhs=onef[0:1, 0:1], start=True, stop=True)
                return ins
            P.op('pe', mmT, reads=['rowtmp%d' % r, 'onef'], writes=[pck])
            bo = SP32_OFF['b_fm'][0] + 48 * l + cc * 4
            mk = 'modp%d_%d' % (l, cc // 2)
            P.op('dve', lambda e, pc=pc, bo=bo: e.tensor_tensor(out=mod_l[l][:, cc * 4:cc * 4 + 4], in0=pc[:, 0:4], in1=sp[:, bo:bo + 4],
                                                                op=ALU.add), reads=[pck, 'sp', mk], writes=[mk])
            if cc == 3:
                g1o = SP32_OFF['g1'][0] + 8 * l
                P.op('dve', lambda e: e.scalar_tensor_tensor(out=modA_l[l][:, 0:8], in0=mod_l[l][:, 8:16], scalar=1.0,
                                                             in1=sp[:, g1o:g1o + 8], op0=ALU.add, op1=ALU.mult),
                     reads=['modp%d_1' % l, 'sp'], writes=['modA%d_0' % l])
            if cc == 9:
                g2o = SP32_OFF['g2'][0] + 8 * l
                P.op('dve', lambda e: e.scalar_tensor_tensor(out=modA_l[l][:, 8:16], in0=mod_l[l][:, 32:40], scalar=1.0,
                                                             in1=sp[:, g2o:g2o + 8], op0=ALU.add, op1=ALU.mult),
                     reads=['modp%d_4' % l, 'sp'], writes=['modA%d_1' % l])

        def decay_tables(l):
            do = SP32_OFF['diffT'][0]
            to = SP32_OFF['triT'][0]
            rq = SP32_OFF['ramp_q'][0]
            rk = SP32_OFF['rampcol_k'][0]
            for ti in range(2):
                for d in range(2):
                    for hl in range(2):
                        h = 2 * ti + hl
                        col = l * 8 + d * 4 + h
                        P.op('act', lambda e, ti=ti, d=d, hl=hl, col=col: e.activation(
                            out=dmaskT[:, ti, d * 2 + hl, :], in_=sp[:, do + d * 128:do + (d + 1) * 128], func=AF.Exp,
                            bias=LN_KS, scale=lgb[:, col:col + 1]), reads=['sp', 'lgb'], writes=['dmaskT'])
                        P.op('dve', lambda e, ti=ti, d=d, hl=hl: e.tensor_tensor(
                            out=dmaskT[:, ti, d * 2 + hl, :], in0=dmaskT[:, ti, d * 2 + hl, :],
                            in1=sp[:, to + d * 128:to + (d + 1) * 128], op=ALU.mult), reads=['dmaskT', 'sp'], writes=['dmaskT'])
            for d in range(2):
                for ti in range(2):
                    col = l * 4 + d * 2 + ti
                    P.op('act', lambda e, d=d, ti=ti, col=col: e.activation(
                        out=qdec[:, d * 2 + ti, :], in_=sp[:, rq + d * 128:rq + (d + 1) * 128], func=AF.Exp,
                        scale=lgcol[:, col:col + 1]), reads=['sp', 'lgcol'], writes=['qdec'])
                P.op('act', lambda e, d=d: e.activation(
                    out=kdec[:, d * 4:(d + 1) * 4], in_=lgb[:, l * 8 + d * 4:l * 8 + d * 4 + 4], func=AF.Exp,
                    bias=LN_KS, scale=sp[:, rk + d:rk + d + 1]), reads=['sp', 'lgb'], writes=['kdec'])
            P.op('act', lambda e: e.activation(out=sdec[:], in_=lg128[:, l * 4:(l + 1) * 4], func=AF.Exp),
                 reads=['lg128'], writes=['sdec'])

        fine = {'n': 0}

        def fm_group(wv_kt_fn, half, pst, pk, wk):
            hs = slice(half * 512, (half + 1) * 512)
            if fine['n'] > 0:
                fine['n'] -= 1
                tok = None
                for kt in range(8):
                    tok = P.op('pe', lambda e, kt=kt: e.matmul(pst[:], lhsT=wv_kt_fn(kt), rhs=hT[:, kt, hs], start=(kt == 0), stop=(kt == 7)),
                               reads=[wk, 'hT%d_%d' % (kt, half)], writes=[pk])
                return tok

            def mm(e):
                for kt in range(8):
                    ins = e.matmul(pst[:], lhsT=wv_kt_fn(kt), rhs=hT[:, kt, hs], start=(kt == 0), stop=(kt == 7))
                return ins
            return P.op('pe', mm, reads=[wk] + hT_keys(half), writes=[pk])

        def mixer_retention(l, ti, filler=None):
            A = arena
            qk = A.alloc(BF16, [128, 4, 1024])
            rawc = A.alloc(BF16, [128, 2, 512])
            raws = A.alloc(BF16, [128, 2, 512])
            qd = A.alloc(BF16, [128, 2, 1024])
            vr = A.alloc(BF16, [128, 8, 128])
            kd = A.alloc(BF16, [128, 2, 8, 128])
            sg = A.alloc(BF16, [128, 1024])
            am = A.alloc(BF16, [128, 2, 512])
            Sb = A.alloc(BF16, [128, 2, 8, 128])
            o_sb = A.alloc(F32, [128, 2, 512])
            ob = A.alloc(BF16, [128, 2, 512])
            rso = A.alloc(F32, [128, 2, 512])
            stage = A.alloc(F32, [128, 2, 4, 128])
            Stmp = None
            Scont = A.alloc(F32, [128, 2, 2, 128])
            scur = {0: 0, 1: 0}
            cs_t = A.alloc(F32, [128, 2, 1024])
            kz = A.alloc(BF16, [128, 2, 2, 1024])
            rop('dve', lambda e: e.memset(kz[:], 0.0), writes=['kz'])
            rop('pool', lambda e: e.memset(Sb[:], 0.0), writes=['Sb0', 'Sb1'])
            P.dma('sp', cs_t, d_tabs[:, 0:2048].rearrange("p (a b) -> p a b", a=2), reads=[EK()], writes=['cossin'])
            cosT = cs_t[:, 0, :]
            sinP = cs_t[:, 1, :]
            ident = SB('ident')
            Pm = SB('Pm')

            wv, wk = w_get(('rqk', l, ti))
            wv2, wk2 = w_get(('rvg', l, ti), ahead=1)
            vg_items = []

            def v_item(tp):
                pst, pk = ps_big()

                def mmv(e):
                    for u2 in range(2):
                        tt = tp * 2 + u2
                        for kt in range(8):
                            ins = e.matmul(pst[:, u2 * 128:(u2 + 1) * 128], lhsT=hT[:, kt, tt * 128:(tt + 1) * 128],
                                           rhs=wv2[:, kt, 0, :], start=(kt == 0), stop=(kt == 7))
                    return ins
                P.op('pe', mmv, reads=[wk2] + hT_all, writes=[pk])
                rop('act', lambda e: e.activation(
                    out=vr[:, tp * 2:tp * 2 + 2, :], in_=pst[:, 0:256].rearrange("p (a b) -> p a b", a=2), func=AF.Copy),
                    reads=[pk], writes=['vr'])

            def g_item(half):
                hs = slice(half * 512, (half + 1) * 512)
                pst, pk = ps_big()
                fm_group(lambda kt: wv2[:, kt, 1, :], half, pst, pk, wk2)
                rop('act', lambda e: e.activation(out=sg[:, hs], in_=pst[:], func=AF.Silu),
                    reads=[pk], writes=['sg%d' % half])
            for tp in range(4):
                vg_items.append(lambda tp=tp: v_item(tp))
            for half in range(2):
                vg_items.append(lambda half=half: g_item(half))
            def rope_tail(u, s4, half):
                rb = u % 2
                hs = slice(half * 512, (half + 1) * 512)
                ps2, pk2 = ps_small()

                def mmr(e, ps2=ps2, rb=rb):
                    e.matmul(ps2[:], lhsT=ident, rhs=rawc[:, rb, :], start=True, stop=False)
                    return e.matmul(ps2[:], lhsT=Pm, rhs=raws[:, rb, :], start=False, stop=True)
                rop('pe', mmr, reads=['rawc%d' % rb, 'raws%d' % rb, 'sb'], writes=[pk2])
                rop('act', lambda e, ps2=ps2, s4=s4, hs=hs: e.activation(out=qk[:, s4, hs], in_=ps2[:], func=AF.Copy),
                    reads=[pk2], writes=['qk%d_%d' % (s4, half)])
                if s4 % 2 == 1:
                    dd_ = s4 // 2
                    for hl in range(2):
                        r_ = slice(hl * 64, (hl + 1) * 64)
                        rop('act', lambda e, ps2=ps2, dd_=dd_, hl=hl, r_=r_, hs=hs: e.activation(
                            out=kz[r_, dd_, hl, hs], in_=ps2[r_, :], func=AF.Copy), reads=[pk2, 'kz'], writes=['kz'])

            pend = None
            u = 0
            for s4 in range(4):
                for half in range(2):
                    hs = slice(half * 512, (half + 1) * 512)
                    rb = u % 2
                    pst, pk = ps_big()
                    fm_group(lambda kt, s4=s4, wv=wv: wv[:, kt, s4, :], half, pst, pk, wk)
                    rop('dve', lambda e, pst=pst, hs=hs, rb=rb: e.tensor_tensor(out=rawc[:, rb, :], in0=pst[:], in1=cosT[:, hs], op=ALU.mult),
                        reads=[pk, 'cossin'], writes=['rawc%d' % rb])
                    rop('dve', lambda e, pst=pst, hs=hs, rb=rb: e.tensor_tensor(out=raws[:, rb, :], in0=pst[:], in1=sinP[:, hs], op=ALU.mult),
                        reads=[pk, 'cossin'], writes=['raws%d' % rb])
                    if pend is not None:
                        rope_tail(*pend)
                        if vg_items:
                            vg_items.pop(0)()
                    pend = (u, s4, half)
                    u += 1
            rope_tail(*pend)
            while vg_items:
                vg_items.pop(0)()
            w_done()
            sub()
            w_done()
            sub()
            qkk = lambda s4: ['qk%d_0' % s4, 'qk%d_1' % s4]
            for d in range(2):
                rop('pool', lambda e, d=d: e.tensor_tensor(
                    out=qd[:, d, :].rearrange("p (c i) -> p c i", c=8),
                    in0=qk[:, 2 * d, :].rearrange("p (c i) -> p c i", c=8),
                    in1=qdec[:, d * 2 + ti, :].unsqueeze(1).to_broadcast([128, 8, 128]), op=ALU.mult),
                    reads=qkk(2 * d) + ['qdec'], writes=['qd%d' % d])
            sub()
            pT = ps[7].bitcast(BF16)
            rnd = 0
            for d in range(2):
                for hc in range(2):
                    hb = rnd % 2
                    rnd += 1
                    if hb == 0:
                        pTh = pT[:, 0:512]
                        pkey = 'ps7'
                    else:
                        pss_, pkey = ps_small()
                        pTh = pss_.bitcast(BF16)[:, 0:512]

                    def tr(e, d=d, hc=hc, pTh=pTh):
                        for c4 in range(4):
                            c = hc * 4 + c4
                            ins = e.transpose(out=pTh[:, c4 * 128:(c4 + 1) * 128], in_=qk[:, 2 * d + 1, c * 128:(c + 1) * 128],
                                              identity=ident)
                        return ins
                    rop('pe', tr, reads=qkk(2 * d + 1) + ['sb'], writes=[pkey])
                    if hb == 0:
                        rop('dve', lambda e, d=d, hc=hc, pTh=pTh: e.tensor_tensor(
                            out=kd[:, d, hc * 4:(hc + 1) * 4, :].rearrange("p c (h x) -> p c h x", h=2),
                            in0=pTh.rearrange("p (c h x) -> p c h x", c=4, h=2),
                            in1=kdec[:, d * 4 + 2 * ti:d * 4 + 2 * ti + 2].unsqueeze(1).unsqueeze(3).to_broadcast([128, 4, 2, 64]),
                            op=ALU.mult), reads=[pkey, 'kdec'], writes=['kd%d' % d])
                    else:
                        for h in range(2):
                            rop('act', lambda e, d=d, hc=hc, pTh=pTh, h=h: e.activation(
                                out=kd[:, d, hc * 4:(hc + 1) * 4, h * 64:(h + 1) * 64],
                                in_=pTh.rearrange("p (c h x) -> p c h x", c=4, h=2)[:, :, h, :], func=AF.Copy,
                                scale=kdec[:, d * 4 + 2 * ti + h:d * 4 + 2 * ti + h + 1]),
                                reads=[pkey, 'kdec'], writes=['kd%d' % d])
            sub()
            bdm = SP('bdmask')
            s0v = lambda d: d_s0[:, l * 4 + d * 2 + ti, :]
            for d in range(2):
                P.dma('sp', Scont[:, 0, d, :], s0v(d), reads=[EK()], writes=['Sc%d_0' % d])
            pU = {}
            for d in range(2):
                for hc in range(2):
                    pst, pk = ps_big()

                    def mmu(e, d=d, hc=hc, pst=pst):
                        for c4 in range(4):
                            c = hc * 4 + c4
                            ins = e.matmul(pst[:, c4 * 128:(c4 + 1) * 128], lhsT=kd[:, d, c, :], rhs=vr[:, c, :],
                                           start=True, stop=True)
                        return ins
                    rop('pe', mmu, reads=['kd%d' % d, 'vr'], writes=[pk])
                    pU[(d, hc)] = (pst, pk)
            psO_h = [(ps[7], 'ps7'), None]

            def att_a(c):
                cs = slice(c * 128, (c + 1) * 128)
                psA, pkA = ps_small()

                def mma(e, psA=psA, cs=cs):
                    for d in range(2):
                        for hl in range(2):
                            ins = e.matmul(psA[:, (d * 2 + hl) * 128:(d * 2 + hl + 1) * 128], lhsT=kz[:, d, hl, cs],
                                           rhs=qk[:, 2 * d, cs], start=True, stop=True)
                    return ins
                rop('pe', mma, reads=qkk(0) + qkk(2) + ['kz'], writes=[pkA])
                a_ = c % 2
                rop('dve', lambda e, psA=psA, a_=a_: e.tensor_tensor(
                    out=am[:, a_, :], in0=psA[:], in1=dmaskT[:, ti, :, :].rearrange("p a b -> p (a b)"), op=ALU.mult),
                    reads=[pkA, 'dmaskT'], writes=['am%d' % a_])

            def att_intra(c):
                half, c4 = c // 4, c % 4
                if psO_h[half] is None:
                    psO_h[half] = ps_big()
                psO, pkO = psO_h[half]
                a_ = c % 2

                def mmi(e, psO=psO, c=c, c4=c4, a_=a_):
                    oc = slice(c4 * 128, (c4 + 1) * 128)
                    for hl in range(2):
                        for d in range(2):
                            ins = e.matmul(psO[hl * 64:(hl + 1) * 64, oc], lhsT=vr[:, c, hl * 64:(hl + 1) * 64],
                                           rhs=am[:, a_, (d * 2 + hl) * 128:(d * 2 + hl + 1) * 128],
                                           start=(c4 == 0 and d == 0), stop=False, tile_position=(0, hl * 64),
                                           skip_group_check=True)
                    return ins
                rop('pe', mmi, reads=['am%d' % a_, 'vr'], writes=[pkO])

            def att_inter(c):
                half, c4 = c // 4, c % 4
                psO, pkO = psO_h[half]
                cs = slice(c * 128, (c + 1) * 128)

                def mmx(e, psO=psO, c=c, c4=c4, cs=cs):
                    oc = slice(c4 * 128, (c4 + 1) * 128)
                    e.matmul(psO[:, oc], lhsT=Sb[:, 0, c, :], rhs=qd[:, 0, cs], start=False, stop=False, skip_group_check=True)
                    return e.matmul(psO[:, oc], lhsT=Sb[:, 1, c, :], rhs=qd[:, 1, cs], start=False, stop=False,
                                    skip_group_check=True)
                rop('pe', mmx, reads=['Sb0', 'Sb1', 'qd0', 'qd1'], writes=[pkO])

            for i in range(8):
                for d in range(2):
                    c = i if d == 0 else 7 - i
                    pst, pk = pU[(d, c // 4)]
                    recur(l, ti, d, [c], pst, pk, Sb, stage, Stmp, Scont, bdm, scur)
                att_a(i)
                if i >= 1:
                    att_intra(i - 1)
            att_intra(7)
            sub()
            for c in range(8):
                att_inter(c)
            sub()
            steps = []
            for half in range(2):
                hs = slice(half * 512, (half + 1) * 512)
                psO, pkO = psO_h[half]
                o_h = o_sb[:, half, :]
                ob_h = ob[:, half, :]
                rs_h = rso[:, half, :]
                ko, kb, kr = 'o_sb%d' % half, 'ob%d' % half, 'rso%d' % half
                st_ = []
                st_.append(lambda psO=psO, pkO=pkO, o_h=o_h, ko=ko: rop(
                    'act', lambda e: e.activation(out=o_h, in_=psO[:], func=AF.Copy), reads=[pkO], writes=[ko]))
                st_.append(lambda psO=psO, pkO=pkO, ob_h=ob_h, kb=kb: rop(
                    'act', lambda e: e.activation(out=ob_h, in_=psO[:], func=AF.Copy), reads=[pkO], writes=[kb]))
                psM_box = {}

                def s_mean(ob_h=ob_h, kb=kb, box=psM_box):
                    psM, pkM = ps_small()
                    box['m'] = (psM, pkM)
                    rop('pe', lambda e: e.matmul(psM[:], lhsT=hbd[:], rhs=ob_h, start=True, stop=True), reads=[kb, 'hbd'], writes=[pkM])
                st_.append(s_mean)

                def s_cen(o_h=o_h, ko=ko, box=psM_box):
                    psM, pkM = box['m']
                    rop('dve', lambda e: e.tensor_tensor(out=o_h, in0=o_h, in1=psM[:], op=ALU.subtract), reads=[ko, pkM], writes=[ko])
                st_.append(s_cen)
                st_.append(lambda o_h=o_h, ko=ko, ob_h=ob_h, kb=kb: rop(
                    'act', lambda e: e.activation(out=ob_h, in_=o_h, func=AF.Square), reads=[ko], writes=[kb]))

                def s_var(ob_h=ob_h, kb=kb, box=psM_box):
                    psV, pkV = ps_small()
                    box['v'] = (psV, pkV)
                    rop('pe', lambda e: e.matmul(psV[:], lhsT=hbd[:], rhs=ob_h, start=True, stop=True), reads=[kb, 'hbd'], writes=[pkV])
                st_.append(s_var)

                def s_ln(rs_h=rs_h, kr=kr, box=psM_box):
                    psV, pkV = box['v']
                    rop('act', lambda e: e.activation(out=rs_h, in_=psV[:], func=AF.Ln, bias=EPS, scale=1.0), reads=[pkV], writes=[kr])
                st_.append(s_ln)
                st_.append(lambda rs_h=rs_h, kr=kr: rop(
                    'act', lambda e: e.activation(out=rs_h, in_=rs_h, func=AF.Exp, scale=-0.5), reads=[kr], writes=[kr]))
                st_.append(lambda o_h=o_h, ko=ko, rs_h=rs_h, kr=kr: rop(
                    'dve', lambda e: e.tensor_tensor(out=o_h, in0=o_h, in1=rs_h, op=ALU.mult), reads=[ko, kr], writes=[ko]))
                st_.append(lambda o_h=o_h, ko=ko, hs=hs, half=half: rop(
                    'dve', lambda e: e.tensor_tensor(out=yT[:, 6 + ti, hs], in0=o_h, in1=sg[:, hs], op=ALU.mult),
                    reads=[ko, 'sg%d' % half], writes=['y%d_%d' % (6 + ti, half)]))
                steps.append(st_)
            fch = list(filler) if filler is not None else []
            for i in range(len(steps[0])):
                steps[0][i]()
                steps[1][i]()
                if fch and i % 2 == 0:
                    j = i // 2
                    if j < len(fch):
                        mod_a(0, fch[j], bank=ps_big)
                    if 1 <= j <= len(fch):
                        mod_b(0, fch[j - 1], bank=ps_big)
            if fch and len(steps[0]) // 2 < len(fch) + 1:
                nsteps = (len(steps[0]) + 1) // 2
                for j in range(nsteps, len(fch) + 1):
                    if j < len(fch):
                        mod_a(0, fch[j])
                    mod_b(0, fch[j - 1])

        def recur(l, ti, d, chunks, pst, pk, Sb, stage, Stmp, Scont, bdm, scur):
            sd = sdec[:, d * 2 + ti:d * 2 + ti + 1]
            for c in chunks:
                c4 = c % 4
                cur = scur[d]
                nxt = 1 - cur
                kcur = 'Sc%d_%d' % (d, cur)
                knxt = 'Sc%d_%d' % (d, nxt)
                Sc = Scont[:, cur, d, :]
                Sn = Scont[:, nxt, d, :]
                for hl in range(2):
                    r = slice(hl * 64, (hl + 1) * 64)
                    rop('act', lambda e, c=c, r=r, Sc=Sc: e.activation(out=Sb[r, d, c, r], in_=Sc[r, r], func=AF.Copy),
                        reads=[kcur], writes=['Sb%d' % d])
                seg_end = (c % 2 == 1) if d == 0 else (c % 2 == 0)
                last = (c == 7) if d == 0 else (c == 0)
                seg = c // 2
                if seg_end:
                    dst = stage[:, d, seg, :]
                    dkey = 'stage%d_%d' % (d, seg)
                    rop('dve', lambda e, c4=c4, dst=dst, Sc=Sc: e.scalar_tensor_tensor(
                        out=dst, in0=Sc, scalar=sd, in1=pst[:, c4 * 128:(c4 + 1) * 128], op0=ALU.mult, op1=ALU.add),
                        reads=[kcur, pk, 'sdec'], writes=[dkey])
                    for hl in range(2):
                        r = slice(hl * 64, (hl + 1) * 64)
                        P.dma('sp', d_st[l, d, seg, 2 * ti + hl], stage[r, d, seg, hl * 64:(hl + 1) * 64], reads=[dkey, EK()])
                    if not last:
                        rop('dve', lambda e, dst=dst, Sn=Sn: e.tensor_scalar(out=Sn, in0=dst, scalar1=flag, scalar2=None, op0=ALU.mult),
                            reads=[dkey, 'sp'], writes=[knxt])
                        scur[d] = nxt
                else:
                    rop('dve', lambda e, c4=c4, Sc=Sc, Sn=Sn: e.scalar_tensor_tensor(
                        out=Sn, in0=Sc, scalar=sd, in1=pst[:, c4 * 128:(c4 + 1) * 128], op0=ALU.mult, op1=ALU.add),
                        reads=[kcur, pk, 'sdec'], writes=[knxt])
                    scur[d] = nxt

        def mixer_conv_gmlp_pool(l):
            A = arena
            hcp = A.alloc(BF16, [128, 2, 4, 286])
            sigt = A.alloc(F32, [128, 2, 512])
            acc = A.alloc(F32, [128, 2, 1024])
            accb = A.alloc(BF16, [128, 2, 512])
            sqc = A.alloc(BF16, [128, 2, 512])
            rsc = A.alloc(F32, [128, 512])
            hsb = A.alloc(BF16, [128, 2, 512])
            dg = A.alloc(BF16, [128, 2, 8, 128])
            u_sb = A.alloc(F32, [128, 2, 1024])
            vg = A.alloc(BF16, [128, 8, 256])
            gtmp = sigt
            pp = A.alloc(F32, [128, 2, 4, 272])
            pA = A.alloc(F32, [128, 4, 272])
            pB = A.alloc(F32, [128, 4, 272])
            dpb = A.alloc(BF16, [128, 2, 1024])

            def pool_part1():
                rop('dve', lambda e: e.memset(pp[:], 0.0), writes=['pp'])
                wv, wk = w_get(('pool', l))
                for ti in range(2):
                    for half in range(2):
                        pst, pk = ps_big()
                        fm_group(lambda kt, ti=ti, wv=wv: wv[:, kt, ti * 128:(ti + 1) * 128], half, pst, pk, wk)
                        rop('act', lambda e, pst=pst, ti=ti, half=half: e.activation(
                            out=pp[:, ti, 2 * half:2 * half + 2, 8:264], in_=pst[:].rearrange("p (s x) -> p s x", s=2), func=AF.Copy),
                            reads=[pk, 'pp'], writes=['pp'])
                w_done()

            def pool_chain():
                for ti in range(2):
                    rop('pool', lambda e, ti=ti: e.tensor_scalar(out=pp[:, ti, 1:4, 0:8], in0=pp[:, ti, 0:3, 256:264], scalar1=flag,
                                                                 scalar2=None, op0=ALU.mult), reads=['pp', 'sp'], writes=['pp'])
                    rop('pool', lambda e, ti=ti: e.tensor_scalar(out=pp[:, ti, 0:3, 264:272], in0=pp[:, ti, 1:4, 8:16], scalar1=flag,
                                                                 scalar2=None, op0=ALU.mult), reads=['pp', 'sp'], writes=['pp'])
                pbd = SB('pool_bd').rearrange("p (l t c) -> p l t c", l=2, t=2)
                pso = SP32_OFF['pool_scale'][0] + l * 2
                for ti in range(2):
                    pv = pp[:, ti]
                    rop('pool', lambda e, pv=pv: e.tensor_tensor(out=pA[:, :, 1:272], in0=pv[:, :, 1:272], in1=pv[:, :, 0:271], op=ALU.add),
                        reads=['pp', 'pA'], writes=['pA'])
                    rop('pool', lambda e: e.tensor_tensor(out=pB[:, :, 2:271], in0=pA[:, :, 1:270], in1=pA[:, :, 3:272], op=ALU.add),
                        reads=['pA', 'pB'], writes=['pB'])
                    if ti == 0:
                        lo_src, hi_src = pA, pB
                    else:
                        rop('pool', lambda e: e.tensor_tensor(out=pA[:, :, 4:269], in0=pB[:, :, 2:267], in1=pB[:, :, 6:271], op=ALU.add),
                            reads=['pB', 'pA'], writes=['pA'])
                        rop('pool', lambda e: e.tensor_tensor(out=pB[:, :, 8:264], in0=pA[:, :, 4:260], in1=pA[:, :, 12:268], op=ALU.add),
                            reads=['pA', 'pB'], writes=['pB'])
                        lo_src, hi_src = pA, pB
                    for (r, src) in ((slice(0, 64), lo_src), (slice(64, 128), hi_src)):
                        rop('pool', lambda e, r=r, src=src, ti=ti: e.tensor_tensor(
                            out=src[r, :, 8:264], in0=src[r, :, 8:264], in1=invc[r, ti, :].rearrange("p (s x) -> p s x", s=4), op=ALU.mult),
                            reads=['pA', 'pB', 'tabs'], writes=['pA', 'pB'])
                        rop('pool', lambda e, r=r, src=src, ti=ti, pv=pv: e.tensor_tensor(
                            out=dpb[r, ti, :].rearrange("p (s x) -> p s x", s=4), in0=src[r, :, 8:264], in1=pv[r, :, 8:264], op=ALU.subtract),
                            reads=['pA', 'pB', 'pp'], writes=['dpb%d' % ti])

            def pool2_item(ti, half):
                pbd = SB('pool_bd').rearrange("p (l t c) -> p l t c", l=2, t=2)
                pso = SP32_OFF['pool_scale'][0] + l * 2
                hs = slice(half * 512, (half + 1) * 512)
                pst, pk = ps_small()
                rop('pe', lambda e: e.matmul(pst[:], lhsT=pbd[:, l, ti, :], rhs=dpb[:, ti, hs], start=True, stop=True),
                    reads=['dpb%d' % ti, 'sb'], writes=[pk])
                rop('act', lambda e: e.activation(
                    out=yT[:, 4 + ti, hs], in_=pst[:], func=AF.Identity, scale=sp[:, pso + ti:pso + ti + 1]),
                    reads=[pk, 'sp'], writes=['y%d_%d' % (4 + ti, half)])

            rop('dve', lambda e: e.memset(hcp[:], 0.0), writes=['hcp'])
            pool_part1()
            wv, wk = w_get(('conv', l))
            for ti in range(2):
                for half in range(2):
                    r = half
                    pst, pk = ps_big()
                    fm_group(lambda kt, ti=ti, wv=wv: wv[:, kt, (2 + ti) * 128:(3 + ti) * 128], half, pst, pk, wk)
                    rop('act', lambda e, pst=pst, r=r: e.activation(out=sigt[:, r, :], in_=pst[:], func=AF.Sigmoid),
                        reads=[pk], writes=['sigt%d' % r])
                    pst2, pk2 = ps_big()
                    fm_group(lambda kt, ti=ti, wv=wv: wv[:, kt, ti * 128:(ti + 1) * 128], half, pst2, pk2, wk)
                    rop('dve', lambda e, pst2=pst2, r=r, ti=ti, half=half: e.tensor_tensor(
                        out=hcp[:, ti, 2 * half:2 * half + 2, 15:271], in0=pst2[:].rearrange("p (s x) -> p s x", s=2),
                        in1=sigt[:, r, :].rearrange("p (s x) -> p s x", s=2), op=ALU.mult),
                        reads=[pk2, 'sigt%d' % r, 'hcp'], writes=['hcp'])
            w_done()
            for ti in range(2):
                rop('act', lambda e, ti=ti: e.activation(out=hcp[:, ti, 1:4, 0:15], in_=hcp[:, ti, 0:3, 256:271], func=AF.Copy,
                                                         scale=flag), reads=['hcp', 'sp'], writes=['hcp'])
                rop('act', lambda e, ti=ti: e.activation(out=hcp[:, ti, 0:3, 271:286], in_=hcp[:, ti, 1:4, 15:30], func=AF.Copy,
                                                         scale=flag), reads=['hcp', 'sp'], writes=['hcp'])
            pool_chain()
            dwo = SP32_OFF['conv_dw'][0]
            cbo = SP32_OFF['conv_b'][0]
            identb = SB('ident')
            psC = {}
            for ti in range(2):
                for half in range(2):
                    psC[(ti, half)] = ps_big()
            dgf = dg.rearrange("p a b c -> p (a b) c")
            dcnt = 0
            for k in range(31):
                for ti in range(2):
                    base = dwo + (l * 2 + ti) * 31
                    r = dcnt % 16
                    dcnt += 1
                    rop('act', lambda e, r=r, base=base, k=k: e.activation(
                        out=dgf[:, r, :], in_=identb, func=AF.Copy, scale=sp[:, base + k:base + k + 1]),
                        reads=['sb', 'sp'], writes=['dg%d' % r])
                    for half in range(2):
                        pst, pk = psC[(ti, half)]
                        rop('pe', lambda e, pst=pst, r=r, ti=ti, half=half, k=k: e.matmul(
                            pst[:].rearrange("p (s x) -> p s x", s=2), lhsT=dgf[:, r, :],
                            rhs=hcp[:, ti, 2 * half:2 * half + 2, k:k + 256], start=(k == 0), stop=(k == 30)),
                            reads=['dg%d' % r, 'hcp'], writes=[pk])
            for ti in range(2):
                for half in range(2):
                    hs = slice(half * 512, (half + 1) * 512)
                    pst, pk = psC[(ti, half)]
                    rop('act', lambda e, pst=pst, ti=ti, hs=hs: e.activation(
                        out=acc[:, ti, hs], in_=pst[:], func=AF.Identity, bias=sp[:, cbo + l * 2 + ti:cbo + l * 2 + ti + 1], scale=1.0),
                        reads=[pk, 'sp'], writes=['acc%d' % ti])
            wvg, wkg = w_get(('gmlp', l))
            G = []

            def gu(ti, half):
                hs = slice(half * 512, (half + 1) * 512)
                pst, pk = ps_big()
                fm_group(lambda kt: wvg[:, kt, ti * 128:(ti + 1) * 128], half, pst, pk, wkg)
                rop('act', lambda e: e.activation(out=u_sb[:, ti, hs], in_=pst[:], func=AF.Copy),
                    reads=[pk], writes=['u%d_%d' % (ti, half)])

            def gv(tp):
                pst, pk = ps_big()

                def mmv(e):
                    for u2 in range(2):
                        tt = tp * 2 + u2
                        for kt in range(8):
                            ins = e.matmul(pst[:, u2 * 256:(u2 + 1) * 256], lhsT=hT[:, kt, tt * 128:(tt + 1) * 128],
                                           rhs=wvg[:, kt, 256:512], start=(kt == 0), stop=(kt == 7))
                    return ins
                P.op('pe', mmv, reads=[wkg] + hT_all, writes=[pk])
                rop('act', lambda e: e.activation(
                    out=vg[:, tp * 2:tp * 2 + 2, :], in_=pst[:].rearrange("p (a b) -> p a b", a=2), func=AF.Copy),
                    reads=[pk], writes=['vg'])
                if tp == 3:
                    w_done()

            wsT = SB('wsT').rearrange("p (l h q) -> p l h q", l=2, h=4)
            gbo = SP32_OFF['gb_tab'][0]

            def gs(ti, half):
                hs = slice(half * 512, (half + 1) * 512)
                pst, pk = ps_small()

                def mmg(e):
                    for c4 in range(4):
                        c = half * 4 + c4
                        for hl in range(2):
                            h = 2 * ti + hl
                            ins = e.matmul(pst[hl * 64:(hl + 1) * 64, c4 * 128:(c4 + 1) * 128], lhsT=vg[:, c, h * 64:(h + 1) * 64],
                                           rhs=wsT[:, l, h, :], start=True, stop=True, tile_position=(0, hl * 64))
                    return ins
                rop('pe', mmg, reads=['vg', 'sb'], writes=[pk])
                g = half
                gb = sp[:, gbo + (l * 2 + ti) * 128:gbo + (l * 2 + ti + 1) * 128]
                rop('dve', lambda e: e.tensor_tensor(
                    out=gtmp[:, g, :].rearrange("p (c i) -> p c i", c=4), in0=pst[:].rearrange("p (c i) -> p c i", c=4),
                    in1=gb.unsqueeze(1).to_broadcast([128, 4, 128]), op=ALU.add), reads=[pk, 'sp'], writes=['sigt%d' % g])
                rop('dve', lambda e: e.tensor_tensor(out=yT[:, 2 + ti, hs], in0=gtmp[:, g, :], in1=u_sb[:, ti, hs], op=ALU.mult),
                    reads=['sigt%d' % g, 'u%d_%d' % (ti, half)], writes=['y%d_%d' % (2 + ti, half)])

            for ti in range(2):
                for half in range(2):
                    G.append(lambda ti=ti, half=half: gu(ti, half))
            for tp in range(4):
                G.append(lambda tp=tp: gv(tp))
            for ti in range(2):
                for half in range(2):
                    G.append(lambda ti=ti, half=half: gs(ti, half))
            for ti in range(2):
                for half in range(2):
                    G.append(lambda ti=ti, half=half: pool2_item(ti, half))

            lgo = SP32_OFF['ln_g'][0] + l * 2
            lbo = SP32_OFF['ln_b'][0] + l * 2
            pwv = SB('conv_pw').rearrange("p (l k c) -> p l k c", l=2, k=2)
            L = []
            for half in range(2):
                hs = slice(half * 512, (half + 1) * 512)
                box = {}

                def l1(hs=hs):
                    rop('act', lambda e: e.activation(out=accb[:], in_=acc[:, :, hs], func=AF.Copy),
                        reads=['acc0', 'acc1'], writes=['accb'])

                def l2(box=box):
                    psM, pkM = ps_small()
                    box['m'] = (psM, pkM)

                    def mmm(e):
                        e.matmul(psM[:], lhsT=ones256[:], rhs=accb[:, 0, :], start=True, stop=False)
                        return e.matmul(psM[:], lhsT=ones256[:], rhs=accb[:, 1, :], start=False, stop=True)
                    rop('pe', mmm, reads=['accb', 'ones256'], writes=[pkM])

                def l3(hs=hs, box=box):
                    psM, pkM = box['m']
                    for ti in range(2):
                        rop('dve', lambda e, ti=ti: e.tensor_tensor(out=acc[:, ti, hs], in0=acc[:, ti, hs], in1=psM[:], op=ALU.subtract),
                            reads=['acc%d' % ti, pkM], writes=['acc%d' % ti])

                def l4(hs=hs):
                    rop('act', lambda e: e.activation(out=sqc[:], in_=acc[:, :, hs], func=AF.Square),
                        reads=['acc0', 'acc1'], writes=['sqc'])

                def l5(box=box):
                    psV, pkV = ps_small()
                    box['v'] = (psV, pkV)

                    def mmv2(e):
                        e.matmul(psV[:], lhsT=ones256[:], rhs=sqc[:, 0, :], start=True, stop=False)
                        return e.matmul(psV[:], lhsT=ones256[:], rhs=sqc[:, 1, :], start=False, stop=True)
                    rop('pe', mmv2, reads=['sqc', 'ones256'], writes=[pkV])

                def l6(box=box):
                    psV, pkV = box['v']
                    rop('act', lambda e: e.activation(out=rsc[:], in_=psV[:], func=AF.Ln, bias=EPS, scale=1.0), reads=[pkV], writes=['rsc'])
                    rop('act', lambda e: e.activation(out=rsc[:], in_=rsc[:], func=AF.Exp, scale=-0.5), reads=['rsc'], writes=['rsc'])

                def l7(hs=hs):
                    for ti in range(2):
                        rop('dve', lambda e, ti=ti: e.scalar_tensor_tensor(
                            out=acc[:, ti, hs], in0=acc[:, ti, hs], scalar=sp[:, lgo + ti:lgo + ti + 1], in1=rsc[:],
                            op0=ALU.mult, op1=ALU.mult), reads=['acc%d' % ti, 'rsc', 'sp'], writes=['acc%d' % ti])
                        rop('act', lambda e, ti=ti: e.activation(
                            out=hsb[:, ti, :], in_=acc[:, ti, hs], func=AF.Silu, bias=sp[:, lbo + ti:lbo + ti + 1], scale=1.0),
                            reads=['acc%d' % ti, 'sp'], writes=['hsb%d' % ti])

                def l8(to, hs=hs, half=half):
                    pst, pk = ps_small()

                    def mmp(e):
                        e.matmul(pst[:], lhsT=pwv[:, l, 0, to * 128:(to + 1) * 128], rhs=hsb[:, 0, :], start=True, stop=False)
                        return e.matmul(pst[:], lhsT=pwv[:, l, 1, to * 128:(to + 1) * 128], rhs=hsb[:, 1, :], start=False, stop=True)
                    rop('pe', mmp, reads=['hsb0', 'hsb1', 'sb'], writes=[pk])
                    rop('act', lambda e: e.activation(out=yT[:, to, hs], in_=pst[:], func=AF.Copy),
                        reads=[pk], writes=['y%d_%d' % (to, half)])
                L += [l1, l2, l3, l4, l5, l6, l7, lambda l8=l8: l8(0), lambda l8=l8: l8(1)]
            while L or G:
                if L:
                    L.pop(0)()
                if G:
                    G.pop(0)()

        def out_proj(l):
            y_keys = lambda half: ['y%d_%d' % (t, half) for t in range(8)]
            wv0, wk0 = w_get(('wo', l, 0))
            wv1, wk1 = w_get(('wo', l, 1), ahead=1)
            for half in range(2):
                hs = slice(half * 512, (half + 1) * 512)
                for ct in range(8):
                    wv, wk = (wv0, wk0) if ct < 4 else (wv1, wk1)
                    c4 = ct % 4
                    pst, pk = ps_big()

                    def mm(e, pst=pst, wv=wv, c4=c4, hs=hs):
                        for kt in range(8):
                            ins = e.matmul(pst[:], lhsT=wv[:, kt, c4 * 128:(c4 + 1) * 128], rhs=yT[:, kt, hs],
                                           start=(kt == 0), stop=(kt == 7))
                        return ins
                    P.op('pe', mm, reads=[wk] + y_keys(half), writes=[pk])
                    P.op('dve', lambda e, pst=pst, ct=ct, hs=hs: e.scalar_tensor_tensor(
                        out=xT[:, ct, hs], in0=pst[:], scalar=mod_l[l][:, 16 + ct:17 + ct], in1=xT[:, ct, hs], op0=ALU.mult, op1=ALU.add),
                        reads=[pk, 'modp%d_2' % l, 'xh%d_%d' % (ct, half)], writes=['xh%d_%d' % (ct, half)])
            w_done()
            w_done()

        def ffn(l):
            A = arena
            act = A.alloc(BF16, [128, 22, 1024])
            sgt = A.alloc(F32, [128, 2, 512])
            for jj in range(11):
                if l == 0 and jj == 3:
                    decay_tables(1)
                wv, wk = w_get(('ffi', l, jj))
                for j2 in range(2):
                    j = jj * 2 + j2
                    for half in range(2):
                        hs = slice(half * 512, (half + 1) * 512)
                        psG, pkG = ps_big()
                        fm_group(lambda kt, j2=j2, wv=wv: wv[:, kt, 0, j2 * 128:(j2 + 1) * 128], half, psG, pkG, wk)
                        r = half
                        rop('act', lambda e, psG=psG, r=r: e.activation(out=sgt[:, r, :], in_=psG[:], func=AF.Silu),
                            reads=[pkG], writes=['sgt%d' % r])
                        psU, pkU = ps_big()
                        fm_group(lambda kt, j2=j2, wv=wv: wv[:, kt, 1, j2 * 128:(j2 + 1) * 128], half, psU, pkU, wk)
                        rop('dve', lambda e, psU=psU, r=r, j=j, hs=hs: e.tensor_tensor(out=act[:, j, hs], in0=psU[:], in1=sgt[:, r, :],
                                                                                     op=ALU.mult),
                            reads=[pkU, 'sgt%d' % r], writes=['act%d_%d' % (j, half)])
                w_done()
            for half in range(2):
                hs = slice(half * 512, (half + 1) * 512)
                for ct in range(8):
                    wv, wk = w_get(('ffo', l, ct, half))
                    pst, pk = ps_big()

                    def mm(e, pst=pst, wv=wv, hs=hs):
                        for j in range(22):
                            ins = e.matmul(pst[:], lhsT=wv[:, j, :], rhs=act[:, j, hs], start=(j == 0), stop=(j == 21))
                        return ins
                    rop('pe', mm, reads=[wk] + ['act%d_%d' % (j, half) for j in range(22)], writes=[pk])
                    rop('dve', lambda e, pst=pst, ct=ct, hs=hs: e.scalar_tensor_tensor(
                        out=xT[:, ct, hs], in0=pst[:], scalar=mod_l[l][:, 40 + ct:41 + ct], in1=xT[:, ct, hs], op0=ALU.mult, op1=ALU.add),
                        reads=[pk, 'modp%d_5' % l, 'xh%d_%d' % (ct, half)], writes=['xh%d_%d' % (ct, half)])
                    w_done()

        stage = {'n': 0}

        def chk():
            stage['n'] += 1
            if stop_after is not None and stage['n'] >= stop_after:
                raise _Stop()
        try:
            chk()
            decay_tables(0)
            rmsnorm_stats(0)
            rmsnorm_stats(1)
            mod_group(0, range(4))
            chk()
            for l in range(DEPTH):
                norm_mod(l, 0, skip_stats=(l == 0))
                fine['n'] = 2
                chk()
                release()
                mixer_retention(l, 0, filler=list(range(4, 8)) if l == 0 else None)
                if l == 0:
                    mod_group(1, range(0, 3))
                chk()
                release()
                mixer_retention(l, 1, filler=list(range(8, 12)) if l == 0 else None)
                if l == 0:
                    mod_group(1, range(3, 6))
                chk()
                release()
                mixer_conv_gmlp_pool(l)
                if l == 0:
                    mod_group(1, range(6, 9))
                chk()
                out_proj(l)
                if l == 0:
                    mod_group(1, range(9, 12))
                chk()
                norm_mod(l, 1)
                fine['n'] = 4
                release()
                ffn(l)
                chk()
            gfo = SP32_OFF['gF'][0]
            yTd = d_y.rearrange("(kt p) t -> p kt t", p=128)
            rmsnorm_stats(0)
            rmsnorm_stats(1)
            for half in range(2):
                hs = slice(half * 512, (half + 1) * 512)
                for kt in range(8):
                    r = kt % 2
                    P.op('dve', lambda e, kt=kt, r=r, hs=hs, half=half: e.scalar_tensor_tensor(
                        out=xT[:, kt, hs], in0=xT[:, kt, hs], scalar=sp[:, gfo + kt:gfo + kt + 1], in1=rs_n[:, half, :],
                        op0=ALU.mult, op1=ALU.mult), reads=['xh%d_%d' % (kt, half), 'rs_n%d' % half, 'sp'], writes=['xo%d_%d' % (kt, half)])
                    P.dma('sp', yTd[:, kt, hs], xT[:, kt, hs], reads=['xo%d_%d' % (kt, half)])
        except _Stop:
            for e_ in ('pe', 'act', 'dve', 'pool'):
                if P.ecount[e_]:
                    P.wait_tok('sp', (P.eidx[e_], P.ecount[e_]))
        for k in range(P.n_dma):
            if P.dma_count[k]:
                P.wait_tok('sp', (P.dma_base + k, P.dma_count[k]))
        P.run(blk)
    return nc


def _host_prepare(inp):
    f32 = np.float32
    x_prompt = np.asarray(inp['x_prompt'], f32)
    x_sample = np.asarray(inp['x_sample'], f32)
    state_ret = np.asarray(inp['state_ret'], f32)
    c = np.asarray(inp['c'], f32)
    c_ctx = np.asarray(inp['c_ctx'], f32)
    p = np.arange(128)

    def colmaj(v, nt):
        return np.ascontiguousarray(v.reshape(nt, 128).T)

    sp_common = np.zeros((128, NS), f32)

    def put(name, arr):
        o, n = SP32_OFF[name]
        sp_common[:, o:o + n] = arr.reshape(128, n)
    put('g1', np.stack([colmaj(inp['g_norm1'][l], 8) for l in range(2)], 1))
    put('g2', np.stack([colmaj(inp['g_norm2'][l], 8) for l in range(2)], 1))
    put('gF', colmaj(np.asarray(inp['g_final'], f32), 8))
    put('b_fm', np.stack([colmaj(np.asarray(inp['b_ada'], f32)[l], 48) for l in range(2)], 1))
    dw = np.asarray(inp['conv_dw'], f32)
    put('conv_dw', np.ascontiguousarray(dw.reshape(2, 31, 2, 128).transpose(3, 0, 2, 1)))
    for nm, key in (('conv_b', 'conv_b'), ('ln_g', 'conv_ln_g'), ('ln_b', 'conv_ln_b'), ('pool_scale', 'pool_scale')):
        a = np.asarray(inp[key], f32).reshape(2, 2, 128).transpose(2, 0, 1)
        put(nm, np.ascontiguousarray(a))
    gb = np.asarray(inp['gmlp_b'], f32)
    gbt = np.zeros((128, 2, 2, 128), f32)
    for l in range(2):
        for ti in range(2):
            for hl in range(2):
                gbt[hl * 64:(hl + 1) * 64, l, ti, :] = gb[l, 2 * ti + hl][None, :]
    put('gb_tab', gbt)
    rd = np.asarray(inp['ret_decay'], f32)
    rdcol = np.zeros((128, 2, 2, 2), f32)
    for ti in range(2):
        for hl in range(2):
            rdcol[hl * 64:(hl + 1) * 64, :, :, ti] = rd[:, :, 2 * ti + hl][None]
    put('rdcol', rdcol)
    put('rdb', np.broadcast_to(rd.reshape(1, 16), (128, 16)).copy())
    ii = np.arange(128)
    diff_f = np.maximum(ii[None, :] - ii[:, None], 0).astype(f32)
    diff_b = np.maximum(ii[:, None] - ii[None, :], 0).astype(f32)
    put('diffT', np.stack([diff_f, diff_b], 1))
    tri_f = (ii[None, :] >= ii[:, None]).astype(f32)
    tri_b = (ii[:, None] >= ii[None, :]).astype(f32)
    put('triT', np.stack([tri_f, tri_b], 1))
    ramp = np.stack([np.broadcast_to((ii + 1).astype(f32), (128, 128)), np.broadcast_to((128 - ii).astype(f32), (128, 128))], 1)
    put('ramp_q', ramp)
    put('rampcol_k', np.stack([(127 - ii).astype(f32), ii.astype(f32)], 1))
    bd = np.zeros((128, 128), f32)
    bd[0:64, 0:64] = 1
    bd[64:, 64:] = 1
    put('bdmask', bd)

    bfp = np.zeros((128, NB), f32)

    def putb(name, arr):
        o, n = SPBF_OFF[name]
        bfp[:, o:o + n] = arr.reshape(128, n)
    pw = np.asarray(inp['conv_pw'], f32)
    putb('conv_pw', np.ascontiguousarray(pw.reshape(2, 2, 128, 256).transpose(2, 0, 1, 3)))
    ws = np.asarray(inp['gmlp_ws'], f32)
    putb('wsT', np.ascontiguousarray(ws.transpose(3, 0, 1, 2)))
    pwl = np.asarray(inp['pool_w'], f32)
    pbd = np.zeros((128, 2, 2, 128), f32)
    for l in range(2):
        for ti in range(2):
            for gl in range(2):
                pbd[gl * 64:(gl + 1) * 64, l, ti, gl * 64:(gl + 1) * 64] = pwl[l, 2 * ti + gl]
    putb('pool_bd', pbd)
    putb('ident', np.eye(128, dtype=f32))
    perm = np.arange(128)
    dd = perm % 32
    perm = np.where(dd < 16, perm + 16, perm - 16)
    Pm = np.zeros((128, 128), f32)
    Pm[perm, np.arange(128)] = 1.0
    putb('Pm', Pm)

    t = np.arange(1024)
    nf = 16
    inv = (f32(10000.0) ** (-np.arange(nf, dtype=f32) / f32(nf))).astype(f32)
    rows = (t // 64).astype(f32)
    cols = (t % 64).astype(f32)
    dd64 = np.arange(128) % 64
    fidx = dd64 % 16
    pos = np.where((dd64 < 32)[:, None], rows[None, :], cols[None, :]).astype(f32)
    ang = (pos * inv[fidx][:, None]).astype(f32)
    cos_s = np.cos(ang).astype(f32)
    sin_s = np.sin(ang).astype(f32)
    is_x1 = ((dd64 % 32) < 16)
    sinS = np.where(is_x1[:, None], -sin_s, sin_s).astype(f32)
    sinP_s = sinS[perm]
    def invcount(L):
        tt = np.arange(1024) % L
        out = np.zeros((128, 2, 1024), f32)
        for ti in range(2):
            for gl in range(2):
                w = (2, 4, 8, 16)[2 * ti + gl]
                lo = np.clip(tt - w // 2, 0, L)
                hi = np.clip(tt + w // 2, 0, L)
                out[gl * 64:(gl + 1) * 64, ti, :] = (1.0 / (hi - lo).astype(f32))[None, :]
        return out
    tabs_sample = np.concatenate([cos_s, sinP_s, invcount(1024).reshape(128, 2048)], 1).astype(f32)
    tabs_prompt = np.concatenate([np.ones((128, 1024), f32), np.zeros((128, 1024), f32), invcount(256).reshape(128, 2048)], 1).astype(f32)

    in_maps = []
    wts = dict(w_ada=np.ascontiguousarray(inp['w_ada'], f32), w_in=np.ascontiguousarray(inp['w_in'], f32),
               w_out=np.ascontiguousarray(inp['w_out'], f32), w_ffn_in=np.ascontiguousarray(inp['w_ffn_in'], f32),
               w_ffn_out=np.ascontiguousarray(inp['w_ffn_out'], f32))
    for core in range(8):
        spc = sp_common.copy()
        if core < 4:
            xs = x_prompt[4 * core:4 * core + 4].reshape(1024, 1024)
            cv = c_ctx
            flagv = 0.0
            tabs = tabs_prompt
            s0 = np.zeros((128, 8, 128), f32)
        else:
            b = core - 4
            xs = x_sample[b]
            cv = c[b]
            flagv = 1.0
            tabs = tabs_sample
            s0 = np.zeros((128, 2, 2, 2, 128), f32)
            for ti in range(2):
                for hl in range(2):
                    s0[hl * 64:(hl + 1) * 64, :, :, ti, hl * 64:(hl + 1) * 64] = state_ret[b, :, :, 2 * ti + hl].transpose(2, 0, 1, 3)
            s0 = s0.reshape(128, 8, 128)
        o, n = SP32_OFF['cvec']
        spc[:, o:o + n] = colmaj(cv, 8)
        o, n = SP32_OFF['flag']
        spc[:, o] = flagv
        m = dict(xT=np.ascontiguousarray(xs.T), sp32=spc, spbf=bfp, tabs=tabs, s0=np.ascontiguousarray(s0))
        m.update(wts)
        in_maps.append(m)
    return in_maps


_NC_CACHE = {}


def kernel(**inputs):
    in_maps = _host_prepare(inputs)
    if 'nc' not in _NC_CACHE:
        _NC_CACHE['nc'] = build_program()
    nc = _NC_CACHE['nc']
    res = run_bass_kernel_spmd(nc, in_maps, core_ids=list(range(8)))
    r = res.results
    y_prompt = np.zeros((16, 256, 1024), np.float32)
    y_sample = np.zeros((4, 1024, 1024), np.float32)
    new_state = np.zeros((16, 2, 2, 4, 64, 64), np.float32)
    for core in range(4):
        y_prompt[4 * core:4 * core + 4] = np.asarray(r[core]['yT']).T.reshape(4, 256, 1024)
        stc = np.asarray(r[core]['st'])
        new_state[4 * core:4 * core + 4] = stc.transpose(2, 0, 1, 3, 4, 5)
    for core in range(4, 8):
        y_sample[core - 4] = np.asarray(r[core]['yT']).T
    return (y_prompt, y_sample, new_state)
```

```python
import contextlib
import math
import numpy as np
import concourse.bass as bass
import concourse.mybir as mybir
from concourse.bass_utils import run_bass_kernel_spmd

F32 = mybir.dt.float32
BF16 = mybir.dt.bfloat16
ALU = mybir.AluOpType
AF = mybir.ActivationFunctionType

DEPTH = 2
DFF = 2816
EPS = 1e-6
NSLOT = 4
SLOTW = 4096
ARENA_WORDS = 15872


class Prog:
    ENG = ['pe', 'act', 'dve', 'pool', 'sp']

    def __init__(self, nc, stack, n_dma_sems=32):
        self.nc = nc
        self.streams = {e: [] for e in self.ENG}
        self.sems = []
        self.eidx = {}
        for e in self.ENG:
            self.eidx[e] = len(self.sems)
            self.sems.append(stack.enter_context(nc.semaphore("es_" + e)))
        self.ecount = {e: 0 for e in self.ENG}
        self.dma_base = len(self.sems)
        self.n_dma = n_dma_sems
        for i in range(n_dma_sems):
            self.sems.append(stack.enter_context(nc.semaphore("ds_%d" % i)))
        self.dma_count = [0] * n_dma_sems
        self.dma_rr = 0
        self.dma_rr_pool = 0
        self.waited = {e: {} for e in self.ENG}
        self.res_w = {}
        self.res_r = {}
        self.nwaits = 0
        self.nops = {e: 0 for e in self.ENG}

    def _deps(self, reads, writes):
        deps = {}

        def add(tok):
            if tok is None:
                return
            s, v = tok
            if deps.get(s, 0) < v:
                deps[s] = v
        for r in reads:
            add(self.res_w.get(r))
        for w in writes:
            add(self.res_w.get(w))
            for s, v in self.res_r.get(w, {}).items():
                add((s, v))
        return deps

    def _emit_waits(self, eng, deps):
        for s, v in sorted(deps.items()):
            if eng == 'pe' and s == self.eidx['pe']:
                continue
            if self.waited[eng].get(s, 0) >= v:
                continue
            self.waited[eng][s] = v
            self.streams[eng].append(('wait', s, v))
            self.nwaits += 1

    def _record(self, tok, reads, writes):
        s, v = tok
        for r in reads:
            d = self.res_r.setdefault(r, {})
            if d.get(s, 0) < v:
                d[s] = v
        for w in writes:
            self.res_w[w] = tok
            self.res_r[w] = {}

    def op(self, eng, fn, reads=(), writes=()):
        deps = self._deps(reads, writes)
        self._emit_waits(eng, deps)
        self.ecount[eng] += 1
        tok = (self.eidx[eng], self.ecount[eng])
        self.streams[eng].append(('op', fn, self.eidx[eng]))
        self._record(tok, reads, writes)
        self.nops[eng] += 1
        return tok

    def dma(self, eng, out, in_, reads=(), writes=(), **kw):
        half = self.n_dma // 2
        if eng == 'pool':
            k = self.dma_rr_pool
            self.dma_rr_pool = (k + 1) % half
        else:
            k = half + self.dma_rr
            self.dma_rr = (self.dma_rr + 1) % (self.n_dma - half)
        s = self.dma_base + k
        deps = self._deps(reads, writes)
        if self.dma_count[k] > 0 and deps.get(s, 0) < self.dma_count[k]:
            deps[s] = self.dma_count[k]
        self._emit_waits(eng, deps)
        self.dma_count[k] += 16
        tok = (s, self.dma_count[k])
        self.streams[eng].append(('dma', out, in_, s, kw))
        self._record(tok, reads, writes)
        return tok

    def wait_tok(self, eng, tok):
        self._emit_waits(eng, {tok[0]: tok[1]})

    def run(self, block):
        sems = self.sems

        def runner(name):
            def f(e):
                for item in self.streams[name]:
                    if item[0] == 'wait':
                        e.wait_ge(sems[item[1]], item[2])
                    elif item[0] == 'op':
                        ins = item[1](e)
                        ins.then_inc(sems[item[2]], 1)
                    else:
                        _, out, in_, s, kw = item
                        e.dma_start(out=out, in_=in_, **kw).then_inc(sems[s], 16)
            return f
        block.tensor(runner('pe'))
        block.scalar(runner('act'))
        block.vector(runner('dve'))
        block.gpsimd(runner('pool'))
        block.sync(runner('sp'))


class Arena:
    def __init__(self, t, words):
        self.t = t
        self.words = words
        self.off = 0
        self.epoch = 0

    def reset(self):
        self.off = 0
        self.epoch += 1

    def alloc(self, dtype, shape):
        n = int(np.prod(shape[1:]))
        sz = 4 if dtype == F32 else 2
        words = (n * sz + 3) // 4
        words = (words + 7) // 8 * 8
        assert self.off + words <= self.words, ("arena overflow", self.off, words)
        ap = self.t[:, self.off:self.off + words]
        self.off += words
        if dtype != F32:
            ap = ap.bitcast(dtype)
        ap = ap[:, 0:n]
        if len(shape) == 3:
            ap = ap.rearrange("p (a b) -> p a b", a=shape[1])
        elif len(shape) == 4:
            ap = ap.rearrange("p (a b c) -> p a b c", a=shape[1], b=shape[2])
        elif len(shape) == 5:
            ap = ap.rearrange("p (a b c d) -> p a b c d", a=shape[1], b=shape[2], c=shape[3])
        return ap


def _sp32_layout():
    off = {}
    cur = 0

    def add(name, n):
        nonlocal cur
        off[name] = (cur, n)
        cur += n
    add('cvec', 8)
    add('flag', 1)
    add('g1', 16)
    add('g2', 16)
    add('gF', 8)
    add('b_fm', 96)
    add('conv_dw', 124)
    add('conv_b', 4)
    add('ln_g', 4)
    add('ln_b', 4)
    add('pool_scale', 4)
    add('gb_tab', 512)
    add('rdcol', 8)
    add('rdb', 16)
    add('diffT', 256)
    add('triT', 256)
    add('ramp_q', 256)
    add('rampcol_k', 2)
    add('bdmask', 128)
    return off, cur


SP32_OFF, NS = _sp32_layout()


def _spbf_layout():
    off = {}
    cur = 0

    def add(name, n):
        nonlocal cur
        off[name] = (cur, n)
        cur += n
    add('conv_pw', 1024)
    add('wsT', 1024)
    add('pool_bd', 512)
    add('ident', 128)
    add('Pm', 128)
    return off, cur


SPBF_OFF, NB = _spbf_layout()


class _Stop(Exception):
    pass


def build_program(debug=None, stop_after=None, stop_sub=None):
    nc = bass.Bass("TRN2", target_bir_lowering=False)
    dram_in = lambda n, s: nc.dram_tensor(n, s, F32, kind="ExternalInput").ap()
    d_x = dram_in("xT", [1024, 1024])
    d_sp = dram_in("sp32", [128, NS])
    d_bf = dram_in("spbf", [128, NB])
    d_tabs = dram_in("tabs", [128, 4096])
    d_s0 = dram_in("s0", [128, 8, 128])
    d_wada = dram_in("w_ada", [2, 1024, 6144])
    d_win = dram_in("w_in", [2, 1024, 2816])
    d_wout = dram_in("w_out", [2, 1024, 1024])
    d_wfi = dram_in("w_ffn_in", [2, 1024, 5632])
    d_wfo = dram_in("w_ffn_out", [2, 2816, 1024])
    d_y = nc.dram_tensor("yT", [1024, 1024], F32, kind="ExternalOutput").ap()
    d_st = nc.dram_tensor("st", [2, 2, 4, 4, 64, 64], F32, kind="ExternalOutput").ap()
    dbg_out = {}
    if debug:
        for name, shape in debug.items():
            dbg_out[name] = nc.dram_tensor("dbg_" + name, list(shape), F32, kind="ExternalOutput").ap()

    with contextlib.ExitStack() as st:
        P = Prog(nc, st)
        T = lambda name, shape, dt: st.enter_context(nc.sbuf_tensor(name, shape, dt))
        xT = T("xT_sb", [128, 8, 1024], F32)
        hT = T("hT", [128, 8, 1024], BF16)
        yT = T("yTm", [128, 8, 1024], BF16)
        wring = T("wring", [128, NSLOT, SLOTW], BF16)
        tabs = T("tabs_sb", [128, 2048], F32)
        arena_t = T("arena", [128, ARENA_WORDS], F32)
        sp = T("sp32_sb", [128, NS], F32)
        sb = T("spbf_sb", [128, NB], BF16)
        dmaskT = T("dmaskT", [128, 2, 4, 128], F32)
        qdec = T("qdec", [128, 4, 128], F32)
        kdec = T("kdec", [128, 8], F32)
        sdec = T("sdec", [128, 4], F32)
        lgcol = T("lgcol", [128, 8], F32)
        lgb = T("lgb", [128, 16], F32)
        lg128 = T("lg128", [128, 8], F32)
        mod_l = [T("mod%d" % i, [128, 48], F32) for i in range(2)]
        modA_l = [T("modA%d" % i, [128, 16], F32) for i in range(2)]
        silu_c = T("silu_c", [128, 8], BF16)
        ones_bf = T("ones_bf", [128, 128], BF16)
        ones256 = T("ones256", [128, 128], BF16)
        hbd = T("hbd", [128, 128], BF16)
        onef = T("onef", [1, 8], F32)
        rowtmp = T("rowtmp", [1, 2, 512], F32)
        rs_n = T("rs_n", [128, 2, 512], F32)
        sqb = T("sqb", [128, 4, 512], BF16)
        ntmp = T("ntmp", [128, 4, 512], F32)
        dummy = T("dummy_t", [128, 8], F32)
        ps = [st.enter_context(nc.psum_tensor("ps%d" % i, [128, 512], F32)) for i in range(8)]
        blk = st.enter_context(nc.Block())

        arena = Arena(arena_t, ARENA_WORDS)
        subc = {'n': 0}

        def sub():
            subc['n'] += 1
            if stop_sub is not None and subc['n'] >= stop_sub:
                raise _Stop()
        invc = tabs[:, 0:2048].rearrange("p (a b) -> p a b", a=2)

        def SP(name, *idx):
            o, n = SP32_OFF[name]
            return sp[:, o:o + n]

        def SB(name):
            o, n = SPBF_OFF[name]
            return sb[:, o:o + n]

        rot = {'big': 0, 'small': 0}

        def ps_big():
            i = rot['big']
            rot['big'] = (i + 1) % 5
            return ps[i], 'ps%d' % i

        def ps_small():
            i = 5 + rot['small']
            rot['small'] = (rot['small'] + 1) % 2
            return ps[i], 'ps%d' % i

        def EK():
            return 'EPOCH'

        def rop(eng, fn, reads=(), writes=()):
            return P.op(eng, fn, reads=list(reads) + [EK()], writes=writes)

        def release():
            P.op('dve', lambda e: e.memset(dummy[0:1, 0:8], 0.0), reads=[], writes=[EK(), 'dummy'])
            arena.reset()

        def dump(name, ap, reads):
            if debug and name in dbg_out:
                P.dma('sp', dbg_out[name], ap, reads=list(reads) + [EK()])

        wchunks = []
        wstate = {'issued': 0, 'slot': 0}
        wkeys = {}

        def wq_add(cid, dst_fn, parts):
            wchunks.append((cid, dst_fn, parts))

        def build_weight_queue():
            defs = {}
            for l in range(DEPTH):
                wa = d_wada[l].rearrange("(kt p) c -> p kt c", p=128)
                for cc in range(12):
                    defs[('ada', l, cc)] = (lambda s: s.rearrange("p (kt c) -> p kt c", kt=8),
                                            [(lambda v: v, wa[:, :, cc * 512:(cc + 1) * 512])])
                wi = d_win[l]
                for ti in range(2):
                    parts = []
                    for s4 in range(4):
                        c0 = 1280 + s4 * 256 + ti * 128
                        parts.append((lambda v, s4=s4: v[:, :, s4, :], wi[:, c0:c0 + 128].rearrange("(kt p) c -> p kt c", p=128)))
                    defs[('rqk', l, ti)] = (lambda s: s.rearrange("p (kt s c) -> p kt s c", kt=8, s=4), parts)
                    parts = []
                    for s2 in range(2):
                        c0 = 2304 + s2 * 256 + ti * 128
                        parts.append((lambda v, s2=s2: v[:, :, s2, :], wi[:, c0:c0 + 128].rearrange("(kt p) c -> p kt c", p=128)))
                    defs[('rvg', l, ti)] = (lambda s: s[:, 0:2048].rearrange("p (kt s c) -> p kt s c", kt=8, s=2), parts)
                defs[('conv', l)] = (lambda s: s.rearrange("p (kt c) -> p kt c", kt=8),
                                     [(lambda v: v, wi[:, 0:512].rearrange("(kt p) c -> p kt c", p=128))])
                defs[('gmlp', l)] = (lambda s: s.rearrange("p (kt c) -> p kt c", kt=8),
                                     [(lambda v: v, wi[:, 512:1024].rearrange("(kt p) c -> p kt c", p=128))])
                defs[('pool', l)] = (lambda s: s[:, 0:2048].rearrange("p (kt c) -> p kt c", kt=8),
                                     [(lambda v: v, wi[:, 1024:1280].rearrange("(kt p) c -> p kt c", p=128))])
                wo = d_wout[l].rearrange("(kt p) c -> p kt c", p=128)
                for hh in range(2):
                    defs[('wo', l, hh)] = (lambda s: s.rearrange("p (kt c) -> p kt c", kt=8),
                                           [(lambda v: v, wo[:, :, hh * 512:(hh + 1) * 512])])
                wf = d_wfi[l].rearrange("(kt p) f -> p kt f", p=128)
                for jj in range(11):
                    parts = []
                    for two in range(2):
                        c0 = two * 2816 + jj * 256
                        parts.append((lambda v, two=two: v[:, :, two, :], wf[:, :, c0:c0 + 256]))
                    defs[('ffi', l, jj)] = (lambda s: s.rearrange("p (kt two c) -> p kt two c", kt=8, two=2), parts)
                wfo = d_wfo[l].rearrange("(j p) c -> p j c", p=128)
                for ct in range(8):
                    for hf in range(2):
                        defs[('ffo', l, ct, hf)] = (lambda s: s[:, 0:2816].rearrange("p (j c) -> p j c", j=22),
                                                    [(lambda v: v, wfo[:, :, ct * 128:(ct + 1) * 128])])
            order = []
            order += [('ada', 0, cc) for cc in range(4)]
            order += [('rqk', 0, 0), ('rvg', 0, 0)] + [('ada', 0, cc) for cc in range(4, 8)] + [('ada', 1, cc) for cc in range(0, 3)]
            order += [('rqk', 0, 1), ('rvg', 0, 1)] + [('ada', 0, cc) for cc in range(8, 12)] + [('ada', 1, cc) for cc in range(3, 6)]
            order += [('pool', 0), ('conv', 0), ('gmlp', 0)] + [('ada', 1, cc) for cc in range(6, 9)]
            order += [('wo', 0, 0), ('wo', 0, 1)] + [('ada', 1, cc) for cc in range(9, 12)]
            order += [('ffi', 0, jj) for jj in range(11)]
            order += [('ffo', 0, ct, hf) for hf in range(2) for ct in range(8)]
            order += [('rqk', 1, 0), ('rvg', 1, 0), ('rqk', 1, 1), ('rvg', 1, 1), ('pool', 1), ('conv', 1), ('gmlp', 1),
                      ('wo', 1, 0), ('wo', 1, 1)]
            order += [('ffi', 1, jj) for jj in range(11)] + [('ffo', 1, ct, hf) for hf in range(2) for ct in range(8)]
            assert len(order) == len(defs)
            for cid in order:
                wq_add(cid, defs[cid][0], defs[cid][1])

        def w_issue():
            i = wstate['issued']
            if i >= len(wchunks):
                return
            cid, dst_fn, parts = wchunks[i]
            slot = i % NSLOT
            dst = dst_fn(wring[:, slot, :])
            for (sel, src) in parts:
                P.dma('pool', sel(dst), src, writes=['wslot%d' % slot])
            wkeys[cid] = (slot, dst)
            wstate['issued'] = i + 1

        wnext = {'i': 0}

        def w_get(cid, ahead=0):
            i = wnext['i'] + ahead
            assert wchunks[i][0] == cid, (wchunks[i][0], cid)
            while wstate['issued'] <= i:
                w_issue()
            slot, dst = wkeys[cid]
            return dst, 'wslot%d' % slot

        def w_done():
            wnext['i'] += 1
            w_issue()
            while wstate['issued'] < min(len(wchunks), wnext['i'] + NSLOT):
                w_issue()

        build_weight_queue()

        xTd = d_x.rearrange("(kt p) t -> p kt t", p=128)
        P.dma('sp', sp[:], d_sp, writes=['sp'])
        for kt in range(8):
            P.dma('sp', xT[:, kt, :], xTd[:, kt, :], writes=['xh%d_0' % kt, 'xh%d_1' % kt])
        for _ in range(NSLOT):
            w_issue()
        P.dma('pool', sb[:], d_bf, writes=['sb'])
        P.op('dve', lambda e: e.memset(ones_bf[:], 1.0 / 1024.0), writes=['ones_bf'])
        P.op('dve', lambda e: e.memset(ones256[:], 1.0 / 256.0), writes=['ones256'])
        P.op('dve', lambda e: e.memset(hbd[:], 0.0), writes=['hbd'])
        P.op('dve', lambda e: e.memset(hbd[0:64, 0:64], 1.0 / 64.0), reads=['hbd'], writes=['hbd'])
        P.op('dve', lambda e: e.memset(hbd[64:128, 64:128], 1.0 / 64.0), reads=['hbd'], writes=['hbd'])
        P.op('dve', lambda e: e.memset(onef[:], 1.0), writes=['onef'])
        P.op('act', lambda e: e.activation(out=silu_c[:], in_=SP('cvec'), func=AF.Silu), reads=['sp'], writes=['silu_c'])
        P.op('act', lambda e: e.activation(out=lgcol[:], in_=SP('rdcol'), func=AF.Exp, scale=-1.0), reads=['sp'], writes=['lgcol'])
        P.op('act', lambda e: e.activation(out=lgb[:], in_=SP('rdb'), func=AF.Exp, scale=-1.0), reads=['sp'], writes=['lgb'])
        P.op('act', lambda e: e.activation(out=lgcol[:], in_=lgcol[:], func=AF.Ln, bias=1.0), reads=['lgcol'], writes=['lgcol'])
        P.op('act', lambda e: e.activation(out=lgb[:], in_=lgb[:], func=AF.Ln, bias=1.0), reads=['lgb'], writes=['lgb'])
        P.op('dve', lambda e: e.tensor_scalar(out=lgcol[:], in0=lgcol[:], scalar1=-1.0, scalar2=None, op0=ALU.mult),
             reads=['lgcol'], writes=['lgcol'])
        P.op('dve', lambda e: e.tensor_scalar(out=lgb[:], in0=lgb[:], scalar1=-1.0, scalar2=None, op0=ALU.mult),
             reads=['lgb'], writes=['lgb'])
        P.op('dve', lambda e: e.tensor_scalar(out=lg128[:], in0=lgcol[:], scalar1=128.0, scalar2=None, op0=ALU.mult),
             reads=['lgcol'], writes=['lg128'])

        LN_KS = math.log(0.125)
        flag = SP('flag')

        def rmsnorm_stats(half):
            hs = slice(half * 512, (half + 1) * 512)
            pst, pk = ps_small()
            for kt in range(8):
                r = kt % 4
                P.op('act', lambda e, kt=kt, r=r: e.activation(out=sqb[:, r, :], in_=xT[:, kt, hs], func=AF.Square),
                     reads=['xh%d_%d' % (kt, half)], writes=['sqb%d' % r])
                P.op('pe', lambda e, kt=kt, r=r: e.matmul(pst[:], lhsT=ones_bf[:], rhs=sqb[:, r, :], start=(kt == 0), stop=(kt == 7)),
                     reads=['sqb%d' % r, 'ones_bf'], writes=[pk])
            P.op('act', lambda e: e.activation(out=rs_n[:, half, :], in_=pst[:], func=AF.Ln, bias=EPS, scale=1.0),
                 reads=[pk], writes=['rs_n%d' % half])
            P.op('act', lambda e: e.activation(out=rs_n[:, half, :], in_=rs_n[:, half, :], func=AF.Exp, scale=-0.5),
                 reads=['rs_n%d' % half], writes=['rs_n%d' % half])

        def norm_mod(l, which, skip_stats=False):
            Acol = modA_l[l][:, 8 * which:8 * which + 8]
            Bcol = mod_l[l][:, 24 * which:24 * which + 8]
            akey = 'modA%d_%d' % (l, which)
            bkey = 'modp%d_%d' % (l, 3 * which)
            if not skip_stats:
                rmsnorm_stats(0)
                rmsnorm_stats(1)
            cnt = 0
            for half in range(2):
                hs = slice(half * 512, (half + 1) * 512)
                for kt in range(8):
                    r = cnt % 4
                    cnt += 1
                    P.op('dve', lambda e, kt=kt, r=r, hs=hs, half=half: e.scalar_tensor_tensor(
                        out=ntmp[:, r, :], in0=xT[:, kt, hs], scalar=Acol[:, kt:kt + 1], in1=rs_n[:, half, :],
                        op0=ALU.mult, op1=ALU.mult), reads=['xh%d_%d' % (kt, half), 'rs_n%d' % half, akey], writes=['ntmp%d' % r])
                    P.op('act', lambda e, kt=kt, r=r, hs=hs: e.activation(
                        out=hT[:, kt, hs], in_=ntmp[:, r, :], func=AF.Identity, bias=Bcol[:, kt:kt + 1], scale=1.0),
                        reads=['ntmp%d' % r, bkey], writes=['hT%d_%d' % (kt, half)])

        hT_keys = lambda half: ['hT%d_%d' % (kt, half) for kt in range(8)]
        hT_all = hT_keys(0) + hT_keys(1)

        def mod_chunk(l, cc):
            mod_a(l, cc)
            mod_b(l, cc)

        def mod_group(l, ccs):
            prev = None
            for cc in ccs:
                mod_a(l, cc)
                if prev is not None:
                    mod_b(l, prev)
                prev = cc
            mod_b(l, prev)

        def mod_a(l, cc):
            wv, wk = w_get(('ada', l, cc))
            prow, prk = ps_small()

            def mm(e, wv=wv, prow=prow):
                for kt in range(8):
                    ins = e.matmul(prow[0:1, :], lhsT=silu_c[:, kt:kt + 1], rhs=wv[:, kt, :],
                                   start=(kt == 0), stop=(kt == 7))
                return ins
            P.op('pe', mm, reads=[wk, 'silu_c'], writes=[prk])
            w_done()
            r = cc % 2
            P.op('act', lambda e, prow=prow, r=r: e.activation(out=rowtmp[0:1, r, :], in_=prow[0:1, :], func=AF.Copy),
                 reads=[prk], writes=['rowtmp%d' % r])

        def mod_b(l, cc):
            r = cc % 2
            pc, pck = ps_small()

            def mmT(e, pc=pc, r=r):
                for j in range(4):
                    ins = e.matmul(pc[:, j:j + 1], lhsT=rowtmp[0:1, r, j * 128:(j + 1) * 128],
                                   rhs=onef[0:1, 0:1], start=True, stop=True)
                return ins
            P.op('pe', mmT, reads=['rowtmp%d' % r, 'onef'], writes=[pck])
            bo = SP32_OFF['b_fm'][0] + 48 * l + cc * 4
            mk = 'modp%d_%d' % (l, cc // 2)
            P.op('dve', lambda e, pc=pc, bo=bo: e.tensor_tensor(out=mod_l[l][:, cc * 4:cc * 4 + 4], in0=pc[:, 0:4], in1=sp[:, bo:bo + 4],
                                                                op=ALU.add), reads=[pck, 'sp', mk], writes=[mk])
            if cc == 3:
                g1o = SP32_OFF['g1'][0] + 8 * l
                P.op('dve', lambda e: e.scalar_tensor_tensor(out=modA_l[l][:, 0:8], in0=mod_l[l][:, 8:16], scalar=1.0,
                                                             in1=sp[:, g1o:g1o + 8], op0=ALU.add, op1=ALU.mult),
                     reads=['modp%d_1' % l, 'sp'], writes=['modA%d_0' % l])
            if cc == 9:
                g2o = SP32_OFF['g2'][0] + 8 * l
                P.op('dve', lambda e: e.scalar_tensor_tensor(out=modA_l[l][:, 8:16], in0=mod_l[l][:, 32:40], scalar=1.0,
                                                             in1=sp[:, g2o:g2o + 8], op0=ALU.add, op1=ALU.mult),
                     reads=['modp%d_4' % l, 'sp'], writes=['modA%d_1' % l])

        def decay_tables(l):
            do = SP32_OFF['diffT'][0]
            to = SP32_OFF['triT'][0]
            rq = SP32_OFF['ramp_q'][0]
            rk = SP32_OFF['rampcol_k'][0]
            for ti in range(2):
                for d in range(2):
                    for hl in range(2):
                        h = 2 * ti + hl
                        col = l * 8 + d * 4 + h
                        P.op('act', lambda e, ti=ti, d=d, hl=hl, col=col: e.activation(
                            out=dmaskT[:, ti, d * 2 + hl, :], in_=sp[:, do + d * 128:do + (d + 1) * 128], func=AF.Exp,
                            bias=LN_KS, scale=lgb[:, col:col + 1]), reads=['sp', 'lgb'], writes=['dmaskT'])
                        P.op('dve', lambda e, ti=ti, d=d, hl=hl: e.tensor_tensor(
                            out=dmaskT[:, ti, d * 2 + hl, :], in0=dmaskT[:, ti, d * 2 + hl, :],
                            in1=sp[:, to + d * 128:to + (d + 1) * 128], op=ALU.mult), reads=['dmaskT', 'sp'], writes=['dmaskT'])
            for d in range(2):
                for ti in range(2):
                    col = l * 4 + d * 2 + ti
                    P.op('act', lambda e, d=d, ti=ti, col=col: e.activation(
                        out=qdec[:, d * 2 + ti, :], in_=sp[:, rq + d * 128:rq + (d + 1) * 128], func=AF.Exp,
                        scale=lgcol[:, col:col + 1]), reads=['sp', 'lgcol'], writes=['qdec'])
                P.op('act', lambda e, d=d: e.activation(
                    out=kdec[:, d * 4:(d + 1) * 4], in_=lgb[:, l * 8 + d * 4:l * 8 + d * 4 + 4], func=AF.Exp,
                    bias=LN_KS, scale=sp[:, rk + d:rk + d + 1]), reads=['sp', 'lgb'], writes=['kdec'])
            P.op('act', lambda e: e.activation(out=sdec[:], in_=lg128[:, l * 4:(l + 1) * 4], func=AF.Exp),
                 reads=['lg128'], writes=['sdec'])

        fine = {'n': 0}

        def fm_group(wv_kt_fn, half, pst, pk, wk):
            hs = slice(half * 512, (half + 1) * 512)
            if fine['n'] > 0:
                fine['n'] -= 1
                tok = None
                for kt in range(8):
                    tok = P.op('pe', lambda e, kt=kt: e.matmul(pst[:], lhsT=wv_kt_fn(kt), rhs=hT[:, kt, hs], start=(kt == 0), stop=(kt == 7)),
                               reads=[wk, 'hT%d_%d' % (kt, half)], writes=[pk])
                return tok

            def mm(e):
                for kt in range(8):
                    ins = e.matmul(pst[:], lhsT=wv_kt_fn(kt), rhs=hT[:, kt, hs], start=(kt == 0), stop=(kt == 7))
                return ins
            return P.op('pe', mm, reads=[wk] + hT_keys(half), writes=[pk])

        def mixer_retention(l, ti, filler=None):
            A = arena
            qk = A.alloc(BF16, [128, 4, 1024])
            rawc = A.alloc(BF16, [128, 2, 512])
            raws = A.alloc(BF16, [128, 2, 512])
            qd = A.alloc(BF16, [128, 2, 1024])
            vr = A.alloc(BF16, [128, 8, 128])
            kd = A.alloc(BF16, [128, 2, 8, 128])
            sg = A.alloc(BF16, [128, 1024])
            am = A.alloc(BF16, [128, 2, 512])
            Sb = A.alloc(BF16, [128, 2, 8, 128])
            o_sb = A.alloc(F32, [128, 2, 512])
            ob = A.alloc(BF16, [128, 2, 512])
            rso = A.alloc(F32, [128, 2, 512])
            stage = A.alloc(F32, [128, 2, 4, 128])
            Stmp = None
            Scont = A.alloc(F32, [128, 2, 2, 128])
            scur = {0: 0, 1: 0}
            cs_t = A.alloc(F32, [128, 2, 1024])
            kz = A.alloc(BF16, [128, 2, 2, 1024])
            if ti == 0:
                rop('dve', lambda e: e.memset(kz[:], 0.0), writes=['kz'])
                rop('pool', lambda e: e.memset(Sb[:], 0.0), writes=['Sb0', 'Sb1'])
                P.dma('sp', cs_t, d_tabs[:, 0:2048].rearrange("p (a b) -> p a b", a=2), reads=[EK()], writes=['cossin'])
            cosT = cs_t[:, 0, :]
            sinP = cs_t[:, 1, :]
            ident = SB('ident')
            Pm = SB('Pm')

            wv, wk = w_get(('rqk', l, ti))
            wv2, wk2 = w_get(('rvg', l, ti), ahead=1)
            vg_items = []

            def v_item(tp):
                pst, pk = ps_big()

                def mmv(e):
                    for u2 in range(2):
                        tt = tp * 2 + u2
                        for kt in range(8):
                            ins = e.matmul(pst[:, u2 * 128:(u2 + 1) * 128], lhsT=hT[:, kt, tt * 128:(tt + 1) * 128],
                                           rhs=wv2[:, kt, 0, :], start=(kt == 0), stop=(kt == 7))
                    return ins
                P.op('pe', mmv, reads=[wk2] + hT_all, writes=[pk])
                rop('act', lambda e: e.activation(
                    out=vr[:, tp * 2:tp * 2 + 2, :], in_=pst[:, 0:256].rearrange("p (a b) -> p a b", a=2), func=AF.Copy),
                    reads=[pk], writes=['vr'])

            def g_item(half):
                hs = slice(half * 512, (half + 1) * 512)
                pst, pk = ps_big()
                fm_group(lambda kt: wv2[:, kt, 1, :], half, pst, pk, wk2)
                rop('act', lambda e: e.activation(out=sg[:, hs], in_=pst[:], func=AF.Silu),
                    reads=[pk], writes=['sg%d' % half])
            for tp in range(4):
                vg_items.append(lambda tp=tp: v_item(tp))
            for half in range(2):
                vg_items.append(lambda half=half: g_item(half))
            def rope_tail(u, s4, half):
                rb = u % 2
                hs = slice(half * 512, (half + 1) * 512)
                ps2, pk2 = ps_small()

                def mmr(e, ps2=ps2, rb=rb):
                    e.matmul(ps2[:], lhsT=ident, rhs=rawc[:, rb, :], start=True, stop=False)
                    return e.matmul(ps2[:], lhsT=Pm, rhs=raws[:, rb, :], start=False, stop=True)
                rop('pe', mmr, reads=['rawc%d' % rb, 'raws%d' % rb, 'sb'], writes=[pk2])
                rop('act', lambda e, ps2=ps2, s4=s4, hs=hs: e.activation(out=qk[:, s4, hs], in_=ps2[:], func=AF.Copy),
                    reads=[pk2], writes=['qk%d_%d' % (s4, half)])
                if s4 % 2 == 1:
                    dd_ = s4 // 2
                    for hl in range(2):
                        r_ = slice(hl * 64, (hl + 1) * 64)
                        rop('act', lambda e, ps2=ps2, dd_=dd_, hl=hl, r_=r_, hs=hs: e.activation(
                            out=kz[r_, dd_, hl, hs], in_=ps2[r_, :], func=AF.Copy), reads=[pk2, 'kz'], writes=['kz'])

            pend = None
            u = 0
            for s4 in range(4):
                for half in range(2):
                    hs = slice(half * 512, (half + 1) * 512)
                    rb = u % 2
                    pst, pk = ps_big()
                    fm_group(lambda kt, s4=s4, wv=wv: wv[:, kt, s4, :], half, pst, pk, wk)
                    rop('dve', lambda e, pst=pst, hs=hs, rb=rb: e.tensor_tensor(out=rawc[:, rb, :], in0=pst[:], in1=cosT[:, hs], op=ALU.mult),
                        reads=[pk, 'cossin'], writes=['rawc%d' % rb])
                    rop('dve', lambda e, pst=pst, hs=hs, rb=rb: e.tensor_tensor(out=raws[:, rb, :], in0=pst[:], in1=sinP[:, hs], op=ALU.mult),
                        reads=[pk, 'cossin'], writes=['raws%d' % rb])
                    if pend is not None:
                        rope_tail(*pend)
                        if vg_items:
                            vg_items.pop(0)()
                    pend = (u, s4, half)
                    u += 1
            rope_tail(*pend)
            while vg_items:
                vg_items.pop(0)()
            w_done()
            sub()
            w_done()
            sub()
            qkk = lambda s4: ['qk%d_0' % s4, 'qk%d_1' % s4]
            for d in range(2):
                rop('pool', lambda e, d=d: e.tensor_tensor(
                    out=qd[:, d, :].rearrange("p (c i) -> p c i", c=8),
                    in0=qk[:, 2 * d, :].rearrange("p (c i) -> p c i", c=8),
                    in1=qdec[:, d * 2 + ti, :].unsqueeze(1).to_broadcast([128, 8, 128]), op=ALU.mult),
                    reads=qkk(2 * d) + ['qdec'], writes=['qd%d' % d])
            sub()
            pT = ps[7].bitcast(BF16)
            rnd = 0
            for d in range(2):
                for hc in range(2):
                    hb = rnd % 2
                    rnd += 1
                    if hb == 0:
                        pTh = pT[:, 0:512]
                        pkey = 'ps7'
                    else:
                        pss_, pkey = ps_small()
                        pTh = pss_.bitcast(BF16)[:, 0:512]

                    def tr(e, d=d, hc=hc, pTh=pTh):
                        for c4 in range(4):
                            c = hc * 4 + c4
                            ins = e.transpose(out=pTh[:, c4 * 128:(c4 + 1) * 128], in_=qk[:, 2 * d + 1, c * 128:(c + 1) * 128],
                                              identity=ident)
                        return ins
                    rop('pe', tr, reads=qkk(2 * d + 1) + ['sb'], writes=[pkey])
                    if hb == 0:
                        rop('dve', lambda e, d=d, hc=hc, pTh=pTh: e.tensor_tensor(
                            out=kd[:, d, hc * 4:(hc + 1) * 4, :].rearrange("p c (h x) -> p c h x", h=2),
                            in0=pTh.rearrange("p (c h x) -> p c h x", c=4, h=2),
                            in1=kdec[:, d * 4 + 2 * ti:d * 4 + 2 * ti + 2].unsqueeze(1).unsqueeze(3).to_broadcast([128, 4, 2, 64]),
                            op=ALU.mult), reads=[pkey, 'kdec'], writes=['kd%d' % d])
                    else:
                        for h in range(2):
                            rop('act', lambda e, d=d, hc=hc, pTh=pTh, h=h: e.activation(
                                out=kd[:, d, hc * 4:(hc + 1) * 4, h * 64:(h + 1) * 64],
                                in_=pTh.rearrange("p (c h x) -> p c h x", c=4, h=2)[:, :, h, :], func=AF.Copy,
                                scale=kdec[:, d * 4 + 2 * ti + h:d * 4 + 2 * ti + h + 1]),
                                reads=[pkey, 'kdec'], writes=['kd%d' % d])
            sub()
            bdm = SP('bdmask')
            s0v = lambda d: d_s0[:, l * 4 + d * 2 + ti, :]
            for d in range(2):
                P.dma('sp', Scont[:, 0, d, :], s0v(d), reads=[EK()], writes=['Sc%d_0' % d])
            pU = {}
            for d in range(2):
                for hc in range(2):
                    pst, pk = ps_big()

                    def mmu(e, d=d, hc=hc, pst=pst):
                        for c4 in range(4):
                            c = hc * 4 + c4
                            ins = e.matmul(pst[:, c4 * 128:(c4 + 1) * 128], lhsT=kd[:, d, c, :], rhs=vr[:, c, :],
                                           start=True, stop=True)
                        return ins
                    rop('pe', mmu, reads=['kd%d' % d, 'vr'], writes=[pk])
                    pU[(d, hc)] = (pst, pk)
            if filler is not None:
                filler()
            psO_h = [(ps[7], 'ps7'), None]

            def att_a(c):
                cs = slice(c * 128, (c + 1) * 128)
                psA, pkA = ps_small()

                def mma(e, psA=psA, cs=cs):
                    for d in range(2):
                        for hl in range(2):
                            ins = e.matmul(psA[:, (d * 2 + hl) * 128:(d * 2 + hl + 1) * 128], lhsT=kz[:, d, hl, cs],
                                           rhs=qk[:, 2 * d, cs], start=True, stop=True)
                    return ins
                rop('pe', mma, reads=qkk(0) + qkk(2) + ['kz'], writes=[pkA])
                a_ = c % 2
                rop('dve', lambda e, psA=psA, a_=a_: e.tensor_tensor(
                    out=am[:, a_, :], in0=psA[:], in1=dmaskT[:, ti, :, :].rearrange("p a b -> p (a b)"), op=ALU.mult),
                    reads=[pkA, 'dmaskT'], writes=['am%d' % a_])

            def att_intra(c):
                half, c4 = c // 4, c % 4
                if psO_h[half] is None:
                    psO_h[half] = ps_big()
                psO, pkO = psO_h[half]
                a_ = c % 2

                def mmi(e, psO=psO, c=c, c4=c4, a_=a_):
                    oc = slice(c4 * 128, (c4 + 1) * 128)
                    for hl in range(2):
                        for d in range(2):
                            ins = e.matmul(psO[hl * 64:(hl + 1) * 64, oc], lhsT=vr[:, c, hl * 64:(hl + 1) * 64],
                                           rhs=am[:, a_, (d * 2 + hl) * 128:(d * 2 + hl + 1) * 128],
                                           start=(c4 == 0 and d == 0), stop=False, tile_position=(0, hl * 64),
                                           skip_group_check=True)
                    return ins
                rop('pe', mmi, reads=['am%d' % a_, 'vr'], writes=[pkO])

            def att_inter(c):
                half, c4 = c // 4, c % 4
                psO, pkO = psO_h[half]
                cs = slice(c * 128, (c + 1) * 128)

                def mmx(e, psO=psO, c=c, c4=c4, cs=cs):
                    oc = slice(c4 * 128, (c4 + 1) * 128)
                    e.matmul(psO[:, oc], lhsT=Sb[:, 0, c, :], rhs=qd[:, 0, cs], start=False, stop=False, skip_group_check=True)
                    return e.matmul(psO[:, oc], lhsT=Sb[:, 1, c, :], rhs=qd[:, 1, cs], start=False, stop=False,
                                    skip_group_check=True)
                rop('pe', mmx, reads=['Sb0', 'Sb1', 'qd0', 'qd1'], writes=[pkO])

            for i in range(8):
                for d in range(2):
                    c = i if d == 0 else 7 - i
                    pst, pk = pU[(d, c // 4)]
                    recur(l, ti, d, [c], pst, pk, Sb, stage, Stmp, Scont, bdm, scur)
                att_a(i)
                if i >= 1:
                    att_intra(i - 1)
            att_intra(7)
            sub()
            for c in range(8):
                att_inter(c)
            sub()
            steps = []
            for half in range(2):
                hs = slice(half * 512, (half + 1) * 512)
                psO, pkO = psO_h[half]
                o_h = o_sb[:, half, :]
                ob_h = ob[:, half, :]
                rs_h = rso[:, half, :]
                ko, kb, kr = 'o_sb%d' % half, 'ob%d' % half, 'rso%d' % half
                st_ = []
                st_.append(lambda psO=psO, pkO=pkO, ob_h=ob_h, kb=kb: rop(
                    'act', lambda e: e.activation(out=ob_h, in_=psO[:], func=AF.Copy), reads=[pkO], writes=[kb]))
                st_.append(lambda psO=psO, pkO=pkO, o_h=o_h, ko=ko: rop(
                    'act', lambda e: e.activation(out=o_h, in_=psO[:], func=AF.Copy), reads=[pkO], writes=[ko]))
                psM_box = {}

                def s_mean(ob_h=ob_h, kb=kb, box=psM_box):
                    psM, pkM = ps_small()
                    box['m'] = (psM, pkM)
                    rop('pe', lambda e: e.matmul(psM[:], lhsT=hbd[:], rhs=ob_h, start=True, stop=True), reads=[kb, 'hbd'], writes=[pkM])
                st_.append(s_mean)

                def s_cen(o_h=o_h, ko=ko, box=psM_box):
                    psM, pkM = box['m']
                    rop('dve', lambda e: e.tensor_tensor(out=o_h, in0=o_h, in1=psM[:], op=ALU.subtract), reads=[ko, pkM], writes=[ko])
                st_.append(s_cen)
                st_.append(lambda o_h=o_h, ko=ko, ob_h=ob_h, kb=kb: rop(
                    'act', lambda e: e.activation(out=ob_h, in_=o_h, func=AF.Square), reads=[ko], writes=[kb]))

                def s_var(ob_h=ob_h, kb=kb, box=psM_box):
                    psV, pkV = ps_small()
                    box['v'] = (psV, pkV)
                    rop('pe', lambda e: e.matmul(psV[:], lhsT=hbd[:], rhs=ob_h, start=True, stop=True), reads=[kb, 'hbd'], writes=[pkV])
                st_.append(s_var)

                def s_ln(rs_h=rs_h, kr=kr, box=psM_box):
                    psV, pkV = box['v']
                    rop('act', lambda e: e.activation(out=rs_h, in_=psV[:], func=AF.Ln, bias=EPS, scale=1.0), reads=[pkV], writes=[kr])
                st_.append(s_ln)
                st_.append(lambda rs_h=rs_h, kr=kr: rop(
                    'act', lambda e: e.activation(out=rs_h, in_=rs_h, func=AF.Exp, scale=-0.5), reads=[kr], writes=[kr]))
                st_.append(lambda o_h=o_h, ko=ko, rs_h=rs_h, kr=kr: rop(
                    'dve', lambda e: e.tensor_tensor(out=o_h, in0=o_h, in1=rs_h, op=ALU.mult), reads=[ko, kr], writes=[ko]))
                st_.append(lambda o_h=o_h, ko=ko, hs=hs, half=half: rop(
                    'dve', lambda e: e.tensor_tensor(out=yT[:, 6 + ti, hs], in0=o_h, in1=sg[:, hs], op=ALU.mult),
                    reads=[ko, 'sg%d' % half], writes=['y%d_%d' % (6 + ti, half)]))
                steps.append(st_)
            for i in range(len(steps[0])):
                steps[0][i]()
                steps[1][i]()

        def recur(l, ti, d, chunks, pst, pk, Sb, stage, Stmp, Scont, bdm, scur):
            sd = sdec[:, d * 2 + ti:d * 2 + ti + 1]
            for c in chunks:
                c4 = c % 4
                cur = scur[d]
                nxt = 1 - cur
                kcur = 'Sc%d_%d' % (d, cur)
                knxt = 'Sc%d_%d' % (d, nxt)
                Sc = Scont[:, cur, d, :]
                Sn = Scont[:, nxt, d, :]
                for hl in range(2):
                    r = slice(hl * 64, (hl + 1) * 64)
                    rop('act', lambda e, c=c, r=r, Sc=Sc: e.activation(out=Sb[r, d, c, r], in_=Sc[r, r], func=AF.Copy),
                        reads=[kcur], writes=['Sb%d' % d])
                seg_end = (c % 2 == 1) if d == 0 else (c % 2 == 0)
                last = (c == 7) if d == 0 else (c == 0)
                seg = c // 2
                if seg_end:
                    dst = stage[:, d, seg, :]
                    dkey = 'stage%d_%d' % (d, seg)
                    rop('dve', lambda e, c4=c4, dst=dst, Sc=Sc: e.scalar_tensor_tensor(
                        out=dst, in0=Sc, scalar=sd, in1=pst[:, c4 * 128:(c4 + 1) * 128], op0=ALU.mult, op1=ALU.add),
                        reads=[kcur, pk, 'sdec'], writes=[dkey])
                    for hl in range(2):
                        r = slice(hl * 64, (hl + 1) * 64)
                        P.dma('sp', d_st[l, d, seg, 2 * ti + hl], stage[r, d, seg, hl * 64:(hl + 1) * 64], reads=[dkey, EK()])
                    if not last:
                        rop('dve', lambda e, dst=dst, Sn=Sn: e.tensor_scalar(out=Sn, in0=dst, scalar1=flag, scalar2=None, op0=ALU.mult),
                            reads=[dkey, 'sp'], writes=[knxt])
                        scur[d] = nxt
                else:
                    rop('dve', lambda e, c4=c4, Sc=Sc, Sn=Sn: e.scalar_tensor_tensor(
                        out=Sn, in0=Sc, scalar=sd, in1=pst[:, c4 * 128:(c4 + 1) * 128], op0=ALU.mult, op1=ALU.add),
                        reads=[kcur, pk, 'sdec'], writes=[knxt])
                    scur[d] = nxt

        def mixer_conv_gmlp_pool(l):
            A = arena
            hcp = A.alloc(BF16, [128, 2, 4, 286])
            sigt = A.alloc(F32, [128, 2, 512])
            acc = A.alloc(F32, [128, 2, 1024])
            accb = A.alloc(BF16, [128, 2, 512])
            sqc = A.alloc(BF16, [128, 2, 512])
            rsc = A.alloc(F32, [128, 512])
            hsb = A.alloc(BF16, [128, 2, 512])
            dg = A.alloc(BF16, [128, 2, 8, 128])
            u_sb = A.alloc(F32, [128, 2, 1024])
            vg = A.alloc(BF16, [128, 8, 256])
            gtmp = sigt
            pp = A.alloc(F32, [128, 2, 4, 272])
            pA = A.alloc(F32, [128, 4, 272])
            pB = A.alloc(F32, [128, 4, 272])
            dpb = A.alloc(BF16, [128, 2, 1024])

            if l == 0:
                P.dma('sp', tabs[:], d_tabs[:, 2048:4096], writes=['tabs'])

            def pool_part1():
                rop('dve', lambda e: e.memset(pp[:], 0.0), writes=['pp'])
                wv, wk = w_get(('pool', l))
                for ti in range(2):
                    for half in range(2):
                        pst, pk = ps_big()
                        fm_group(lambda kt, ti=ti, wv=wv: wv[:, kt, ti * 128:(ti + 1) * 128], half, pst, pk, wk)
                        rop('act', lambda e, pst=pst, ti=ti, half=half: e.activation(
                            out=pp[:, ti, 2 * half:2 * half + 2, 8:264], in_=pst[:].rearrange("p (s x) -> p s x", s=2), func=AF.Copy),
                            reads=[pk, 'pp'], writes=['pp'])
                w_done()

            def pool_chain():
                for ti in range(2):
                    rop('pool', lambda e, ti=ti: e.tensor_scalar(out=pp[:, ti, 1:4, 0:8], in0=pp[:, ti, 0:3, 256:264], scalar1=flag,
                                                                 scalar2=None, op0=ALU.mult), reads=['pp', 'sp'], writes=['pp'])
                    rop('pool', lambda e, ti=ti: e.tensor_scalar(out=pp[:, ti, 0:3, 264:272], in0=pp[:, ti, 1:4, 8:16], scalar1=flag,
                                                                 scalar2=None, op0=ALU.mult), reads=['pp', 'sp'], writes=['pp'])
                pbd = SB('pool_bd').rearrange("p (l t c) -> p l t c", l=2, t=2)
                pso = SP32_OFF['pool_scale'][0] + l * 2
                for ti in range(2):
                    pv = pp[:, ti]
                    rop('pool', lambda e, pv=pv: e.tensor_tensor(out=pA[:, :, 1:272], in0=pv[:, :, 1:272], in1=pv[:, :, 0:271], op=ALU.add),
                        reads=['pp', 'pA'], writes=['pA'])
                    rop('pool', lambda e: e.tensor_tensor(out=pB[:, :, 2:271], in0=pA[:, :, 1:270], in1=pA[:, :, 3:272], op=ALU.add),
                        reads=['pA', 'pB'], writes=['pB'])
                    if ti == 0:
                        lo_src, hi_src = pA, pB
                    else:
                        rop('pool', lambda e: e.tensor_tensor(out=pA[:, :, 4:269], in0=pB[:, :, 2:267], in1=pB[:, :, 6:271], op=ALU.add),
                            reads=['pB', 'pA'], writes=['pA'])
                        rop('pool', lambda e: e.tensor_tensor(out=pB[:, :, 8:264], in0=pA[:, :, 4:260], in1=pA[:, :, 12:268], op=ALU.add),
                            reads=['pA', 'pB'], writes=['pB'])
                        lo_src, hi_src = pA, pB
                    for (r, src) in ((slice(0, 64), lo_src), (slice(64, 128), hi_src)):
                        rop('pool', lambda e, r=r, src=src, ti=ti: e.tensor_tensor(
                            out=src[r, :, 8:264], in0=src[r, :, 8:264], in1=invc[r, ti, :].rearrange("p (s x) -> p s x", s=4), op=ALU.mult),
                            reads=['pA', 'pB', 'tabs'], writes=['pA', 'pB'])
                        rop('pool', lambda e, r=r, src=src, ti=ti, pv=pv: e.tensor_tensor(
                            out=dpb[r, ti, :].rearrange("p (s x) -> p s x", s=4), in0=src[r, :, 8:264], in1=pv[r, :, 8:264], op=ALU.subtract),
                            reads=['pA', 'pB', 'pp'], writes=['dpb%d' % ti])

            def pool2_item(ti, half):
                pbd = SB('pool_bd').rearrange("p (l t c) -> p l t c", l=2, t=2)
                pso = SP32_OFF['pool_scale'][0] + l * 2
                hs = slice(half * 512, (half + 1) * 512)
                pst, pk = ps_small()
                rop('pe', lambda e: e.matmul(pst[:], lhsT=pbd[:, l, ti, :], rhs=dpb[:, ti, hs], start=True, stop=True),
                    reads=['dpb%d' % ti, 'sb'], writes=[pk])
                rop('act', lambda e: e.activation(
                    out=yT[:, 4 + ti, hs], in_=pst[:], func=AF.Identity, scale=sp[:, pso + ti:pso + ti + 1]),
                    reads=[pk, 'sp'], writes=['y%d_%d' % (4 + ti, half)])

            rop('dve', lambda e: e.memset(hcp[:], 0.0), writes=['hcp'])
            pool_part1()
            wv, wk = w_get(('conv', l))
            for ti in range(2):
                for half in range(2):
                    r = half
                    pst, pk = ps_big()
                    fm_group(lambda kt, ti=ti, wv=wv: wv[:, kt, (2 + ti) * 128:(3 + ti) * 128], half, pst, pk, wk)
                    rop('act', lambda e, pst=pst, r=r: e.activation(out=sigt[:, r, :], in_=pst[:], func=AF.Sigmoid),
                        reads=[pk], writes=['sigt%d' % r])
                    pst2, pk2 = ps_big()
                    fm_group(lambda kt, ti=ti, wv=wv: wv[:, kt, ti * 128:(ti + 1) * 128], half, pst2, pk2, wk)
                    rop('dve', lambda e, pst2=pst2, r=r, ti=ti, half=half: e.tensor_tensor(
                        out=hcp[:, ti, 2 * half:2 * half + 2, 15:271], in0=pst2[:].rearrange("p (s x) -> p s x", s=2),
                        in1=sigt[:, r, :].rearrange("p (s x) -> p s x", s=2), op=ALU.mult),
                        reads=[pk2, 'sigt%d' % r, 'hcp'], writes=['hcp'])
            w_done()
            for ti in range(2):
                rop('act', lambda e, ti=ti: e.activation(out=hcp[:, ti, 1:4, 0:15], in_=hcp[:, ti, 0:3, 256:271], func=AF.Copy,
                                                         scale=flag), reads=['hcp', 'sp'], writes=['hcp'])
                rop('act', lambda e, ti=ti: e.activation(out=hcp[:, ti, 0:3, 271:286], in_=hcp[:, ti, 1:4, 15:30], func=AF.Copy,
                                                         scale=flag), reads=['hcp', 'sp'], writes=['hcp'])
            pool_chain()
            dwo = SP32_OFF['conv_dw'][0]
            cbo = SP32_OFF['conv_b'][0]
            identb = SB('ident')
            psC = {}
            for ti in range(2):
                for half in range(2):
                    psC[(ti, half)] = ps_big()
            dgf = dg.rearrange("p a b c -> p (a b) c")
            dcnt = 0
            for k in range(31):
                for ti in range(2):
                    base = dwo + (l * 2 + ti) * 31
                    r = dcnt % 16
                    dcnt += 1
                    rop('act', lambda e, r=r, base=base, k=k: e.activation(
                        out=dgf[:, r, :], in_=identb, func=AF.Copy, scale=sp[:, base + k:base + k + 1]),
                        reads=['sb', 'sp'], writes=['dg%d' % r])
                    for half in range(2):
                        pst, pk = psC[(ti, half)]
                        rop('pe', lambda e, pst=pst, r=r, ti=ti, half=half, k=k: e.matmul(
                            pst[:].rearrange("p (s x) -> p s x", s=2), lhsT=dgf[:, r, :],
                            rhs=hcp[:, ti, 2 * half:2 * half + 2, k:k + 256], start=(k == 0), stop=(k == 30)),
                            reads=['dg%d' % r, 'hcp'], writes=[pk])
            for ti in range(2):
                for half in range(2):
                    hs = slice(half * 512, (half + 1) * 512)
                    pst, pk = psC[(ti, half)]
                    rop('act', lambda e, pst=pst, ti=ti, hs=hs: e.activation(
                        out=acc[:, ti, hs], in_=pst[:], func=AF.Identity, bias=sp[:, cbo + l * 2 + ti:cbo + l * 2 + ti + 1], scale=1.0),
                        reads=[pk, 'sp'], writes=['acc%d' % ti])
            wvg, wkg = w_get(('gmlp', l))
            G = []

            def gu(ti, half):
                hs = slice(half * 512, (half + 1) * 512)
                pst, pk = ps_big()
                fm_group(lambda kt: wvg[:, kt, ti * 128:(ti + 1) * 128], half, pst, pk, wkg)
                rop('act', lambda e: e.activation(out=u_sb[:, ti, hs], in_=pst[:], func=AF.Copy),
                    reads=[pk], writes=['u%d_%d' % (ti, half)])

            def gv(tp):
                pst, pk = ps_big()

                def mmv(e):
                    for u2 in range(2):
                        tt = tp * 2 + u2
                        for kt in range(8):
                            ins = e.matmul(pst[:, u2 * 256:(u2 + 1) * 256], lhsT=hT[:, kt, tt * 128:(tt + 1) * 128],
                                           rhs=wvg[:, kt, 256:512], start=(kt == 0), stop=(kt == 7))
                    return ins
                P.op('pe', mmv, reads=[wkg] + hT_all, writes=[pk])
                rop('act', lambda e: e.activation(
                    out=vg[:, tp * 2:tp * 2 + 2, :], in_=pst[:].rearrange("p (a b) -> p a b", a=2), func=AF.Copy),
                    reads=[pk], writes=['vg'])
                if tp == 3:
                    w_done()

            wsT = SB('wsT').rearrange("p (l h q) -> p l h q", l=2, h=4)
            gbo = SP32_OFF['gb_tab'][0]

            def gs(ti, half):
                hs = slice(half * 512, (half + 1) * 512)
                pst, pk = ps_small()

                def mmg(e):
                    for c4 in range(4):
                        c = half * 4 + c4
                        for hl in range(2):
                            h = 2 * ti + hl
                            ins = e.matmul(pst[hl * 64:(hl + 1) * 64, c4 * 128:(c4 + 1) * 128], lhsT=vg[:, c, h * 64:(h + 1) * 64],
                                           rhs=wsT[:, l, h, :], start=True, stop=True, tile_position=(0, hl * 64))
                    return ins
                rop('pe', mmg, reads=['vg', 'sb'], writes=[pk])
                g = half
                gb = sp[:, gbo + (l * 2 + ti) * 128:gbo + (l * 2 + ti + 1) * 128]
                rop('dve', lambda e: e.tensor_tensor(
                    out=gtmp[:, g, :].rearrange("p (c i) -> p c i", c=4), in0=pst[:].rearrange("p (c i) -> p c i", c=4),
                    in1=gb.unsqueeze(1).to_broadcast([128, 4, 128]), op=ALU.add), reads=[pk, 'sp'], writes=['sigt%d' % g])
                rop('dve', lambda e: e.tensor_tensor(out=yT[:, 2 + ti, hs], in0=gtmp[:, g, :], in1=u_sb[:, ti, hs], op=ALU.mult),
                    reads=['sigt%d' % g, 'u%d_%d' % (ti, half)], writes=['y%d_%d' % (2 + ti, half)])

            for ti in range(2):
                for half in range(2):
                    G.append(lambda ti=ti, half=half: gu(ti, half))
            for tp in range(4):
                G.append(lambda tp=tp: gv(tp))
            for ti in range(2):
                for half in range(2):
                    G.append(lambda ti=ti, half=half: gs(ti, half))
            for ti in range(2):
                for half in range(2):
                    G.append(lambda ti=ti, half=half: pool2_item(ti, half))

            lgo = SP32_OFF['ln_g'][0] + l * 2
            lbo = SP32_OFF['ln_b'][0] + l * 2
            pwv = SB('conv_pw').rearrange("p (l k c) -> p l k c", l=2, k=2)
            L = []
            for half in range(2):
                hs = slice(half * 512, (half + 1) * 512)
                box = {}

                def l1(hs=hs):
                    rop('act', lambda e: e.activation(out=accb[:], in_=acc[:, :, hs], func=AF.Copy),
                        reads=['acc0', 'acc1'], writes=['accb'])

                def l2(box=box):
                    psM, pkM = ps_small()
                    box['m'] = (psM, pkM)

                    def mmm(e):
                        e.matmul(psM[:], lhsT=ones256[:], rhs=accb[:, 0, :], start=True, stop=False)
                        return e.matmul(psM[:], lhsT=ones256[:], rhs=accb[:, 1, :], start=False, stop=True)
                    rop('pe', mmm, reads=['accb', 'ones256'], writes=[pkM])

                def l3(hs=hs, box=box):
                    psM, pkM = box['m']
                    for ti in range(2):
                        rop('dve', lambda e, ti=ti: e.tensor_tensor(out=acc[:, ti, hs], in0=acc[:, ti, hs], in1=psM[:], op=ALU.subtract),
                            reads=['acc%d' % ti, pkM], writes=['acc%d' % ti])

                def l4(hs=hs):
                    rop('act', lambda e: e.activation(out=sqc[:], in_=acc[:, :, hs], func=AF.Square),
                        reads=['acc0', 'acc1'], writes=['sqc'])

                def l5(box=box):
                    psV, pkV = ps_small()
                    box['v'] = (psV, pkV)

                    def mmv2(e):
                        e.matmul(psV[:], lhsT=ones256[:], rhs=sqc[:, 0, :], start=True, stop=False)
                        return e.matmul(psV[:], lhsT=ones256[:], rhs=sqc[:, 1, :], start=False, stop=True)
                    rop('pe', mmv2, reads=['sqc', 'ones256'], writes=[pkV])

                def l6(box=box):
                    psV, pkV = box['v']
                    rop('act', lambda e: e.activation(out=rsc[:], in_=psV[:], func=AF.Ln, bias=EPS, scale=1.0), reads=[pkV], writes=['rsc'])
                    rop('act', lambda e: e.activation(out=rsc[:], in_=rsc[:], func=AF.Exp, scale=-0.5), reads=['rsc'], writes=['rsc'])

                def l7(hs=hs):
                    for ti in range(2):
                        rop('dve', lambda e, ti=ti: e.scalar_tensor_tensor(
                            out=acc[:, ti, hs], in0=acc[:, ti, hs], scalar=sp[:, lgo + ti:lgo + ti + 1], in1=rsc[:],
                            op0=ALU.mult, op1=ALU.mult), reads=['acc%d' % ti, 'rsc', 'sp'], writes=['acc%d' % ti])
                        rop('act', lambda e, ti=ti: e.activation(
                            out=hsb[:, ti, :], in_=acc[:, ti, hs], func=AF.Silu, bias=sp[:, lbo + ti:lbo + ti + 1], scale=1.0),
                            reads=['acc%d' % ti, 'sp'], writes=['hsb%d' % ti])

                def l8(to, hs=hs, half=half):
                    pst, pk = ps_small()

                    def mmp(e):
                        e.matmul(pst[:], lhsT=pwv[:, l, 0, to * 128:(to + 1) * 128], rhs=hsb[:, 0, :], start=True, stop=False)
                        return e.matmul(pst[:], lhsT=pwv[:, l, 1, to * 128:(to + 1) * 128], rhs=hsb[:, 1, :], start=False, stop=True)
                    rop('pe', mmp, reads=['hsb0', 'hsb1', 'sb'], writes=[pk])
                    rop('act', lambda e: e.activation(out=yT[:, to, hs], in_=pst[:], func=AF.Copy),
                        reads=[pk], writes=['y%d_%d' % (to, half)])
                L += [l1, l2, l3, l4, l5, l6, l7, lambda l8=l8: l8(0), lambda l8=l8: l8(1)]
            while L or G:
                if L:
                    L.pop(0)()
                if G:
                    G.pop(0)()

        def out_proj(l):
            y_keys = lambda half: ['y%d_%d' % (t, half) for t in range(8)]
            wv0, wk0 = w_get(('wo', l, 0))
            wv1, wk1 = w_get(('wo', l, 1), ahead=1)
            for half in range(2):
                hs = slice(half * 512, (half + 1) * 512)
                for ct in range(8):
                    wv, wk = (wv0, wk0) if ct < 4 else (wv1, wk1)
                    c4 = ct % 4
                    pst, pk = ps_big()

                    def mm(e, pst=pst, wv=wv, c4=c4, hs=hs):
                        for kt in range(8):
                            ins = e.matmul(pst[:], lhsT=wv[:, kt, c4 * 128:(c4 + 1) * 128], rhs=yT[:, kt, hs],
                                           start=(kt == 0), stop=(kt == 7))
                        return ins
                    P.op('pe', mm, reads=[wk] + y_keys(half), writes=[pk])
                    P.op('dve', lambda e, pst=pst, ct=ct, hs=hs: e.scalar_tensor_tensor(
                        out=xT[:, ct, hs], in0=pst[:], scalar=mod_l[l][:, 16 + ct:17 + ct], in1=xT[:, ct, hs], op0=ALU.mult, op1=ALU.add),
                        reads=[pk, 'modp%d_2' % l, 'xh%d_%d' % (ct, half)], writes=['xh%d_%d' % (ct, half)])
            w_done()
            w_done()

        def ffn(l):
            A = arena
            act = A.alloc(BF16, [128, 22, 1024])
            sgt = A.alloc(F32, [128, 2, 512])
            for jj in range(11):
                if l == 0 and jj == 3:
                    decay_tables(1)
                wv, wk = w_get(('ffi', l, jj))
                for j2 in range(2):
                    j = jj * 2 + j2
                    for half in range(2):
                        hs = slice(half * 512, (half + 1) * 512)
                        psG, pkG = ps_big()
                        fm_group(lambda kt, j2=j2, wv=wv: wv[:, kt, 0, j2 * 128:(j2 + 1) * 128], half, psG, pkG, wk)
                        r = half
                        rop('act', lambda e, psG=psG, r=r: e.activation(out=sgt[:, r, :], in_=psG[:], func=AF.Silu),
                            reads=[pkG], writes=['sgt%d' % r])
                        psU, pkU = ps_big()
                        fm_group(lambda kt, j2=j2, wv=wv: wv[:, kt, 1, j2 * 128:(j2 + 1) * 128], half, psU, pkU, wk)
                        rop('dve', lambda e, psU=psU, r=r, j=j, hs=hs: e.tensor_tensor(out=act[:, j, hs], in0=psU[:], in1=sgt[:, r, :],
                                                                                     op=ALU.mult),
                            reads=[pkU, 'sgt%d' % r], writes=['act%d_%d' % (j, half)])
                w_done()
            for half in range(2):
                hs = slice(half * 512, (half + 1) * 512)
                for ct in range(8):
                    wv, wk = w_get(('ffo', l, ct, half))
                    pst, pk = ps_big()

                    def mm(e, pst=pst, wv=wv, hs=hs):
                        for j in range(22):
                            ins = e.matmul(pst[:], lhsT=wv[:, j, :], rhs=act[:, j, hs], start=(j == 0), stop=(j == 21))
                        return ins
                    rop('pe', mm, reads=[wk] + ['act%d_%d' % (j, half) for j in range(22)], writes=[pk])
                    rop('dve', lambda e, pst=pst, ct=ct, hs=hs: e.scalar_tensor_tensor(
                        out=xT[:, ct, hs], in0=pst[:], scalar=mod_l[l][:, 40 + ct:41 + ct], in1=xT[:, ct, hs], op0=ALU.mult, op1=ALU.add),
                        reads=[pk, 'modp%d_5' % l, 'xh%d_%d' % (ct, half)], writes=['xh%d_%d' % (ct, half)])
                    w_done()

        stage = {'n': 0}

        def chk():
            stage['n'] += 1
            if stop_after is not None and stage['n'] >= stop_after:
                raise _Stop()
        try:
            chk()
            decay_tables(0)
            rmsnorm_stats(0)
            rmsnorm_stats(1)
            mod_group(0, range(4))
            chk()
            for l in range(DEPTH):
                norm_mod(l, 0, skip_stats=(l == 0))
                fine['n'] = 2
                chk()
                release()
                mixer_retention(l, 0, filler=(lambda: mod_group(0, range(4, 8))) if l == 0 else None)
                if l == 0:
                    mod_group(1, range(0, 3))
                chk()
                release()
                mixer_retention(l, 1, filler=(lambda: mod_group(0, range(8, 12))) if l == 0 else None)
                if l == 0:
                    mod_group(1, range(3, 6))
                chk()
                release()
                mixer_conv_gmlp_pool(l)
                if l == 0:
                    mod_group(1, range(6, 9))
                chk()
                out_proj(l)
                if l == 0:
                    mod_group(1, range(9, 12))
                chk()
                norm_mod(l, 1)
                fine['n'] = 4
                release()
                ffn(l)
                chk()
            gfo = SP32_OFF['gF'][0]
            yTd = d_y.rearrange("(kt p) t -> p kt t", p=128)
            rmsnorm_stats(0)
            rmsnorm_stats(1)
            for half in range(2):
                hs = slice(half * 512, (half + 1) * 512)
                for kt in range(8):
                    r = kt % 2
                    P.op('dve', lambda e, kt=kt, r=r, hs=hs, half=half: e.scalar_tensor_tensor(
                        out=xT[:, kt, hs], in0=xT[:, kt, hs], scalar=sp[:, gfo + kt:gfo + kt + 1], in1=rs_n[:, half, :],
                        op0=ALU.mult, op1=ALU.mult), reads=['xh%d_%d' % (kt, half), 'rs_n%d' % half, 'sp'], writes=['xo%d_%d' % (kt, half)])
                    P.dma('sp', yTd[:, kt, hs], xT[:, kt, hs], reads=['xo%d_%d' % (kt, half)])
        except _Stop:
            for e_ in ('pe', 'act', 'dve', 'pool'):
                if P.ecount[e_]:
                    P.wait_tok('sp', (P.eidx[e_], P.ecount[e_]))
        for k in range(P.n_dma):
            if P.dma_count[k]:
                P.wait_tok('sp', (P.dma_base + k, P.dma_count[k]))
        P.run(blk)
    return nc


def _host_prepare(inp):
    f32 = np.float32
    x_prompt = np.asarray(inp['x_prompt'], f32)
    x_sample = np.asarray(inp['x_sample'], f32)
    state_ret = np.asarray(inp['state_ret'], f32)
    c = np.asarray(inp['c'], f32)
    c_ctx = np.asarray(inp['c_ctx'], f32)
    p = np.arange(128)

    def colmaj(v, nt):
        return np.ascontiguousarray(v.reshape(nt, 128).T)

    sp_common = np.zeros((128, NS), f32)

    def put(name, arr):
        o, n = SP32_OFF[name]
        sp_common[:, o:o + n] = arr.reshape(128, n)
    put('g1', np.stack([colmaj(inp['g_norm1'][l], 8) for l in range(2)], 1))
    put('g2', np.stack([colmaj(inp['g_norm2'][l], 8) for l in range(2)], 1))
    put('gF', colmaj(np.asarray(inp['g_final'], f32), 8))
    put('b_fm', np.stack([colmaj(np.asarray(inp['b_ada'], f32)[l], 48) for l in range(2)], 1))
    dw = np.asarray(inp['conv_dw'], f32)
    put('conv_dw', np.ascontiguousarray(dw.reshape(2, 31, 2, 128).transpose(3, 0, 2, 1)))
    for nm, key in (('conv_b', 'conv_b'), ('ln_g', 'conv_ln_g'), ('ln_b', 'conv_ln_b'), ('pool_scale', 'pool_scale')):
        a = np.asarray(inp[key], f32).reshape(2, 2, 128).transpose(2, 0, 1)
        put(nm, np.ascontiguousarray(a))
    gb = np.asarray(inp['gmlp_b'], f32)
    gbt = np.zeros((128, 2, 2, 128), f32)
    for l in range(2):
        for ti in range(2):
            for hl in range(2):
                gbt[hl * 64:(hl + 1) * 64, l, ti, :] = gb[l, 2 * ti + hl][None, :]
    put('gb_tab', gbt)
    rd = np.asarray(inp['ret_decay'], f32)
    rdcol = np.zeros((128, 2, 2, 2), f32)
    for ti in range(2):
        for hl in range(2):
            rdcol[hl * 64:(hl + 1) * 64, :, :, ti] = rd[:, :, 2 * ti + hl][None]
    put('rdcol', rdcol)
    put('rdb', np.broadcast_to(rd.reshape(1, 16), (128, 16)).copy())
    ii = np.arange(128)
    diff_f = np.maximum(ii[None, :] - ii[:, None], 0).astype(f32)
    diff_b = np.maximum(ii[:, None] - ii[None, :], 0).astype(f32)
    put('diffT', np.stack([diff_f, diff_b], 1))
    tri_f = (ii[None, :] >= ii[:, None]).astype(f32)
    tri_b = (ii[:, None] >= ii[None, :]).astype(f32)
    put('triT', np.stack([tri_f, tri_b], 1))
    ramp = np.stack([np.broadcast_to((ii + 1).astype(f32), (128, 128)), np.broadcast_to((128 - ii).astype(f32), (128, 128))], 1)
    put('ramp_q', ramp)
    put('rampcol_k', np.stack([(127 - ii).astype(f32), ii.astype(f32)], 1))
    bd = np.zeros((128, 128), f32)
    bd[0:64, 0:64] = 1
    bd[64:, 64:] = 1
    put('bdmask', bd)

    bfp = np.zeros((128, NB), f32)

    def putb(name, arr):
        o, n = SPBF_OFF[name]
        bfp[:, o:o + n] = arr.reshape(128, n)
    pw = np.asarray(inp['conv_pw'], f32)
    putb('conv_pw', np.ascontiguousarray(pw.reshape(2, 2, 128, 256).transpose(2, 0, 1, 3)))
    ws = np.asarray(inp['gmlp_ws'], f32)
    putb('wsT', np.ascontiguousarray(ws.transpose(3, 0, 1, 2)))
    pwl = np.asarray(inp['pool_w'], f32)
    pbd = np.zeros((128, 2, 2, 128), f32)
    for l in range(2):
        for ti in range(2):
            for gl in range(2):
                pbd[gl * 64:(gl + 1) * 64, l, ti, gl * 64:(gl + 1) * 64] = pwl[l, 2 * ti + gl]
    putb('pool_bd', pbd)
    putb('ident', np.eye(128, dtype=f32))
    perm = np.arange(128)
    dd = perm % 32
    perm = np.where(dd < 16, perm + 16, perm - 16)
    Pm = np.zeros((128, 128), f32)
    Pm[perm, np.arange(128)] = 1.0
    putb('Pm', Pm)

    t = np.arange(1024)
    nf = 16
    inv = (f32(10000.0) ** (-np.arange(nf, dtype=f32) / f32(nf))).astype(f32)
    rows = (t // 64).astype(f32)
    cols = (t % 64).astype(f32)
    dd64 = np.arange(128) % 64
    fidx = dd64 % 16
    pos = np.where((dd64 < 32)[:, None], rows[None, :], cols[None, :]).astype(f32)
    ang = (pos * inv[fidx][:, None]).astype(f32)
    cos_s = np.cos(ang).astype(f32)
    sin_s = np.sin(ang).astype(f32)
    is_x1 = ((dd64 % 32) < 16)
    sinS = np.where(is_x1[:, None], -sin_s, sin_s).astype(f32)
    sinP_s = sinS[perm]
    def invcount(L):
        tt = np.arange(1024) % L
        out = np.zeros((128, 2, 1024), f32)
        for ti in range(2):
            for gl in range(2):
                w = (2, 4, 8, 16)[2 * ti + gl]
                lo = np.clip(tt - w // 2, 0, L)
                hi = np.clip(tt + w // 2, 0, L)
                out[gl * 64:(gl + 1) * 64, ti, :] = (1.0 / (hi - lo).astype(f32))[None, :]
        return out
    tabs_sample = np.concatenate([cos_s, sinP_s, invcount(1024).reshape(128, 2048)], 1).astype(f32)
    tabs_prompt = np.concatenate([np.ones((128, 1024), f32), np.zeros((128, 1024), f32), invcount(256).reshape(128, 2048)], 1).astype(f32)

    in_maps = []
    wts = dict(w_ada=np.ascontiguousarray(inp['w_ada'], f32), w_in=np.ascontiguousarray(inp['w_in'], f32),
               w_out=np.ascontiguousarray(inp['w_out'], f32), w_ffn_in=np.ascontiguousarray(inp['w_ffn_in'], f32),
               w_ffn_out=np.ascontiguousarray(inp['w_ffn_out'], f32))
    for core in range(8):
        spc = sp_common.copy()
        if core < 4:
            xs = x_prompt[4 * core:4 * core + 4].reshape(1024, 1024)
            cv = c_ctx
            flagv = 0.0
            tabs = tabs_prompt
            s0 = np.zeros((128, 8, 128), f32)
        else:
            b = core - 4
            xs = x_sample[b]
            cv = c[b]
            flagv = 1.0
            tabs = tabs_sample
            s0 = np.zeros((128, 2, 2, 2, 128), f32)
            for ti in range(2):
                for hl in range(2):
                    s0[hl * 64:(hl + 1) * 64, :, :, ti, hl * 64:(hl + 1) * 64] = state_ret[b, :, :, 2 * ti + hl].transpose(2, 0, 1, 3)
            s0 = s0.reshape(128, 8, 128)
        o, n = SP32_OFF['cvec']
        spc[:, o:o + n] = colmaj(cv, 8)
        o, n = SP32_OFF['flag']
        spc[:, o] = flagv
        m = dict(xT=np.ascontiguousarray(xs.T), sp32=spc, spbf=bfp, tabs=tabs, s0=np.ascontiguousarray(s0))
        m.update(wts)
        in_maps.append(m)
    return in_maps


_NC_CACHE = {}


def kernel(**inputs):
    in_maps = _host_prepare(inputs)
    if 'nc' not in _NC_CACHE:
        _NC_CACHE['nc'] = build_program()
    nc = _NC_CACHE['nc']
    res = run_bass_kernel_spmd(nc, in_maps, core_ids=list(range(8)))
    r = res.results
    y_prompt = np.zeros((16, 256, 1024), np.float32)
    y_sample = np.zeros((4, 1024, 1024), np.float32)
    new_state = np.zeros((16, 2, 2, 4, 64, 64), np.float32)
    for core in range(4):
        y_prompt[4 * core:4 * core + 4] = np.asarray(r[core]['yT']).T.reshape(4, 256, 1024)
        stc = np.asarray(r[core]['st'])
        new_state[4 * core:4 * core + 4] = stc.transpose(2, 0, 1, 3, 4, 5)
    for core in range(4, 8):
        y_sample[core - 4] = np.asarray(r[core]['yT']).T
    return (y_prompt, y_sample, new_state)
```

```python
import contextlib
import math
import numpy as np
import concourse.bass as bass
import concourse.mybir as mybir
from concourse.bass_utils import run_bass_kernel_spmd

F32 = mybir.dt.float32
BF16 = mybir.dt.bfloat16
ALU = mybir.AluOpType
AF = mybir.ActivationFunctionType

DEPTH = 2
DFF = 2816
EPS = 1e-6
NSLOT = 4
SLOTW = 4096
ARENA_WORDS = 15872


class Prog:
    ENG = ['pe', 'act', 'dve', 'pool', 'sp']

    def __init__(self, nc, stack, n_dma_sems=32):
        self.nc = nc
        self.streams = {e: [] for e in self.ENG}
        self.sems = []
        self.eidx = {}
        for e in self.ENG:
            self.eidx[e] = len(self.sems)
            self.sems.append(stack.enter_context(nc.semaphore("es_" + e)))
        self.ecount = {e: 0 for e in self.ENG}
        self.dma_base = len(self.sems)
        self.n_dma = n_dma_sems
        for i in range(n_dma_sems):
            self.sems.append(stack.enter_context(nc.semaphore("ds_%d" % i)))
        self.dma_count = [0] * n_dma_sems
        self.dma_rr = 0
        self.dma_rr_pool = 0
        self.waited = {e: {} for e in self.ENG}
        self.res_w = {}
        self.res_r = {}
        self.nwaits = 0
        self.nops = {e: 0 for e in self.ENG}

    def _deps(self, reads, writes):
        deps = {}

        def add(tok):
            if tok is None:
                return
            s, v = tok
            if deps.get(s, 0) < v:
                deps[s] = v
        for r in reads:
            add(self.res_w.get(r))
        for w in writes:
            add(self.res_w.get(w))
            for s, v in self.res_r.get(w, {}).items():
                add((s, v))
        return deps

    def _emit_waits(self, eng, deps):
        for s, v in sorted(deps.items()):
            if eng == 'pe' and s == self.eidx['pe']:
                continue
            if self.waited[eng].get(s, 0) >= v:
                continue
            self.waited[eng][s] = v
            self.streams[eng].append(('wait', s, v))
            self.nwaits += 1

    def _record(self, tok, reads, writes):
        s, v = tok
        for r in reads:
            d = self.res_r.setdefault(r, {})
            if d.get(s, 0) < v:
                d[s] = v
        for w in writes:
            self.res_w[w] = tok
            self.res_r[w] = {}

    def op(self, eng, fn, reads=(), writes=()):
        deps = self._deps(reads, writes)
        self._emit_waits(eng, deps)
        self.ecount[eng] += 1
        tok = (self.eidx[eng], self.ecount[eng])
        self.streams[eng].append(('op', fn, self.eidx[eng]))
        self._record(tok, reads, writes)
        self.nops[eng] += 1
        return tok

    def dma(self, eng, out, in_, reads=(), writes=(), **kw):
        half = self.n_dma // 2
        if eng == 'pool':
            k = self.dma_rr_pool
            self.dma_rr_pool = (k + 1) % half
        else:
            k = half + self.dma_rr
            self.dma_rr = (self.dma_rr + 1) % (self.n_dma - half)
        s = self.dma_base + k
        deps = self._deps(reads, writes)
        if self.dma_count[k] > 0 and deps.get(s, 0) < self.dma_count[k]:
            deps[s] = self.dma_count[k]
        self._emit_waits(eng, deps)
        self.dma_count[k] += 16
        tok = (s, self.dma_count[k])
        self.streams[eng].append(('dma', out, in_, s, kw))
        self._record(tok, reads, writes)
        return tok

    def wait_tok(self, eng, tok):
        self._emit_waits(eng, {tok[0]: tok[1]})

    def run(self, block):
        sems = self.sems

        def runner(name):
            def f(e):
                for item in self.streams[name]:
                    if item[0] == 'wait':
                        e.wait_ge(sems[item[1]], item[2])
                    elif item[0] == 'op':
                        ins = item[1](e)
                        ins.then_inc(sems[item[2]], 1)
                    else:
                        _, out, in_, s, kw = item
                        e.dma_start(out=out, in_=in_, **kw).then_inc(sems[s], 16)
            return f
        block.tensor(runner('pe'))
        block.scalar(runner('act'))
        block.vector(runner('dve'))
        block.gpsimd(runner('pool'))
        block.sync(runner('sp'))


class Arena:
    def __init__(self, t, words):
        self.t = t
        self.words = words
        self.off = 0
        self.epoch = 0

    def reset(self):
        self.off = 0
        self.epoch += 1

    def alloc(self, dtype, shape):
        n = int(np.prod(shape[1:]))
        sz = 4 if dtype == F32 else 2
        words = (n * sz + 3) // 4
        words = (words + 7) // 8 * 8
        assert self.off + words <= self.words, ("arena overflow", self.off, words)
        ap = self.t[:, self.off:self.off + words]
        self.off += words
        if dtype != F32:
            ap = ap.bitcast(dtype)
        ap = ap[:, 0:n]
        if len(shape) == 3:
            ap = ap.rearrange("p (a b) -> p a b", a=shape[1])
        elif len(shape) == 4:
            ap = ap.rearrange("p (a b c) -> p a b c", a=shape[1], b=shape[2])
        elif len(shape) == 5:
            ap = ap.rearrange("p (a b c d) -> p a b c d", a=shape[1], b=shape[2], c=shape[3])
        return ap


def _sp32_layout():
    off = {}
    cur = 0

    def add(name, n):
        nonlocal cur
        off[name] = (cur, n)
        cur += n
    add('cvec', 8)
    add('flag', 1)
    add('g1', 16)
    add('g2', 16)
    add('gF', 8)
    add('b_fm', 96)
    add('conv_dw', 124)
    add('conv_b', 4)
    add('ln_g', 4)
    add('ln_b', 4)
    add('pool_scale', 4)
    add('gb_tab', 512)
    add('rdcol', 8)
    add('rdb', 16)
    add('diffT', 256)
    add('triT', 256)
    add('ramp_q', 256)
    add('rampcol_k', 2)
    add('bdmask', 128)
    return off, cur


SP32_OFF, NS = _sp32_layout()


def _spbf_layout():
    off = {}
    cur = 0

    def add(name, n):
        nonlocal cur
        off[name] = (cur, n)
        cur += n
    add('conv_pw', 1024)
    add('wsT', 1024)
    add('pool_bd', 512)
    add('ident', 128)
    add('Pm', 128)
    return off, cur


SPBF_OFF, NB = _spbf_layout()


class _Stop(Exception):
    pass


def build_program(debug=None, stop_after=None, stop_sub=None):
    nc = bass.Bass("TRN2", target_bir_lowering=False)
    dram_in = lambda n, s: nc.dram_tensor(n, s, F32, kind="ExternalInput").ap()
    d_x = dram_in("xT", [1024, 1024])
    d_sp = dram_in("sp32", [128, NS])
    d_bf = dram_in("spbf", [128, NB])
    d_tabs = dram_in("tabs", [128, 4096])
    d_s0 = dram_in("s0", [128, 8, 128])
    d_wada = dram_in("w_ada", [2, 1024, 6144])
    d_win = dram_in("w_in", [2, 1024, 2816])
    d_wout = dram_in("w_out", [2, 1024, 1024])
    d_wfi = dram_in("w_ffn_in", [2, 1024, 5632])
    d_wfo = dram_in("w_ffn_out", [2, 2816, 1024])
    d_y = nc.dram_tensor("yT", [1024, 1024], F32, kind="ExternalOutput").ap()
    d_st = nc.dram_tensor("st", [2, 2, 4, 4, 64, 64], F32, kind="ExternalOutput").ap()
    dbg_out = {}
    if debug:
        for name, shape in debug.items():
            dbg_out[name] = nc.dram_tensor("dbg_" + name, list(shape), F32, kind="ExternalOutput").ap()

    with contextlib.ExitStack() as st:
        P = Prog(nc, st)
        T = lambda name, shape, dt: st.enter_context(nc.sbuf_tensor(name, shape, dt))
        xT = T("xT_sb", [128, 8, 1024], F32)
        hT = T("hT", [128, 8, 1024], BF16)
        yT = T("yTm", [128, 8, 1024], BF16)
        wring = T("wring", [128, NSLOT, SLOTW], BF16)
        tabs = T("tabs_sb", [128, 2048], F32)
        arena_t = T("arena", [128, ARENA_WORDS], F32)
        sp = T("sp32_sb", [128, NS], F32)
        sb = T("spbf_sb", [128, NB], BF16)
        dmaskT = T("dmaskT", [128, 2, 4, 128], F32)
        qdec = T("qdec", [128, 4, 128], F32)
        kdec = T("kdec", [128, 8], F32)
        sdec = T("sdec", [128, 4], F32)
        lgcol = T("lgcol", [128, 8], F32)
        lgb = T("lgb", [128, 16], F32)
        lg128 = T("lg128", [128, 8], F32)
        mod_l = [T("mod%d" % i, [128, 48], F32) for i in range(2)]
        modA_l = [T("modA%d" % i, [128, 16], F32) for i in range(2)]
        silu_c = T("silu_c", [128, 8], BF16)
        ones_bf = T("ones_bf", [128, 128], BF16)
        ones256 = T("ones256", [128, 128], BF16)
        hbd = T("hbd", [128, 128], BF16)
        onef = T("onef", [1, 8], F32)
        rowtmp = T("rowtmp", [1, 2, 512], F32)
        rs_n = T("rs_n", [128, 2, 512], F32)
        sqb = T("sqb", [128, 4, 512], BF16)
        ntmp = T("ntmp", [128, 4, 512], F32)
        dummy = T("dummy_t", [128, 8], F32)
        ps = [st.enter_context(nc.psum_tensor("ps%d" % i, [128, 512], F32)) for i in range(8)]
        blk = st.enter_context(nc.Block())

        arena = Arena(arena_t, ARENA_WORDS)
        subc = {'n': 0}

        def sub():
            subc['n'] += 1
            if stop_sub is not None and subc['n'] >= stop_sub:
                raise _Stop()
        invc = tabs[:, 0:2048].rearrange("p (a b) -> p a b", a=2)

        def SP(name, *idx):
            o, n = SP32_OFF[name]
            return sp[:, o:o + n]

        def SB(name):
            o, n = SPBF_OFF[name]
            return sb[:, o:o + n]

        rot = {'big': 0, 'small': 0}

        def ps_big():
            i = rot['big']
            rot['big'] = (i + 1) % 5
            return ps[i], 'ps%d' % i

        def ps_small():
            i = 5 + rot['small']
            rot['small'] = (rot['small'] + 1) % 2
            return ps[i], 'ps%d' % i

        def EK():
            return 'EPOCH'

        def rop(eng, fn, reads=(), writes=()):
            return P.op(eng, fn, reads=list(reads) + [EK()], writes=writes)

        def release():
            P.op('dve', lambda e: e.memset(dummy[0:1, 0:8], 0.0), reads=[], writes=[EK(), 'dummy'])
            arena.reset()

        def dump(name, ap, reads):
            if debug and name in dbg_out:
                P.dma('sp', dbg_out[name], ap, reads=list(reads) + [EK()])

        wchunks = []
        wstate = {'issued': 0, 'slot': 0}
        wkeys = {}

        def wq_add(cid, dst_fn, parts):
            wchunks.append((cid, dst_fn, parts))

        def build_weight_queue():
            defs = {}
            for l in range(DEPTH):
                wa = d_wada[l].rearrange("(kt p) c -> p kt c", p=128)
                for cc in range(12):
                    defs[('ada', l, cc)] = (lambda s: s.rearrange("p (kt c) -> p kt c", kt=8),
                                            [(lambda v: v, wa[:, :, cc * 512:(cc + 1) * 512])])
                wi = d_win[l]
                for ti in range(2):
                    parts = []
                    for s4 in range(4):
                        c0 = 1280 + s4 * 256 + ti * 128
                        parts.append((lambda v, s4=s4: v[:, :, s4, :], wi[:, c0:c0 + 128].rearrange("(kt p) c -> p kt c", p=128)))
                    defs[('rqk', l, ti)] = (lambda s: s.rearrange("p (kt s c) -> p kt s c", kt=8, s=4), parts)
                    parts = []
                    for s2 in range(2):
                        c0 = 2304 + s2 * 256 + ti * 128
                        parts.append((lambda v, s2=s2: v[:, :, s2, :], wi[:, c0:c0 + 128].rearrange("(kt p) c -> p kt c", p=128)))
                    defs[('rvg', l, ti)] = (lambda s: s[:, 0:2048].rearrange("p (kt s c) -> p kt s c", kt=8, s=2), parts)
                defs[('conv', l)] = (lambda s: s.rearrange("p (kt c) -> p kt c", kt=8),
                                     [(lambda v: v, wi[:, 0:512].rearrange("(kt p) c -> p kt c", p=128))])
                defs[('gmlp', l)] = (lambda s: s.rearrange("p (kt c) -> p kt c", kt=8),
                                     [(lambda v: v, wi[:, 512:1024].rearrange("(kt p) c -> p kt c", p=128))])
                defs[('pool', l)] = (lambda s: s[:, 0:2048].rearrange("p (kt c) -> p kt c", kt=8),
                                     [(lambda v: v, wi[:, 1024:1280].rearrange("(kt p) c -> p kt c", p=128))])
                wo = d_wout[l].rearrange("(kt p) c -> p kt c", p=128)
                for hh in range(2):
                    defs[('wo', l, hh)] = (lambda s: s.rearrange("p (kt c) -> p kt c", kt=8),
                                           [(lambda v: v, wo[:, :, hh * 512:(hh + 1) * 512])])
                wf = d_wfi[l].rearrange("(kt p) f -> p kt f", p=128)
                for jj in range(11):
                    parts = []
                    for two in range(2):
                        c0 = two * 2816 + jj * 256
                        parts.append((lambda v, two=two: v[:, :, two, :], wf[:, :, c0:c0 + 256]))
                    defs[('ffi', l, jj)] = (lambda s: s.rearrange("p (kt two c) -> p kt two c", kt=8, two=2), parts)
                wfo = d_wfo[l].rearrange("(j p) c -> p j c", p=128)
                for ct in range(8):
                    for hf in range(2):
                        defs[('ffo', l, ct, hf)] = (lambda s: s[:, 0:2816].rearrange("p (j c) -> p j c", j=22),
                                                    [(lambda v: v, wfo[:, :, ct * 128:(ct + 1) * 128])])
            order = []
            order += [('ada', 0, cc) for cc in range(4)]
            order += [('rqk', 0, 0), ('rvg', 0, 0)] + [('ada', 0, cc) for cc in range(4, 8)] + [('ada', 1, cc) for cc in range(0, 3)]
            order += [('rqk', 0, 1), ('rvg', 0, 1)] + [('ada', 0, cc) for cc in range(8, 12)] + [('ada', 1, cc) for cc in range(3, 6)]
            order += [('pool', 0), ('conv', 0), ('gmlp', 0)] + [('ada', 1, cc) for cc in range(6, 9)]
            order += [('wo', 0, 0), ('wo', 0, 1)] + [('ada', 1, cc) for cc in range(9, 12)]
            order += [('ffi', 0, jj) for jj in range(11)]
            order += [('ffo', 0, ct, hf) for hf in range(2) for ct in range(8)]
            order += [('rqk', 1, 0), ('rvg', 1, 0), ('rqk', 1, 1), ('rvg', 1, 1), ('pool', 1), ('conv', 1), ('gmlp', 1),
                      ('wo', 1, 0), ('wo', 1, 1)]
            order += [('ffi', 1, jj) for jj in range(11)] + [('ffo', 1, ct, hf) for hf in range(2) for ct in range(8)]
            assert len(order) == len(defs)
            for cid in order:
                wq_add(cid, defs[cid][0], defs[cid][1])

        def w_issue():
            i = wstate['issued']
            if i >= len(wchunks):
                return
            cid, dst_fn, parts = wchunks[i]
            slot = i % NSLOT
            dst = dst_fn(wring[:, slot, :])
            for (sel, src) in parts:
                P.dma('pool', sel(dst), src, writes=['wslot%d' % slot])
            wkeys[cid] = (slot, dst)
            wstate['issued'] = i + 1

        wnext = {'i': 0}

        def w_get(cid, ahead=0):
            i = wnext['i'] + ahead
            assert wchunks[i][0] == cid, (wchunks[i][0], cid)
            while wstate['issued'] <= i:
                w_issue()
            slot, dst = wkeys[cid]
            return dst, 'wslot%d' % slot

        def w_done():
            wnext['i'] += 1
            w_issue()
            while wstate['issued'] < min(len(wchunks), wnext['i'] + NSLOT):
                w_issue()

        build_weight_queue()

        xTd = d_x.rearrange("(kt p) t -> p kt t", p=128)
        P.dma('sp', sp[:], d_sp, writes=['sp'])
        for kt in range(8):
            P.dma('sp', xT[:, kt, :], xTd[:, kt, :], writes=['xh%d_0' % kt, 'xh%d_1' % kt])
        for _ in range(NSLOT):
            w_issue()
        P.dma('pool', sb[:], d_bf, writes=['sb'])
        P.op('dve', lambda e: e.memset(ones_bf[:], 1.0 / 1024.0), writes=['ones_bf'])
        P.op('dve', lambda e: e.memset(ones256[:], 1.0 / 256.0), writes=['ones256'])
        P.op('dve', lambda e: e.memset(hbd[:], 0.0), writes=['hbd'])
        P.op('dve', lambda e: e.memset(hbd[0:64, 0:64], 1.0 / 64.0), reads=['hbd'], writes=['hbd'])
        P.op('dve', lambda e: e.memset(hbd[64:128, 64:128], 1.0 / 64.0), reads=['hbd'], writes=['hbd'])
        P.op('dve', lambda e: e.memset(onef[:], 1.0), writes=['onef'])
        P.op('act', lambda e: e.activation(out=silu_c[:], in_=SP('cvec'), func=AF.Silu), reads=['sp'], writes=['silu_c'])
        P.op('act', lambda e: e.activation(out=lgcol[:], in_=SP('rdcol'), func=AF.Exp, scale=-1.0), reads=['sp'], writes=['lgcol'])
        P.op('act', lambda e: e.activation(out=lgb[:], in_=SP('rdb'), func=AF.Exp, scale=-1.0), reads=['sp'], writes=['lgb'])
        P.op('act', lambda e: e.activation(out=lgcol[:], in_=lgcol[:], func=AF.Ln, bias=1.0), reads=['lgcol'], writes=['lgcol'])
        P.op('act', lambda e: e.activation(out=lgb[:], in_=lgb[:], func=AF.Ln, bias=1.0), reads=['lgb'], writes=['lgb'])
        P.op('dve', lambda e: e.tensor_scalar(out=lgcol[:], in0=lgcol[:], scalar1=-1.0, scalar2=None, op0=ALU.mult),
             reads=['lgcol'], writes=['lgcol'])
        P.op('dve', lambda e: e.tensor_scalar(out=lgb[:], in0=lgb[:], scalar1=-1.0, scalar2=None, op0=ALU.mult),
             reads=['lgb'], writes=['lgb'])
        P.op('dve', lambda e: e.tensor_scalar(out=lg128[:], in0=lgcol[:], scalar1=128.0, scalar2=None, op0=ALU.mult),
             reads=['lgcol'], writes=['lg128'])

        LN_KS = math.log(0.125)
        flag = SP('flag')

        def rmsnorm_stats(half):
            hs = slice(half * 512, (half + 1) * 512)
            pst, pk = ps_small()
            for kt in range(8):
                r = kt % 4
                P.op('act', lambda e, kt=kt, r=r: e.activation(out=sqb[:, r, :], in_=xT[:, kt, hs], func=AF.Square),
                     reads=['xh%d_%d' % (kt, half)], writes=['sqb%d' % r])
                P.op('pe', lambda e, kt=kt, r=r: e.matmul(pst[:], lhsT=ones_bf[:], rhs=sqb[:, r, :], start=(kt == 0), stop=(kt == 7)),
                     reads=['sqb%d' % r, 'ones_bf'], writes=[pk])
            P.op('act', lambda e: e.activation(out=rs_n[:, half, :], in_=pst[:], func=AF.Ln, bias=EPS, scale=1.0),
                 reads=[pk], writes=['rs_n%d' % half])
            P.op('act', lambda e: e.activation(out=rs_n[:, half, :], in_=rs_n[:, half, :], func=AF.Exp, scale=-0.5),
                 reads=['rs_n%d' % half], writes=['rs_n%d' % half])

        def norm_mod(l, which, skip_stats=False):
            Acol = modA_l[l][:, 8 * which:8 * which + 8]
            Bcol = mod_l[l][:, 24 * which:24 * which + 8]
            akey = 'modA%d_%d' % (l, which)
            bkey = 'modp%d_%d' % (l, 3 * which)
            if not skip_stats:
                rmsnorm_stats(0)
                rmsnorm_stats(1)
            cnt = 0
            for half in range(2):
                hs = slice(half * 512, (half + 1) * 512)
                for kt in range(8):
                    r = cnt % 4
                    cnt += 1
                    P.op('dve', lambda e, kt=kt, r=r, hs=hs, half=half: e.scalar_tensor_tensor(
                        out=ntmp[:, r, :], in0=xT[:, kt, hs], scalar=Acol[:, kt:kt + 1], in1=rs_n[:, half, :],
                        op0=ALU.mult, op1=ALU.mult), reads=['xh%d_%d' % (kt, half), 'rs_n%d' % half, akey], writes=['ntmp%d' % r])
                    P.op('act', lambda e, kt=kt, r=r, hs=hs: e.activation(
                        out=hT[:, kt, hs], in_=ntmp[:, r, :], func=AF.Identity, bias=Bcol[:, kt:kt + 1], scale=1.0),
                        reads=['ntmp%d' % r, bkey], writes=['hT%d_%d' % (kt, half)])

        hT_keys = lambda half: ['hT%d_%d' % (kt, half) for kt in range(8)]
        hT_all = hT_keys(0) + hT_keys(1)

        def mod_chunk(l, cc):
            mod_a(l, cc)
            mod_b(l, cc)

        def mod_group(l, ccs):
            prev = None
            for cc in ccs:
                mod_a(l, cc)
                if prev is not None:
                    mod_b(l, prev)
                prev = cc
            mod_b(l, prev)

        def mod_a(l, cc):
            wv, wk = w_get(('ada', l, cc))
            prow, prk = ps_small()

            def mm(e, wv=wv, prow=prow):
                for kt in range(8):
                    ins = e.matmul(prow[0:1, :], lhsT=silu_c[:, kt:kt + 1], rhs=wv[:, kt, :],
                                   start=(kt == 0), stop=(kt == 7))
                return ins
            P.op('pe', mm, reads=[wk, 'silu_c'], writes=[prk])
            w_done()
            r = cc % 2
            P.op('act', lambda e, prow=prow, r=r: e.activation(out=rowtmp[0:1, r, :], in_=prow[0:1, :], func=AF.Copy),
                 reads=[prk], writes=['rowtmp%d' % r])

        def mod_b(l, cc):
            r = cc % 2
            pc, pck = ps_small()

            def mmT(e, pc=pc, r=r):
                for j in range(4):
                    ins = e.matmul(pc[:, j:j + 1], lhsT=rowtmp[0:1, r, j * 128:(j + 1) * 128],
                                   rhs=onef[0:1, 0:1], start=True, stop=True)
                return ins
            P.op('pe', mmT, reads=['rowtmp%d' % r, 'onef'], writes=[pck])
            bo = SP32_OFF['b_fm'][0] + 48 * l + cc * 4
            mk = 'modp%d_%d' % (l, cc // 2)
            P.op('dve', lambda e, pc=pc, bo=bo: e.tensor_tensor(out=mod_l[l][:, cc * 4:cc * 4 + 4], in0=pc[:, 0:4], in1=sp[:, bo:bo + 4],
                                                                op=ALU.add), reads=[pck, 'sp', mk], writes=[mk])
            if cc == 3:
                g1o = SP32_OFF['g1'][0] + 8 * l
                P.op('dve', lambda e: e.scalar_tensor_tensor(out=modA_l[l][:, 0:8], in0=mod_l[l][:, 8:16], scalar=1.0,
                                                             in1=sp[:, g1o:g1o + 8], op0=ALU.add, op1=ALU.mult),
                     reads=['modp%d_1' % l, 'sp'], writes=['modA%d_0' % l])
            if cc == 9:
                g2o = SP32_OFF['g2'][0] + 8 * l
                P.op('dve', lambda e: e.scalar_tensor_tensor(out=modA_l[l][:, 8:16], in0=mod_l[l][:, 32:40], scalar=1.0,
                                                             in1=sp[:, g2o:g2o + 8], op0=ALU.add, op1=ALU.mult),
                     reads=['modp%d_4' % l, 'sp'], writes=['modA%d_1' % l])

        def decay_tables(l):
            do = SP32_OFF['diffT'][0]
            to = SP32_OFF['triT'][0]
            rq = SP32_OFF['ramp_q'][0]
            rk = SP32_OFF['rampcol_k'][0]
            for ti in range(2):
                for d in range(2):
                    for hl in range(2):
                        h = 2 * ti + hl
                        col = l * 8 + d * 4 + h
                        P.op('act', lambda e, ti=ti, d=d, hl=hl, col=col: e.activation(
                            out=dmaskT[:, ti, d * 2 + hl, :], in_=sp[:, do + d * 128:do + (d + 1) * 128], func=AF.Exp,
                            bias=LN_KS, scale=lgb[:, col:col + 1]), reads=['sp', 'lgb'], writes=['dmaskT'])
                        P.op('dve', lambda e, ti=ti, d=d, hl=hl: e.tensor_tensor(
                            out=dmaskT[:, ti, d * 2 + hl, :], in0=dmaskT[:, ti, d * 2 + hl, :],
                            in1=sp[:, to + d * 128:to + (d + 1) * 128], op=ALU.mult), reads=['dmaskT', 'sp'], writes=['dmaskT'])
            for d in range(2):
                for ti in range(2):
                    col = l * 4 + d * 2 + ti
                    P.op('act', lambda e, d=d, ti=ti, col=col: e.activation(
                        out=qdec[:, d * 2 + ti, :], in_=sp[:, rq + d * 128:rq + (d + 1) * 128], func=AF.Exp,
                        scale=lgcol[:, col:col + 1]), reads=['sp', 'lgcol'], writes=['qdec'])
                P.op('act', lambda e, d=d: e.activation(
                    out=kdec[:, d * 4:(d + 1) * 4], in_=lgb[:, l * 8 + d * 4:l * 8 + d * 4 + 4], func=AF.Exp,
                    bias=LN_KS, scale=sp[:, rk + d:rk + d + 1]), reads=['sp', 'lgb'], writes=['kdec'])
            P.op('act', lambda e: e.activation(out=sdec[:], in_=lg128[:, l * 4:(l + 1) * 4], func=AF.Exp),
                 reads=['lg128'], writes=['sdec'])

        fine = {'n': 0}

        def fm_group(wv_kt_fn, half, pst, pk, wk):
            hs = slice(half * 512, (half + 1) * 512)
            if fine['n'] > 0:
                fine['n'] -= 1
                tok = None
                for kt in range(8):
                    tok = P.op('pe', lambda e, kt=kt: e.matmul(pst[:], lhsT=wv_kt_fn(kt), rhs=hT[:, kt, hs], start=(kt == 0), stop=(kt == 7)),
                               reads=[wk, 'hT%d_%d' % (kt, half)], writes=[pk])
                return tok

            def mm(e):
                for kt in range(8):
                    ins = e.matmul(pst[:], lhsT=wv_kt_fn(kt), rhs=hT[:, kt, hs], start=(kt == 0), stop=(kt == 7))
                return ins
            return P.op('pe', mm, reads=[wk] + hT_keys(half), writes=[pk])

        def mixer_retention(l, ti, filler=None):
            A = arena
            qk = A.alloc(BF16, [128, 4, 1024])
            rawc = A.alloc(BF16, [128, 2, 512])
            raws = A.alloc(BF16, [128, 2, 512])
            qd = A.alloc(BF16, [128, 2, 1024])
            vr = A.alloc(BF16, [128, 8, 128])
            kd = A.alloc(BF16, [128, 2, 8, 128])
            sg = A.alloc(BF16, [128, 1024])
            am = A.alloc(BF16, [128, 2, 512])
            Sb = A.alloc(BF16, [128, 2, 8, 128])
            o_sb = A.alloc(F32, [128, 2, 512])
            ob = A.alloc(BF16, [128, 2, 512])
            rso = A.alloc(F32, [128, 2, 512])
            stage = A.alloc(F32, [128, 2, 4, 128])
            Stmp = None
            Scont = A.alloc(F32, [128, 2, 2, 128])
            scur = {0: 0, 1: 0}
            cs_t = A.alloc(F32, [128, 2, 1024])
            kz = A.alloc(BF16, [128, 2, 2, 1024])
            if ti == 0:
                rop('dve', lambda e: e.memset(kz[:], 0.0), writes=['kz'])
                rop('pool', lambda e: e.memset(Sb[:], 0.0), writes=['Sb0', 'Sb1'])
                P.dma('sp', cs_t, d_tabs[:, 0:2048].rearrange("p (a b) -> p a b", a=2), reads=[EK()], writes=['cossin'])
            cosT = cs_t[:, 0, :]
            sinP = cs_t[:, 1, :]
            ident = SB('ident')
            Pm = SB('Pm')

            wv, wk = w_get(('rqk', l, ti))
            wv2, wk2 = w_get(('rvg', l, ti), ahead=1)
            vg_items = []

            def v_item(tp):
                pst, pk = ps_big()

                def mmv(e):
                    for u2 in range(2):
                        tt = tp * 2 + u2
                        for kt in range(8):
                            ins = e.matmul(pst[:, u2 * 128:(u2 + 1) * 128], lhsT=hT[:, kt, tt * 128:(tt + 1) * 128],
                                           rhs=wv2[:, kt, 0, :], start=(kt == 0), stop=(kt == 7))
                    return ins
                P.op('pe', mmv, reads=[wk2] + hT_all, writes=[pk])
                rop('act', lambda e: e.activation(
                    out=vr[:, tp * 2:tp * 2 + 2, :], in_=pst[:, 0:256].rearrange("p (a b) -> p a b", a=2), func=AF.Copy),
                    reads=[pk], writes=['vr'])

            def g_item(half):
                hs = slice(half * 512, (half + 1) * 512)
                pst, pk = ps_big()
                fm_group(lambda kt: wv2[:, kt, 1, :], half, pst, pk, wk2)
                rop('act', lambda e: e.activation(out=sg[:, hs], in_=pst[:], func=AF.Silu),
                    reads=[pk], writes=['sg%d' % half])
            for tp in range(4):
                vg_items.append(lambda tp=tp: v_item(tp))
            for half in range(2):
                vg_items.append(lambda half=half: g_item(half))
            def rope_tail(u, s4, half):
                rb = u % 2
                hs = slice(half * 512, (half + 1) * 512)
                ps2, pk2 = ps_small()

                def mmr(e, ps2=ps2, rb=rb):
                    e.matmul(ps2[:], lhsT=ident, rhs=rawc[:, rb, :], start=True, stop=False)
                    return e.matmul(ps2[:], lhsT=Pm, rhs=raws[:, rb, :], start=False, stop=True)
                rop('pe', mmr, reads=['rawc%d' % rb, 'raws%d' % rb, 'sb'], writes=[pk2])
                rop('act', lambda e, ps2=ps2, s4=s4, hs=hs: e.activation(out=qk[:, s4, hs], in_=ps2[:], func=AF.Copy),
                    reads=[pk2], writes=['qk%d_%d' % (s4, half)])
                if s4 % 2 == 1:
                    dd_ = s4 // 2
                    for hl in range(2):
                        r_ = slice(hl * 64, (hl + 1) * 64)
                        rop('act', lambda e, ps2=ps2, dd_=dd_, hl=hl, r_=r_, hs=hs: e.activation(
                            out=kz[r_, dd_, hl, hs], in_=ps2[r_, :], func=AF.Copy), reads=[pk2, 'kz'], writes=['kz'])

            pend = None
            u = 0
            for s4 in range(4):
                for half in range(2):
                    hs = slice(half * 512, (half + 1) * 512)
                    rb = u % 2
                    pst, pk = ps_big()
                    fm_group(lambda kt, s4=s4, wv=wv: wv[:, kt, s4, :], half, pst, pk, wk)
                    rop('dve', lambda e, pst=pst, hs=hs, rb=rb: e.tensor_tensor(out=rawc[:, rb, :], in0=pst[:], in1=cosT[:, hs], op=ALU.mult),
                        reads=[pk, 'cossin'], writes=['rawc%d' % rb])
                    rop('dve', lambda e, pst=pst, hs=hs, rb=rb: e.tensor_tensor(out=raws[:, rb, :], in0=pst[:], in1=sinP[:, hs], op=ALU.mult),
                        reads=[pk, 'cossin'], writes=['raws%d' % rb])
                    if pend is not None:
                        rope_tail(*pend)
                        if vg_items:
                            vg_items.pop(0)()
                    pend = (u, s4, half)
                    u += 1
            rope_tail(*pend)
            while vg_items:
                vg_items.pop(0)()
            w_done()
            sub()
            w_done()
            sub()
            qkk = lambda s4: ['qk%d_0' % s4, 'qk%d_1' % s4]
            for d in range(2):
                rop('pool', lambda e, d=d: e.tensor_tensor(
                    out=qd[:, d, :].rearrange("p (c i) -> p c i", c=8),
                    in0=qk[:, 2 * d, :].rearrange("p (c i) -> p c i", c=8),
                    in1=qdec[:, d * 2 + ti, :].unsqueeze(1).to_broadcast([128, 8, 128]), op=ALU.mult),
                    reads=qkk(2 * d) + ['qdec'], writes=['qd%d' % d])
            sub()
            pT = ps[7].bitcast(BF16)
            rnd = 0
            for d in range(2):
                for hc in range(2):
                    hb = rnd % 2
                    rnd += 1
                    if hb == 0:
                        pTh = pT[:, 0:512]
                        pkey = 'ps7'
                    else:
                        pss_, pkey = ps_small()
                        pTh = pss_.bitcast(BF16)[:, 0:512]

                    def tr(e, d=d, hc=hc, pTh=pTh):
                        for c4 in range(4):
                            c = hc * 4 + c4
                            ins = e.transpose(out=pTh[:, c4 * 128:(c4 + 1) * 128], in_=qk[:, 2 * d + 1, c * 128:(c + 1) * 128],
                                              identity=ident)
                        return ins
                    rop('pe', tr, reads=qkk(2 * d + 1) + ['sb'], writes=[pkey])
                    if hb == 0:
                        rop('dve', lambda e, d=d, hc=hc, pTh=pTh: e.tensor_tensor(
                            out=kd[:, d, hc * 4:(hc + 1) * 4, :].rearrange("p c (h x) -> p c h x", h=2),
                            in0=pTh.rearrange("p (c h x) -> p c h x", c=4, h=2),
                            in1=kdec[:, d * 4 + 2 * ti:d * 4 + 2 * ti + 2].unsqueeze(1).unsqueeze(3).to_broadcast([128, 4, 2, 64]),
                            op=ALU.mult), reads=[pkey, 'kdec'], writes=['kd%d' % d])
                    else:
                        for h in range(2):
                            rop('act', lambda e, d=d, hc=hc, pTh=pTh, h=h: e.activation(
                                out=kd[:, d, hc * 4:(hc + 1) * 4, h * 64:(h + 1) * 64],
                                in_=pTh.rearrange("p (c h x) -> p c h x", c=4, h=2)[:, :, h, :], func=AF.Copy,
                                scale=kdec[:, d * 4 + 2 * ti + h:d * 4 + 2 * ti + h + 1]),
                                reads=[pkey, 'kdec'], writes=['kd%d' % d])
            sub()
            bdm = SP('bdmask')
            s0v = lambda d: d_s0[:, l * 4 + d * 2 + ti, :]
            for d in range(2):
                P.dma('sp', Scont[:, 0, d, :], s0v(d), reads=[EK()], writes=['Sc%d_0' % d])
            pU = {}
            for d in range(2):
                for hc in range(2):
                    pst, pk = ps_big()

                    def mmu(e, d=d, hc=hc, pst=pst):
                        for c4 in range(4):
                            c = hc * 4 + c4
                            ins = e.matmul(pst[:, c4 * 128:(c4 + 1) * 128], lhsT=kd[:, d, c, :], rhs=vr[:, c, :],
                                           start=True, stop=True)
                        return ins
                    rop('pe', mmu, reads=['kd%d' % d, 'vr'], writes=[pk])
                    pU[(d, hc)] = (pst, pk)
            if filler is not None:
                filler()
            psO_h = [(ps[7], 'ps7'), None]

            def att_a(c):
                cs = slice(c * 128, (c + 1) * 128)
                psA, pkA = ps_small()

                def mma(e, psA=psA, cs=cs):
                    for d in range(2):
                        for hl in range(2):
                            ins = e.matmul(psA[:, (d * 2 + hl) * 128:(d * 2 + hl + 1) * 128], lhsT=kz[:, d, hl, cs],
                                           rhs=qk[:, 2 * d, cs], start=True, stop=True)
                    return ins
                rop('pe', mma, reads=qkk(0) + qkk(2) + ['kz'], writes=[pkA])
                a_ = c % 2
                rop('dve', lambda e, psA=psA, a_=a_: e.tensor_tensor(
                    out=am[:, a_, :], in0=psA[:], in1=dmaskT[:, ti, :, :].rearrange("p a b -> p (a b)"), op=ALU.mult),
                    reads=[pkA, 'dmaskT'], writes=['am%d' % a_])

            def att_intra(c):
                half, c4 = c // 4, c % 4
                if psO_h[half] is None:
                    psO_h[half] = ps_big()
                psO, pkO = psO_h[half]
                a_ = c % 2

                def mmi(e, psO=psO, c=c, c4=c4, a_=a_):
                    oc = slice(c4 * 128, (c4 + 1) * 128)
                    for hl in range(2):
                        for d in range(2):
                            ins = e.matmul(psO[hl * 64:(hl + 1) * 64, oc], lhsT=vr[:, c, hl * 64:(hl + 1) * 64],
                                           rhs=am[:, a_, (d * 2 + hl) * 128:(d * 2 + hl + 1) * 128],
                                           start=(c4 == 0 and d == 0), stop=False, tile_position=(0, hl * 64),
                                           skip_group_check=True)
                    return ins
                rop('pe', mmi, reads=['am%d' % a_, 'vr'], writes=[pkO])

            def att_inter(c):
                half, c4 = c // 4, c % 4
                psO, pkO = psO_h[half]
                cs = slice(c * 128, (c + 1) * 128)

                def mmx(e, psO=psO, c=c, c4=c4, cs=cs):
                    oc = slice(c4 * 128, (c4 + 1) * 128)
                    e.matmul(psO[:, oc], lhsT=Sb[:, 0, c, :], rhs=qd[:, 0, cs], start=False, stop=False, skip_group_check=True)
                    return e.matmul(psO[:, oc], lhsT=Sb[:, 1, c, :], rhs=qd[:, 1, cs], start=False, stop=False,
                                    skip_group_check=True)
                rop('pe', mmx, reads=['Sb0', 'Sb1', 'qd0', 'qd1'], writes=[pkO])

            for i in range(8):
                for d in range(2):
                    c = i if d == 0 else 7 - i
                    pst, pk = pU[(d, c // 4)]
                    recur(l, ti, d, [c], pst, pk, Sb, stage, Stmp, Scont, bdm, scur)
                att_a(i)
                if i >= 1:
                    att_intra(i - 1)
            att_intra(7)
            sub()
            for c in range(8):
                att_inter(c)
            sub()
            steps = []
            for half in range(2):
                hs = slice(half * 512, (half + 1) * 512)
                psO, pkO = psO_h[half]
                o_h = o_sb[:, half, :]
                ob_h = ob[:, half, :]
                rs_h = rso[:, half, :]
                ko, kb, kr = 'o_sb%d' % half, 'ob%d' % half, 'rso%d' % half
                st_ = []
                st_.append(lambda psO=psO, pkO=pkO, ob_h=ob_h, kb=kb: rop(
                    'act', lambda e: e.activation(out=ob_h, in_=psO[:], func=AF.Copy), reads=[pkO], writes=[kb]))
                st_.append(lambda psO=psO, pkO=pkO, o_h=o_h, ko=ko: rop(
                    'act', lambda e: e.activation(out=o_h, in_=psO[:], func=AF.Copy), reads=[pkO], writes=[ko]))
                psM_box = {}

                def s_mean(ob_h=ob_h, kb=kb, box=psM_box):
                    psM, pkM = ps_small()
                    box['m'] = (psM, pkM)
                    rop('pe', lambda e: e.matmul(psM[:], lhsT=hbd[:], rhs=ob_h, start=True, stop=True), reads=[kb, 'hbd'], writes=[pkM])
                st_.append(s_mean)

                def s_cen(o_h=o_h, ko=ko, box=psM_box):
                    psM, pkM = box['m']
                    rop('dve', lambda e: e.tensor_tensor(out=o_h, in0=o_h, in1=psM[:], op=ALU.subtract), reads=[ko, pkM], writes=[ko])
                st_.append(s_cen)
                st_.append(lambda o_h=o_h, ko=ko, ob_h=ob_h, kb=kb: rop(
                    'act', lambda e: e.activation(out=ob_h, in_=o_h, func=AF.Square), reads=[ko], writes=[kb]))

                def s_var(ob_h=ob_h, kb=kb, box=psM_box):
                    psV, pkV = ps_small()
                    box['v'] = (psV, pkV)
                    rop('pe', lambda e: e.matmul(psV[:], lhsT=hbd[:], rhs=ob_h, start=True, stop=True), reads=[kb, 'hbd'], writes=[pkV])
                st_.append(s_var)

                def s_ln(rs_h=rs_h, kr=kr, box=psM_box):
                    psV, pkV = box['v']
                    rop('act', lambda e: e.activation(out=rs_h, in_=psV[:], func=AF.Ln, bias=EPS, scale=1.0), reads=[pkV], writes=[kr])
                st_.append(s_ln)
                st_.append(lambda rs_h=rs_h, kr=kr: rop(
                    'act', lambda e: e.activation(out=rs_h, in_=rs_h, func=AF.Exp, scale=-0.5), reads=[kr], writes=[kr]))
                st_.append(lambda o_h=o_h, ko=ko, rs_h=rs_h, kr=kr: rop(
                    'dve', lambda e: e.tensor_tensor(out=o_h, in0=o_h, in1=rs_h, op=ALU.mult), reads=[ko, kr], writes=[ko]))
                st_.append(lambda o_h=o_h, ko=ko, hs=hs, half=half: rop(
                    'dve', lambda e: e.tensor_tensor(out=yT[:, 6 + ti, hs], in0=o_h, in1=sg[:, hs], op=ALU.mult),
                    reads=[ko, 'sg%d' % half], writes=['y%d_%d' % (6 + ti, half)]))
                steps.append(st_)
            for i in range(len(steps[0])):
                steps[0][i]()
                steps[1][i]()

        def recur(l, ti, d, chunks, pst, pk, Sb, stage, Stmp, Scont, bdm, scur):
            sd = sdec[:, d * 2 + ti:d * 2 + ti + 1]
            for c in chunks:
                c4 = c % 4
                cur = scur[d]
                nxt = 1 - cur
                kcur = 'Sc%d_%d' % (d, cur)
                knxt = 'Sc%d_%d' % (d, nxt)
                Sc = Scont[:, cur, d, :]
                Sn = Scont[:, nxt, d, :]
                for hl in range(2):
                    r = slice(hl * 64, (hl + 1) * 64)
                    rop('act', lambda e, c=c, r=r, Sc=Sc: e.activation(out=Sb[r, d, c, r], in_=Sc[r, r], func=AF.Copy),
                        reads=[kcur], writes=['Sb%d' % d])
                seg_end = (c % 2 == 1) if d == 0 else (c % 2 == 0)
                last = (c == 7) if d == 0 else (c == 0)
                seg = c // 2
                if seg_end:
                    dst = stage[:, d, seg, :]
                    dkey = 'stage%d_%d' % (d, seg)
                    rop('dve', lambda e, c4=c4, dst=dst, Sc=Sc: e.scalar_tensor_tensor(
                        out=dst, in0=Sc, scalar=sd, in1=pst[:, c4 * 128:(c4 + 1) * 128], op0=ALU.mult, op1=ALU.add),
                        reads=[kcur, pk, 'sdec'], writes=[dkey])
                    for hl in range(2):
                        r = slice(hl * 64, (hl + 1) * 64)
                        P.dma('sp', d_st[l, d, seg, 2 * ti + hl], stage[r, d, seg, hl * 64:(hl + 1) * 64], reads=[dkey, EK()])
                    if not last:
                        rop('dve', lambda e, dst=dst, Sn=Sn: e.tensor_scalar(out=Sn, in0=dst, scalar1=flag, scalar2=None, op0=ALU.mult),
                            reads=[dkey, 'sp'], writes=[knxt])
                        scur[d] = nxt
                else:
                    rop('dve', lambda e, c4=c4, Sc=Sc, Sn=Sn: e.scalar_tensor_tensor(
                        out=Sn, in0=Sc, scalar=sd, in1=pst[:, c4 * 128:(c4 + 1) * 128], op0=ALU.mult, op1=ALU.add),
                        reads=[kcur, pk, 'sdec'], writes=[knxt])
                    scur[d] = nxt

        def mixer_conv_gmlp_pool(l):
            A = arena
            hcp = A.alloc(BF16, [128, 2, 4, 286])
            sigt = A.alloc(F32, [128, 2, 512])
            acc = A.alloc(F32, [128, 2, 1024])
            accb = A.alloc(BF16, [128, 2, 512])
            sqc = A.alloc(BF16, [128, 2, 512])
            rsc = A.alloc(F32, [128, 512])
            hsb = A.alloc(BF16, [128, 2, 512])
            dg = A.alloc(BF16, [128, 2, 8, 128])
            u_sb = A.alloc(F32, [128, 2, 1024])
            vg = A.alloc(BF16, [128, 8, 256])
            gtmp = sigt
            pp = A.alloc(F32, [128, 2, 4, 272])
            pA = A.alloc(F32, [128, 4, 272])
            pB = A.alloc(F32, [128, 4, 272])
            dpb = A.alloc(BF16, [128, 2, 1024])

            if l == 0:
                P.dma('sp', tabs[:], d_tabs[:, 2048:4096], writes=['tabs'])

            def pool_part1():
                rop('dve', lambda e: e.memset(pp[:], 0.0), writes=['pp'])
                wv, wk = w_get(('pool', l))
                for ti in range(2):
                    for half in range(2):
                        pst, pk = ps_big()
                        fm_group(lambda kt, ti=ti, wv=wv: wv[:, kt, ti * 128:(ti + 1) * 128], half, pst, pk, wk)
                        rop('act', lambda e, pst=pst, ti=ti, half=half: e.activation(
                            out=pp[:, ti, 2 * half:2 * half + 2, 8:264], in_=pst[:].rearrange("p (s x) -> p s x", s=2), func=AF.Copy),
                            reads=[pk, 'pp'], writes=['pp'])
                w_done()

            def pool_chain():
                for ti in range(2):
                    rop('pool', lambda e, ti=ti: e.tensor_scalar(out=pp[:, ti, 1:4, 0:8], in0=pp[:, ti, 0:3, 256:264], scalar1=flag,
                                                                 scalar2=None, op0=ALU.mult), reads=['pp', 'sp'], writes=['pp'])
                    rop('pool', lambda e, ti=ti: e.tensor_scalar(out=pp[:, ti, 0:3, 264:272], in0=pp[:, ti, 1:4, 8:16], scalar1=flag,
                                                                 scalar2=None, op0=ALU.mult), reads=['pp', 'sp'], writes=['pp'])
                pbd = SB('pool_bd').rearrange("p (l t c) -> p l t c", l=2, t=2)
                pso = SP32_OFF['pool_scale'][0] + l * 2
                for ti in range(2):
                    pv = pp[:, ti]
                    rop('pool', lambda e, pv=pv: e.tensor_tensor(out=pA[:, :, 1:272], in0=pv[:, :, 1:272], in1=pv[:, :, 0:271], op=ALU.add),
                        reads=['pp', 'pA'], writes=['pA'])
                    rop('pool', lambda e: e.tensor_tensor(out=pB[:, :, 2:271], in0=pA[:, :, 1:270], in1=pA[:, :, 3:272], op=ALU.add),
                        reads=['pA', 'pB'], writes=['pB'])
                    if ti == 0:
                        lo_src, hi_src = pA, pB
                    else:
                        rop('pool', lambda e: e.tensor_tensor(out=pA[:, :, 4:269], in0=pB[:, :, 2:267], in1=pB[:, :, 6:271], op=ALU.add),
                            reads=['pB', 'pA'], writes=['pA'])
                        rop('pool', lambda e: e.tensor_tensor(out=pB[:, :, 8:264], in0=pA[:, :, 4:260], in1=pA[:, :, 12:268], op=ALU.add),
                            reads=['pA', 'pB'], writes=['pB'])
                        lo_src, hi_src = pA, pB
                    for (r, src) in ((slice(0, 64), lo_src), (slice(64, 128), hi_src)):
                        rop('pool', lambda e, r=r, src=src, ti=ti: e.tensor_tensor(
                            out=src[r, :, 8:264], in0=src[r, :, 8:264], in1=invc[r, ti, :].rearrange("p (s x) -> p s x", s=4), op=ALU.mult),
                            reads=['pA', 'pB', 'tabs'], writes=['pA', 'pB'])
                        rop('pool', lambda e, r=r, src=src, ti=ti, pv=pv: e.tensor_tensor(
                            out=dpb[r, ti, :].rearrange("p (s x) -> p s x", s=4), in0=src[r, :, 8:264], in1=pv[r, :, 8:264], op=ALU.subtract),
                            reads=['pA', 'pB', 'pp'], writes=['dpb%d' % ti])

            def pool2_item(ti, half):
                pbd = SB('pool_bd').rearrange("p (l t c) -> p l t c", l=2, t=2)
                pso = SP32_OFF['pool_scale'][0] + l * 2
                hs = slice(half * 512, (half + 1) * 512)
                pst, pk = ps_small()
                rop('pe', lambda e: e.matmul(pst[:], lhsT=pbd[:, l, ti, :], rhs=dpb[:, ti, hs], start=True, stop=True),
                    reads=['dpb%d' % ti, 'sb'], writes=[pk])
                rop('act', lambda e: e.activation(
                    out=yT[:, 4 + ti, hs], in_=pst[:], func=AF.Identity, scale=sp[:, pso + ti:pso + ti + 1]),
                    reads=[pk, 'sp'], writes=['y%d_%d' % (4 + ti, half)])

            rop('dve', lambda e: e.memset(hcp[:], 0.0), writes=['hcp'])
            pool_part1()
            wv, wk = w_get(('conv', l))
            for ti in range(2):
                for half in range(2):
                    r = half
                    pst, pk = ps_big()
                    fm_group(lambda kt, ti=ti, wv=wv: wv[:, kt, (2 + ti) * 128:(3 + ti) * 128], half, pst, pk, wk)
                    rop('act', lambda e, pst=pst, r=r: e.activation(out=sigt[:, r, :], in_=pst[:], func=AF.Sigmoid),
                        reads=[pk], writes=['sigt%d' % r])
                    pst2, pk2 = ps_big()
                    fm_group(lambda kt, ti=ti, wv=wv: wv[:, kt, ti * 128:(ti + 1) * 128], half, pst2, pk2, wk)
                    rop('dve', lambda e, pst2=pst2, r=r, ti=ti, half=half: e.tensor_tensor(
                        out=hcp[:, ti, 2 * half:2 * half + 2, 15:271], in0=pst2[:].rearrange("p (s x) -> p s x", s=2),
                        in1=sigt[:, r, :].rearrange("p (s x) -> p s x", s=2), op=ALU.mult),
                        reads=[pk2, 'sigt%d' % r, 'hcp'], writes=['hcp'])
            w_done()
            for ti in range(2):
                rop('act', lambda e, ti=ti: e.activation(out=hcp[:, ti, 1:4, 0:15], in_=hcp[:, ti, 0:3, 256:271], func=AF.Copy,
                                                         scale=flag), reads=['hcp', 'sp'], writes=['hcp'])
                rop('act', lambda e, ti=ti: e.activation(out=hcp[:, ti, 0:3, 271:286], in_=hcp[:, ti, 1:4, 15:30], func=AF.Copy,
                                                         scale=flag), reads=['hcp', 'sp'], writes=['hcp'])
            pool_chain()
            dwo = SP32_OFF['conv_dw'][0]
            cbo = SP32_OFF['conv_b'][0]
            identb = SB('ident')
            psC = {}
            for ti in range(2):
                for half in range(2):
                    psC[(ti, half)] = ps_big()
            dgf = dg.rearrange("p a b c -> p (a b) c")
            dcnt = 0
            for k in range(31):
                for ti in range(2):
                    base = dwo + (l * 2 + ti) * 31
                    r = dcnt % 16
                    dcnt += 1
                    rop('act', lambda e, r=r, base=base, k=k: e.activation(
                        out=dgf[:, r, :], in_=identb, func=AF.Copy, scale=sp[:, base + k:base + k + 1]),
                        reads=['sb', 'sp'], writes=['dg%d' % r])
                    for half in range(2):
                        pst, pk = psC[(ti, half)]
                        rop('pe', lambda e, pst=pst, r=r, ti=ti, half=half, k=k: e.matmul(
                            pst[:].rearrange("p (s x) -> p s x", s=2), lhsT=dgf[:, r, :],
                            rhs=hcp[:, ti, 2 * half:2 * half + 2, k:k + 256], start=(k == 0), stop=(k == 30)),
                            reads=['dg%d' % r, 'hcp'], writes=[pk])
            for ti in range(2):
                for half in range(2):
                    hs = slice(half * 512, (half + 1) * 512)
                    pst, pk = psC[(ti, half)]
                    rop('act', lambda e, pst=pst, ti=ti, hs=hs: e.activation(
                        out=acc[:, ti, hs], in_=pst[:], func=AF.Identity, bias=sp[:, cbo + l * 2 + ti:cbo + l * 2 + ti + 1], scale=1.0),
                        reads=[pk, 'sp'], writes=['acc%d' % ti])
            wvg, wkg = w_get(('gmlp', l))
            G = []

            def gu(ti, half):
                hs = slice(half * 512, (half + 1) * 512)
                pst, pk = ps_big()
                fm_group(lambda kt: wvg[:, kt, ti * 128:(ti + 1) * 128], half, pst, pk, wkg)
                rop('act', lambda e: e.activation(out=u_sb[:, ti, hs], in_=pst[:], func=AF.Copy),
                    reads=[pk], writes=['u%d_%d' % (ti, half)])

            def gv(tp):
                pst, pk = ps_big()

                def mmv(e):
                    for u2 in range(2):
                        tt = tp * 2 + u2
                        for kt in range(8):
                            ins = e.matmul(pst[:, u2 * 256:(u2 + 1) * 256], lhsT=hT[:, kt, tt * 128:(tt + 1) * 128],
                                           rhs=wvg[:, kt, 256:512], start=(kt == 0), stop=(kt == 7))
                    return ins
                P.op('pe', mmv, reads=[wkg] + hT_all, writes=[pk])
                rop('act', lambda e: e.activation(
                    out=vg[:, tp * 2:tp * 2 + 2, :], in_=pst[:].rearrange("p (a b) -> p a b", a=2), func=AF.Copy),
                    reads=[pk], writes=['vg'])
                if tp == 3:
                    w_done()

            wsT = SB('wsT').rearrange("p (l h q) -> p l h q", l=2, h=4)
            gbo = SP32_OFF['gb_tab'][0]

            def gs(ti, half):
                hs = slice(half * 512, (half + 1) * 512)
                pst, pk = ps_small()

                def mmg(e):
                    for c4 in range(4):
                        c = half * 4 + c4
                        for hl in range(2):
                            h = 2 * ti + hl
                            ins = e.matmul(pst[hl * 64:(hl + 1) * 64, c4 * 128:(c4 + 1) * 128], lhsT=vg[:, c, h * 64:(h + 1) * 64],
                                           rhs=wsT[:, l, h, :], start=True, stop=True, tile_position=(0, hl * 64))
                    return ins
                rop('pe', mmg, reads=['vg', 'sb'], writes=[pk])
                g = half
                gb = sp[:, gbo + (l * 2 + ti) * 128:gbo + (l * 2 + ti + 1) * 128]
                rop('dve', lambda e: e.tensor_tensor(
                    out=gtmp[:, g, :].rearrange("p (c i) -> p c i", c=4), in0=pst[:].rearrange("p (c i) -> p c i", c=4),
                    in1=gb.unsqueeze(1).to_broadcast([128, 4, 128]), op=ALU.add), reads=[pk, 'sp'], writes=['sigt%d' % g])
                rop('dve', lambda e: e.tensor_tensor(out=yT[:, 2 + ti, hs], in0=gtmp[:, g, :], in1=u_sb[:, ti, hs], op=ALU.mult),
                    reads=['sigt%d' % g, 'u%d_%d' % (ti, half)], writes=['y%d_%d' % (2 + ti, half)])

            for ti in range(2):
                for half in range(2):
                    G.append(lambda ti=ti, half=half: gu(ti, half))
            for tp in range(4):
                G.append(lambda tp=tp: gv(tp))
            for ti in range(2):
                for half in range(2):
                    G.append(lambda ti=ti, half=half: gs(ti, half))
            for ti in range(2):
                for half in range(2):
                    G.append(lambda ti=ti, half=half: pool2_item(ti, half))

            lgo = SP32_OFF['ln_g'][0] + l * 2
            lbo = SP32_OFF['ln_b'][0] + l * 2
            pwv = SB('conv_pw').rearrange("p (l k c) -> p l k c", l=2, k=2)
            L = []
            for half in range(2):
                hs = slice(half * 512, (half + 1) * 512)
                box = {}

                def l1(hs=hs):
                    rop('act', lambda e: e.activation(out=accb[:], in_=acc[:, :, hs], func=AF.Copy),
                        reads=['acc0', 'acc1'], writes=['accb'])

                def l2(box=box):
                    psM, pkM = ps_small()
                    box['m'] = (psM, pkM)

                    def mmm(e):
                        e.matmul(psM[:], lhsT=ones256[:], rhs=accb[:, 0, :], start=True, stop=False)
                        return e.matmul(psM[:], lhsT=ones256[:], rhs=accb[:, 1, :], start=False, stop=True)
                    rop('pe', mmm, reads=['accb', 'ones256'], writes=[pkM])

                def l3(hs=hs, box=box):
                    psM, pkM = box['m']
                    for ti in range(2):
                        rop('dve', lambda e, ti=ti: e.tensor_tensor(out=acc[:, ti, hs], in0=acc[:, ti, hs], in1=psM[:], op=ALU.subtract),
                            reads=['acc%d' % ti, pkM], writes=['acc%d' % ti])

                def l4(hs=hs):
                    rop('act', lambda e: e.activation(out=sqc[:], in_=acc[:, :, hs], func=AF.Square),
                        reads=['acc0', 'acc1'], writes=['sqc'])

                def l5(box=box):
                    psV, pkV = ps_small()
                    box['v'] = (psV, pkV)

                    def mmv2(e):
                        e.matmul(psV[:], lhsT=ones256[:], rhs=sqc[:, 0, :], start=True, stop=False)
                        return e.matmul(psV[:], lhsT=ones256[:], rhs=sqc[:, 1, :], start=False, stop=True)
                    rop('pe', mmv2, reads=['sqc', 'ones256'], writes=[pkV])

                def l6(box=box):
                    psV, pkV = box['v']
                    rop('act', lambda e: e.activation(out=rsc[:], in_=psV[:], func=AF.Ln, bias=EPS, scale=1.0), reads=[pkV], writes=['rsc'])
                    rop('act', lambda e: e.activation(out=rsc[:], in_=rsc[:], func=AF.Exp, scale=-0.5), reads=['rsc'], writes=['rsc'])

                def l7(hs=hs):
                    for ti in range(2):
                        rop('dve', lambda e, ti=ti: e.scalar_tensor_tensor(
                            out=acc[:, ti, hs], in0=acc[:, ti, hs], scalar=sp[:, lgo + ti:lgo + ti + 1], in1=rsc[:],
                            op0=ALU.mult, op1=ALU.mult), reads=['acc%d' % ti, 'rsc', 'sp'], writes=['acc%d' % ti])
                        rop('act', lambda e, ti=ti: e.activation(
                            out=hsb[:, ti, :], in_=acc[:, ti, hs], func=AF.Silu, bias=sp[:, lbo + ti:lbo + ti + 1], scale=1.0),
                            reads=['acc%d' % ti, 'sp'], writes=['hsb%d' % ti])

                def l8(to, hs=hs, half=half):
                    pst, pk = ps_small()

                    def mmp(e):
                        e.matmul(pst[:], lhsT=pwv[:, l, 0, to * 128:(to + 1) * 128], rhs=hsb[:, 0, :], start=True, stop=False)
                        return e.matmul(pst[:], lhsT=pwv[:, l, 1, to * 128:(to + 1) * 128], rhs=hsb[:, 1, :], start=False, stop=True)
                    rop('pe', mmp, reads=['hsb0', 'hsb1', 'sb'], writes=[pk])
                    rop('act', lambda e: e.activation(out=yT[:, to, hs], in_=pst[:], func=AF.Copy),
                        reads=[pk], writes=['y%d_%d' % (to, half)])
                L += [l1, l2, l3, l4, l5, l6, l7, lambda l8=l8: l8(0), lambda l8=l8: l8(1)]
            while L or G:
                if L:
                    L.pop(0)()
                if G:
                    G.pop(0)()

        def out_proj(l):
            y_keys = lambda half: ['y%d_%d' % (t, half) for t in range(8)]
            wv0, wk0 = w_get(('wo', l, 0))
            wv1, wk1 = w_get(('wo', l, 1), ahead=1)
            for half in range(2):
                hs = slice(half * 512, (half + 1) * 512)
                for ct in range(8):
                    wv, wk = (wv0, wk0) if ct < 4 else (wv1, wk1)
                    c4 = ct % 4
                    pst, pk = ps_big()

                    def mm(e, pst=pst, wv=wv, c4=c4, hs=hs):
                        for kt in range(8):
                            ins = e.matmul(pst[:], lhsT=wv[:, kt, c4 * 128:(c4 + 1) * 128], rhs=yT[:, kt, hs],
                                           start=(kt == 0), stop=(kt == 7))
                        return ins
                    P.op('pe', mm, reads=[wk] + y_keys(half), writes=[pk])
                    P.op('dve', lambda e, pst=pst, ct=ct, hs=hs: e.scalar_tensor_tensor(
                        out=xT[:, ct, hs], in0=pst[:], scalar=mod_l[l][:, 16 + ct:17 + ct], in1=xT[:, ct, hs], op0=ALU.mult, op1=ALU.add),
                        reads=[pk, 'modp%d_2' % l, 'xh%d_%d' % (ct, half)], writes=['xh%d_%d' % (ct, half)])
            w_done()
            w_done()

        def ffn(l):
            A = arena
            act = A.alloc(BF16, [128, 22, 1024])
            sgt = A.alloc(F32, [128, 2, 512])
            for jj in range(11):
                if l == 0 and jj == 3:
                    decay_tables(1)
                wv, wk = w_get(('ffi', l, jj))
                for j2 in range(2):
                    j = jj * 2 + j2
                    for half in range(2):
                        hs = slice(half * 512, (half + 1) * 512)
                        psG, pkG = ps_big()
                        fm_group(lambda kt, j2=j2, wv=wv: wv[:, kt, 0, j2 * 128:(j2 + 1) * 128], half, psG, pkG, wk)
                        r = half
                        rop('act', lambda e, psG=psG, r=r: e.activation(out=sgt[:, r, :], in_=psG[:], func=AF.Silu),
                            reads=[pkG], writes=['sgt%d' % r])
                        psU, pkU = ps_big()
                        fm_group(lambda kt, j2=j2, wv=wv: wv[:, kt, 1, j2 * 128:(j2 + 1) * 128], half, psU, pkU, wk)
                        rop('dve', lambda e, psU=psU, r=r, j=j, hs=hs: e.tensor_tensor(out=act[:, j, hs], in0=psU[:], in1=sgt[:, r, :],
                                                                                     op=ALU.mult),
                            reads=[pkU, 'sgt%d' % r], writes=['act%d_%d' % (j, half)])
                w_done()
            for half in range(2):
                hs = slice(half * 512, (half + 1) * 512)
                for ct in range(8):
                    wv, wk = w_get(('ffo', l, ct, half))
                    pst, pk = ps_big()

                    def mm(e, pst=pst, wv=wv, hs=hs):
                        for j in range(22):
                            ins = e.matmul(pst[:], lhsT=wv[:, j, :], rhs=act[:, j, hs], start=(j == 0), stop=(j == 21))
                        return ins
                    rop('pe', mm, reads=[wk] + ['act%d_%d' % (j, half) for j in range(22)], writes=[pk])
                    rop('dve', lambda e, pst=pst, ct=ct, hs=hs: e.scalar_tensor_tensor(
                        out=xT[:, ct, hs], in0=pst[:], scalar=mod_l[l][:, 40 + ct:41 + ct], in1=xT[:, ct, hs], op0=ALU.mult, op1=ALU.add),
                        reads=[pk, 'modp%d_5' % l, 'xh%d_%d' % (ct, half)], writes=['xh%d_%d' % (ct, half)])
                    w_done()

        stage = {'n': 0}

        def chk():
            stage['n'] += 1
            if stop_after is not None and stage['n'] >= stop_after:
                raise _Stop()
        try:
            chk()
            decay_tables(0)
            rmsnorm_stats(0)
            rmsnorm_stats(1)
            mod_group(0, range(4))
            chk()
            for l in range(DEPTH):
                norm_mod(l, 0, skip_stats=(l == 0))
                fine['n'] = 4
                chk()
                release()
                mixer_retention(l, 0, filler=(lambda: mod_group(0, range(4, 8))) if l == 0 else None)
                if l == 0:
                    mod_group(1, range(0, 3))
                chk()
                release()
                mixer_retention(l, 1, filler=(lambda: mod_group(0, range(8, 12))) if l == 0 else None)
                if l == 0:
                    mod_group(1, range(3, 6))
                chk()
                release()
                mixer_conv_gmlp_pool(l)
                if l == 0:
                    mod_group(1, range(6, 9))
                chk()
                out_proj(l)
                if l == 0:
                    mod_group(1, range(9, 12))
                chk()
                norm_mod(l, 1)
                fine['n'] = 4
                release()
                ffn(l)
                chk()
            gfo = SP32_OFF['gF'][0]
            yTd = d_y.rearrange("(kt p) t -> p kt t", p=128)
            rmsnorm_stats(0)
            rmsnorm_stats(1)
            for half in range(2):
                hs = slice(half * 512, (half + 1) * 512)
                for kt in range(8):
                    r = kt % 2
                    P.op('dve', lambda e, kt=kt, r=r, hs=hs, half=half: e.scalar_tensor_tensor(
                        out=xT[:, kt, hs], in0=xT[:, kt, hs], scalar=sp[:, gfo + kt:gfo + kt + 1], in1=rs_n[:, half, :],
                        op0=ALU.mult, op1=ALU.mult), reads=['xh%d_%d' % (kt, half), 'rs_n%d' % half, 'sp'], writes=['xo%d_%d' % (kt, half)])
                    P.dma('sp', yTd[:, kt, hs], xT[:, kt, hs], reads=['xo%d_%d' % (kt, half)])
        except _Stop:
            for e_ in ('pe', 'act', 'dve', 'pool'):
                if P.ecount[e_]:
                    P.wait_tok('sp', (P.eidx[e_], P.ecount[e_]))
        for k in range(P.n_dma):
            if P.dma_count[k]:
                P.wait_tok('sp', (P.dma_base + k, P.dma_count[k]))
        P.run(blk)
    return nc


def _host_prepare(inp):
    f32 = np.float32
    x_prompt = np.asarray(inp['x_prompt'], f32)
    x_sample = np.asarray(inp['x_sample'], f32)
    state_ret = np.asarray(inp['state_ret'], f32)
    c = np.asarray(inp['c'], f32)
    c_ctx = np.asarray(inp['c_ctx'], f32)
    p = np.arange(128)

    def colmaj(v, nt):
        return np.ascontiguousarray(v.reshape(nt, 128).T)

    sp_common = np.zeros((128, NS), f32)

    def put(name, arr):
        o, n = SP32_OFF[name]
        sp_common[:, o:o + n] = arr.reshape(128, n)
    put('g1', np.stack([colmaj(inp['g_norm1'][l], 8) for l in range(2)], 1))
    put('g2', np.stack([colmaj(inp['g_norm2'][l], 8) for l in range(2)], 1))
    put('gF', colmaj(np.asarray(inp['g_final'], f32), 8))
    put('b_fm', np.stack([colmaj(np.asarray(inp['b_ada'], f32)[l], 48) for l in range(2)], 1))
    dw = np.asarray(inp['conv_dw'], f32)
    put('conv_dw', np.ascontiguousarray(dw.reshape(2, 31, 2, 128).transpose(3, 0, 2, 1)))
    for nm, key in (('conv_b', 'conv_b'), ('ln_g', 'conv_ln_g'), ('ln_b', 'conv_ln_b'), ('pool_scale', 'pool_scale')):
        a = np.asarray(inp[key], f32).reshape(2, 2, 128).transpose(2, 0, 1)
        put(nm, np.ascontiguousarray(a))
    gb = np.asarray(inp['gmlp_b'], f32)
    gbt = np.zeros((128, 2, 2, 128), f32)
    for l in range(2):
        for ti in range(2):
            for hl in range(2):
                gbt[hl * 64:(hl + 1) * 64, l, ti, :] = gb[l, 2 * ti + hl][None, :]
    put('gb_tab', gbt)
    rd = np.asarray(inp['ret_decay'], f32)
    rdcol = np.zeros((128, 2, 2, 2), f32)
    for ti in range(2):
        for hl in range(2):
            rdcol[hl * 64:(hl + 1) * 64, :, :, ti] = rd[:, :, 2 * ti + hl][None]
    put('rdcol', rdcol)
    put('rdb', np.broadcast_to(rd.reshape(1, 16), (128, 16)).copy())
    ii = np.arange(128)
    diff_f = np.maximum(ii[None, :] - ii[:, None], 0).astype(f32)
    diff_b = np.maximum(ii[:, None] - ii[None, :], 0).astype(f32)
    put('diffT', np.stack([diff_f, diff_b], 1))
    tri_f = (ii[None, :] >= ii[:, None]).astype(f32)
    tri_b = (ii[:, None] >= ii[None, :]).astype(f32)
    put('triT', np.stack([tri_f, tri_b], 1))
    ramp = np.stack([np.broadcast_to((ii + 1).astype(f32), (128, 128)), np.broadcast_to((128 - ii).astype(f32), (128, 128))], 1)
    put('ramp_q', ramp)
    put('rampcol_k', np.stack([(127 - ii).astype(f32), ii.astype(f32)], 1))
    bd = np.zeros((128, 128), f32)
    bd[0:64, 0:64] = 1
    bd[64:, 64:] = 1
    put('bdmask', bd)

    bfp = np.zeros((128, NB), f32)

    def putb(name, arr):
        o, n = SPBF_OFF[name]
        bfp[:, o:o + n] = arr.reshape(128, n)
    pw = np.asarray(inp['conv_pw'], f32)
    putb('conv_pw', np.ascontiguousarray(pw.reshape(2, 2, 128, 256).transpose(2, 0, 1, 3)))
    ws = np.asarray(inp['gmlp_ws'], f32)
    putb('wsT', np.ascontiguousarray(ws.transpose(3, 0, 1, 2)))
    pwl = np.asarray(inp['pool_w'], f32)
    pbd = np.zeros((128, 2, 2, 128), f32)
    for l in range(2):
        for ti in range(2):
            for gl in range(2):
                pbd[gl * 64:(gl + 1) * 64, l, ti, gl * 64:(gl + 1) * 64] = pwl[l, 2 * ti + gl]
    putb('pool_bd', pbd)
    putb('ident', np.eye(128, dtype=f32))
    perm = np.arange(128)
    dd = perm % 32
    perm = np.where(dd < 16, perm + 16, perm - 16)
    Pm = np.zeros((128, 128), f32)
    Pm[perm, np.arange(128)] = 1.0
    putb('Pm', Pm)

    t = np.arange(1024)
    nf = 16
    inv = (f32(10000.0) ** (-np.arange(nf, dtype=f32) / f32(nf))).astype(f32)
    rows = (t // 64).astype(f32)
    cols = (t % 64).astype(f32)
    dd64 = np.arange(128) % 64
    fidx = dd64 % 16
    pos = np.where((dd64 < 32)[:, None], rows[None, :], cols[None, :]).astype(f32)
    ang = (pos * inv[fidx][:, None]).astype(f32)
    cos_s = np.cos(ang).astype(f32)
    sin_s = np.sin(ang).astype(f32)
    is_x1 = ((dd64 % 32) < 16)
    sinS = np.where(is_x1[:, None], -sin_s, sin_s).astype(f32)
    sinP_s = sinS[perm]
    def invcount(L):
        tt = np.arange(1024) % L
        out = np.zeros((128, 2, 1024), f32)
        for ti in range(2):
            for gl in range(2):
                w = (2, 4, 8, 16)[2 * ti + gl]
                lo = np.clip(tt - w // 2, 0, L)
                hi = np.clip(tt + w // 2, 0, L)
                out[gl * 64:(gl + 1) * 64, ti, :] = (1.0 / (hi - lo).astype(f32))[None, :]
        return out
    tabs_sample = np.concatenate([cos_s, sinP_s, invcount(1024).reshape(128, 2048)], 1).astype(f32)
    tabs_prompt = np.concatenate([np.ones((128, 1024), f32), np.zeros((128, 1024), f32), invcount(256).reshape(128, 2048)], 1).astype(f32)

    in_maps = []
    wts = dict(w_ada=np.ascontiguousarray(inp['w_ada'], f32), w_in=np.ascontiguousarray(inp['w_in'], f32),
               w_out=np.ascontiguousarray(inp['w_out'], f32), w_ffn_in=np.ascontiguousarray(inp['w_ffn_in'], f32),
               w_ffn_out=np.ascontiguousarray(inp['w_ffn_out'], f32))
    for core in range(8):
        spc = sp_common.copy()
        if core < 4:
            xs = x_prompt[4 * core:4 * core + 4].reshape(1024, 1024)
            cv = c_ctx
            flagv = 0.0
            tabs = tabs_prompt
            s0 = np.zeros((128, 8, 128), f32)
        else:
            b = core - 4
            xs = x_sample[b]
            cv = c[b]
            flagv = 1.0
            tabs = tabs_sample
            s0 = np.zeros((128, 2, 2, 2, 128), f32)
            for ti in range(2):
                for hl in range(2):
                    s0[hl * 64:(hl + 1) * 64, :, :, ti, hl * 64:(hl + 1) * 64] = state_ret[b, :, :, 2 * ti + hl].transpose(2, 0, 1, 3)
            s0 = s0.reshape(128, 8, 128)
        o, n = SP32_OFF['cvec']
        spc[:, o:o + n] = colmaj(cv, 8)
        o, n = SP32_OFF['flag']
        spc[:, o] = flagv
        m = dict(xT=np.ascontiguousarray(xs.T), sp32=spc, spbf=bfp, tabs=tabs, s0=np.ascontiguousarray(s0))
        m.update(wts)
        in_maps.append(m)
    return in_maps


_NC_CACHE = {}


def kernel(**inputs):
    in_maps = _host_prepare(inputs)
    if 'nc' not in _NC_CACHE:
        _NC_CACHE['nc'] = build_program()
    nc = _NC_CACHE['nc']
    res = run_bass_kernel_spmd(nc, in_maps, core_ids=list(range(8)))
    r = res.results
    y_prompt = np.zeros((16, 256, 1024), np.float32)
    y_sample = np.zeros((4, 1024, 1024), np.float32)
    new_state = np.zeros((16, 2, 2, 4, 64, 64), np.float32)
    for core in range(4):
        y_prompt[4 * core:4 * core + 4] = np.asarray(r[core]['yT']).T.reshape(4, 256, 1024)
        stc = np.asarray(r[core]['st'])
        new_state[4 * core:4 * core + 4] = stc.transpose(2, 0, 1, 3, 4, 5)
    for core in range(4, 8):
        y_sample[core - 4] = np.asarray(r[core]['yT']).T
    return (y_prompt, y_sample, new_state)
```
